# Optimizing a Trainium2 kernel written in Bass

```python
import math
import jax, jax.numpy as jnp
from jax import lax
import numpy as np

D_MODEL = 1024
BATCH = 16
SEQ = 256
DEPTH = 4
DEC_BATCH = 8
DEC_SEQ = 1024
PAST_LEN = 512

GRID_W = 64
N_BRANCH = 3
MIX_WIDTH = D_MODEL
D_FF = 2816
SSD_HEAD_DIM = 64
SSD_HEADS = MIX_WIDTH // SSD_HEAD_DIM
SSD_GROUPS = 4
SSD_STATE = 64
SSD_CONV = 5
SSD_CHUNK = 128
SSD_CONV_CH = MIX_WIDTH + 2 * SSD_GROUPS * SSD_STATE
DIFF_HEAD_DIM = 64
DIFF_HEADS = MIX_WIDTH // (2 * DIFF_HEAD_DIM)
DIFF_QK_WIDTH = DIFF_HEADS * 2 * DIFF_HEAD_DIM
DIFF_V_WIDTH = DIFF_HEADS * 2 * DIFF_HEAD_DIM
Q_BLOCK = 128
POOL_WINDOWS = (2, 4, 8, 16)
POOL_GROUPS = 4
POOL_GROUP_WIDTH = MIX_WIDTH // POOL_GROUPS
IN_WIDTH = MIX_WIDTH + SSD_CONV_CH + 2 * SSD_HEADS + 2 * DIFF_QK_WIDTH + DIFF_V_WIDTH + MIX_WIDTH + N_BRANCH * D_MODEL
N_MOD = 9
ROPE_THETA = 10000.0
RMS_EPS = 1e-6

kernel_name = "hybrid_diffusion_ssd_diffattn_pool_step"


def rmsnorm(x, gain):
    xf = x.astype(jnp.float32)
    y = xf * lax.rsqrt(jnp.mean(xf * xf, axis=-1, keepdims=True) + RMS_EPS)
    return (y * gain.astype(jnp.float32)).astype(x.dtype)


def modulate(h, shift, scale):
    return h * (1 + scale) + shift


def swiglu(h, w_in, w_out):
    g, u = jnp.split(h @ w_in, 2, axis=-1)
    return (jax.nn.silu(g) * u) @ w_out


def split_columns(a, sizes):
    points = []
    acc = 0
    for s in sizes[:-1]:
        acc += s
        points.append(acc)
    return jnp.split(a, points, axis=-1)


def depthwise_conv(u, w, bias):
    taps, ch = w.shape
    out = lax.conv_general_dilated(u, w[:, None, :].astype(u.dtype), window_strides=(1,),
                                   padding=[(taps // 2, taps // 2)],
                                   dimension_numbers=("NWC", "WIO", "NWC"),
                                   feature_group_count=ch)
    return out + bias


def ssd_chunked(x, dt, a_coef, bm, cm, h0):
    b, L, H, P = x.shape
    nc = L // SSD_CHUNK
    rep = H // bm.shape[2]
    dtype = x.dtype
    bh = jnp.repeat(bm, rep, axis=2).reshape(b, nc, SSD_CHUNK, H, -1)
    ch = jnp.repeat(cm, rep, axis=2).reshape(b, nc, SSD_CHUNK, H, -1)
    xdt = (x * dt[..., None].astype(dtype)).reshape(b, nc, SSD_CHUNK, H, P)
    a = (dt * a_coef).reshape(b, nc, SSD_CHUNK, H).transpose(0, 1, 3, 2)
    a_cum = jnp.cumsum(a, axis=-1)
    causal = jnp.tril(jnp.ones((SSD_CHUNK, SSD_CHUNK), dtype=bool))
    seg = a_cum[..., :, None] - a_cum[..., None, :]
    decay_in = jnp.exp(jnp.where(causal, seg, -jnp.inf)).astype(dtype)
    scores = jnp.einsum('bclhn,bcshn->bchls', ch, bh) * decay_in
    y_diag = jnp.einsum('bchls,bcshp->bclhp', scores, xdt)
    decay_to_end = jnp.exp(a_cum[..., -1:] - a_cum).astype(dtype)
    chunk_states = jnp.einsum('bclhn,bchl,bclhp->bchpn', bh, decay_to_end, xdt)
    chunk_decay = jnp.exp(a_cum[..., -1]).astype(dtype)

    def step(h, inp):
        st, dec = inp
        return h * dec[..., None, None] + st, h

    h_final, h_prev = lax.scan(step, h0.astype(dtype),
                               (chunk_states.swapaxes(0, 1), chunk_decay.swapaxes(0, 1)))
    h_prev = h_prev.swapaxes(0, 1)
    decay_from_start = jnp.exp(a_cum).astype(dtype)
    y_off = jnp.einsum('bclhn,bchpn,bchl->bclhp', ch, h_prev, decay_from_start)
    return (y_diag + y_off).reshape(b, L, H, P), h_final


def axial_rope(t, rows):
    pos_row = jnp.repeat(jnp.arange(rows), GRID_W)
    pos_col = jnp.tile(jnp.arange(GRID_W), rows)
    n_freq = DIFF_HEAD_DIM // 4
    inv_freq = ROPE_THETA ** (-jnp.arange(n_freq, dtype=jnp.float32) / n_freq)

    def rotate(u, pos):
        ang = pos.astype(jnp.float32)[:, None] * inv_freq
        cos = jnp.cos(ang)[None, :, None, None, :].astype(u.dtype)
        sin = jnp.sin(ang)[None, :, None, None, :].astype(u.dtype)
        u1, u2 = u[..., :n_freq], u[..., n_freq:]
        return jnp.concatenate([u1 * cos - u2 * sin, u1 * sin + u2 * cos], axis=-1)

    half = DIFF_HEAD_DIM // 2
    return jnp.concatenate([rotate(t[..., :half], pos_row), rotate(t[..., half:], pos_col)], axis=-1)


def diff_attention(q, k, v, lam):
    b, lq, H, _, d = q.shape
    nb = lq // Q_BLOCK
    qb = q.reshape(b, nb, Q_BLOCK, H, 2, d).swapaxes(0, 1)

    def attend(q_blk):
        s = jnp.einsum('bqhcd,bkhcd->bhcqk', q_blk, k).astype(jnp.float32) * (d ** -0.5)
        p = jax.nn.softmax(s, axis=-1)
        a = p[:, :, 0] - lam * p[:, :, 1]
        return jnp.einsum('bhqk,bkhe->bqhe', a.astype(v.dtype), v)

    o = lax.map(attend, qb)
    return o.swapaxes(0, 1).reshape(b, lq, H, -1)


def multiscale_pool(u, w_map, scale):
    b, L, _ = u.shape
    ug = u.reshape(b, L, POOL_GROUPS, POOL_GROUP_WIDTH)
    cs = lax.cumsum(ug.astype(jnp.float32), axis=1)
    cs = jnp.pad(cs, ((0, 0), (1, 0), (0, 0), (0, 0)))
    win = jnp.array(POOL_WINDOWS, dtype=jnp.int32)
    t = jnp.arange(L, dtype=jnp.int32)[:, None]
    lo = jnp.clip(t - win // 2, 0, L)
    hi = jnp.clip(t + win - win // 2, 0, L)
    g = jnp.arange(POOL_GROUPS)
    window_sum = cs[:, hi, g] - cs[:, lo, g]
    count = (hi - lo).astype(jnp.float32)[None, :, :, None]
    pooled = (window_sum / count).astype(u.dtype) - ug
    mixed = jnp.einsum('blgc,gce->blge', pooled, w_map)
    return mixed.reshape(b, L, MIX_WIDTH) * scale


def token_mixer(h, l, p, ctx):
    b, L, _ = h.shape
    proj = h @ p["w_in"][l]
    sizes = (MIX_WIDTH, SSD_CONV_CH, 2 * SSD_HEADS, DIFF_QK_WIDTH, DIFF_QK_WIDTH, DIFF_V_WIDTH,
             MIX_WIDTH, N_BRANCH * D_MODEL)
    z, xbc, dt_raw, q, k, v, u_pool, gate_logits = split_columns(proj, sizes)

    xbc = jax.nn.silu(depthwise_conv(xbc, p["ssd_conv_w"][l], p["ssd_conv_b"][l]))
    xs, bm, cm = split_columns(xbc, (MIX_WIDTH, SSD_GROUPS * SSD_STATE, SSD_GROUPS * SSD_STATE))
    xs = xs.reshape(b, L, SSD_HEADS, SSD_HEAD_DIM)
    bm = bm.reshape(b, L, SSD_GROUPS, SSD_STATE)
    cm = cm.reshape(b, L, SSD_GROUPS, SSD_STATE)
    dt = jax.nn.softplus(dt_raw.reshape(b, L, 2, SSD_HEADS).astype(jnp.float32)
                         + p["ssd_dt_bias"][l].astype(jnp.float32))
    a_coef = -jnp.exp(p["ssd_a_log"][l].astype(jnp.float32))
    if ctx is None:
        h0 = jnp.zeros((b, 2, SSD_HEADS, SSD_HEAD_DIM, SSD_STATE), h.dtype)
    else:
        h0 = ctx[2]
    y_fwd, hf_fwd = ssd_chunked(xs, dt[:, :, 0], a_coef[0], bm, cm, h0[:, 0])
    y_bwd, hf_bwd = ssd_chunked(xs[:, ::-1], dt[:, ::-1, 1], a_coef[1], bm[:, ::-1], cm[:, ::-1], h0[:, 1])
    y = y_fwd + y_bwd[:, ::-1] + p["ssd_d"][l][:, None] * xs
    y_ssd = rmsnorm(y.reshape(b, L, MIX_WIDTH) * jax.nn.silu(z), p["ssd_norm_gain"][l])

    q = q.reshape(b, L, DIFF_HEADS, 2, DIFF_HEAD_DIM)
    k = k.reshape(b, L, DIFF_HEADS, 2, DIFF_HEAD_DIM)
    v = v.reshape(b, L, DIFF_HEADS, 2 * DIFF_HEAD_DIM)
    lam_init = 0.8 - 0.6 * math.exp(-0.3 * l)
    lq1, lk1, lq2, lk2 = p["diff_lambda"][l].astype(jnp.float32)
    lam = jnp.exp(jnp.sum(lq1 * lk1)) - jnp.exp(jnp.sum(lq2 * lk2)) + lam_init
    if ctx is None:
        k_all, v_all = k, v
    else:
        rows = L // GRID_W
        q = axial_rope(q, rows)
        k = axial_rope(k, rows)
        k_ctx = ctx[0].reshape(b, -1, DIFF_HEADS, 2, DIFF_HEAD_DIM)
        k_all = jnp.concatenate([k_ctx, k], axis=1)
        v_all = jnp.concatenate([ctx[1], v], axis=1)
    o = diff_attention(q, k_all, v_all, lam)
    o_diff = (rmsnorm(o, p["diff_norm_gain"][l]) * (1 - lam_init)).reshape(b, L, MIX_WIDTH)

    y_pool = multiscale_pool(u_pool, p["pool_map"][l], p["pool_scale"][l])

    branches = jnp.stack([y_ssd, o_diff, y_pool], axis=2)
    proj_b = jnp.einsum('blnw,nwd->blnd', branches, p["w_branch"][l])
    gates = jax.nn.sigmoid(gate_logits.reshape(b, L, N_BRANCH, D_MODEL))
    merged = jnp.sum(gates * proj_b, axis=2)
    out = merged @ p["w_out"][l]
    if ctx is None:
        ctx_out = (k.reshape(b, L, DIFF_HEADS, 2 * DIFF_HEAD_DIM), v, jnp.stack([hf_fwd, hf_bwd], axis=1))
    else:
        ctx_out = None
    return out, ctx_out


def trunk_layer(x, cond, l, p, ctx):
    mod = jax.nn.silu(cond) @ p["w_ada"][l] + p["b_ada"][l]
    mod = mod.reshape(mod.shape[0], 1, N_MOD, D_MODEL)
    g = p["norm_gain"][l]
    h = modulate(rmsnorm(x, g[0]), mod[:, :, 0], mod[:, :, 1])
    x = x + 0.5 * mod[:, :, 2] * swiglu(h, p["ffn_w_in"][l, 0], p["ffn_w_out"][l, 0])
    h = modulate(rmsnorm(x, g[1]), mod[:, :, 3], mod[:, :, 4])
    mix, ctx_out = token_mixer(h, l, p, ctx)
    x = x + mod[:, :, 5] * mix
    h = modulate(rmsnorm(x, g[2]), mod[:, :, 6], mod[:, :, 7])
    x = x + 0.5 * mod[:, :, 8] * swiglu(h, p["ffn_w_in"][l, 1], p["ffn_w_out"][l, 1])
    return x, ctx_out


def setup_inputs(seed: int = 0) -> dict:
    key = jax.random.key(seed)
    ks = jax.random.split(key, 26)
    f32 = jnp.float32

    def nrm(k, shape, scale):
        return jax.random.normal(k, shape, f32) * scale

    dt0 = jnp.exp(jax.random.uniform(ks[12], (DEPTH, 2, SSD_HEADS), f32, math.log(1e-3), math.log(1e-1)))
    return {
        "x_prompt": nrm(ks[0], (BATCH, SEQ, D_MODEL), 1.0),
        "x_sample": nrm(ks[1], (DEC_BATCH, DEC_SEQ, D_MODEL), 1.0),
        "cache_k": nrm(ks[2], (DEC_BATCH, DEPTH, PAST_LEN, DIFF_HEADS, 2 * DIFF_HEAD_DIM), 1.0),
        "cache_v": nrm(ks[3], (DEC_BATCH, DEPTH, PAST_LEN, DIFF_HEADS, 2 * DIFF_HEAD_DIM), 1.0),
        "state_ssm": nrm(ks[4], (DEC_BATCH, DEPTH, 2, SSD_HEADS, SSD_HEAD_DIM, SSD_STATE), 0.1),
        "c": nrm(ks[5], (DEC_BATCH, D_MODEL), 1.0),
        "c_ctx": nrm(ks[6], (D_MODEL,), 1.0),
        "w_ada": nrm(ks[7], (DEPTH, D_MODEL, N_MOD * D_MODEL), 0.5 * D_MODEL ** -0.5),
        "b_ada": nrm(ks[8], (DEPTH, N_MOD * D_MODEL), 0.02),
        "norm_gain": 1.0 + nrm(ks[9], (DEPTH, 3, D_MODEL), 0.05),
        "ffn_w_in": nrm(ks[10], (DEPTH, 2, D_MODEL, 2 * D_FF), D_MODEL ** -0.5),
        "ffn_w_out": nrm(ks[11], (DEPTH, 2, D_FF, D_MODEL), D_FF ** -0.5),
        "w_in": nrm(ks[13], (DEPTH, D_MODEL, IN_WIDTH), D_MODEL ** -0.5),
        "ssd_conv_w": nrm(ks[14], (DEPTH, SSD_CONV, SSD_CONV_CH), SSD_CONV ** -0.5),
        "ssd_conv_b": nrm(ks[15], (DEPTH, SSD_CONV_CH), 0.02),
        "ssd_dt_bias": dt0 + jnp.log(-jnp.expm1(-dt0)),
        "ssd_a_log": jnp.log(jax.random.uniform(ks[16], (DEPTH, 2, SSD_HEADS), f32, 1.0, 16.0)),
        "ssd_d": 1.0 + nrm(ks[17], (DEPTH, SSD_HEADS), 0.1),
        "ssd_norm_gain": 1.0 + nrm(ks[18], (DEPTH, MIX_WIDTH), 0.05),
        "diff_lambda": nrm(ks[19], (DEPTH, 4, DIFF_HEAD_DIM), 0.1),
        "diff_norm_gain": 1.0 + nrm(ks[20], (DEPTH, 2 * DIFF_HEAD_DIM), 0.05),
        "pool_map": nrm(ks[21], (DEPTH, POOL_GROUPS, POOL_GROUP_WIDTH, POOL_GROUP_WIDTH), POOL_GROUP_WIDTH ** -0.5),
        "pool_scale": 1.0 + nrm(ks[22], (DEPTH, MIX_WIDTH), 0.05),
        "w_branch": nrm(ks[23], (DEPTH, N_BRANCH, MIX_WIDTH, D_MODEL), MIX_WIDTH ** -0.5),
        "w_out": nrm(ks[24], (DEPTH, D_MODEL, D_MODEL), D_MODEL ** -0.5),
        "final_gain": 1.0 + nrm(ks[25], (D_MODEL,), 0.05),
    }


def reference(x_prompt, x_sample, cache_k, cache_v, state_ssm, c, c_ctx, w_ada, b_ada, norm_gain,
              ffn_w_in, ffn_w_out, w_in, ssd_conv_w, ssd_conv_b, ssd_dt_bias, ssd_a_log, ssd_d,
              ssd_norm_gain, diff_lambda, diff_norm_gain, pool_map, pool_scale, w_branch, w_out,
              final_gain):
    p = {
        "w_ada": w_ada, "b_ada": b_ada, "norm_gain": norm_gain, "ffn_w_in": ffn_w_in,
        "ffn_w_out": ffn_w_out, "w_in": w_in, "ssd_conv_w": ssd_conv_w, "ssd_conv_b": ssd_conv_b,
        "ssd_dt_bias": ssd_dt_bias, "ssd_a_log": ssd_a_log, "ssd_d": ssd_d,
        "ssd_norm_gain": ssd_norm_gain, "diff_lambda": diff_lambda, "diff_norm_gain": diff_norm_gain,
        "pool_map": pool_map, "pool_scale": pool_scale, "w_branch": w_branch, "w_out": w_out,
    }
    xp = x_prompt
    cond_ctx = c_ctx[None, :]
    ks_list, vs_list, ss_list = [], [], []
    for l in range(DEPTH):
        xp, (k_l, v_l, s_l) = trunk_layer(xp, cond_ctx, l, p, None)
        ks_list.append(k_l)
        vs_list.append(v_l)
        ss_list.append(s_l)
    y_prompt = rmsnorm(xp, final_gain)
    new_cache_k = jnp.stack(ks_list, axis=1)
    new_cache_v = jnp.stack(vs_list, axis=1)
    new_state_ssm = jnp.stack(ss_list, axis=1)

    xs = x_sample
    for l in range(DEPTH):
        xs, _ = trunk_layer(xs, c, l, p, (cache_k[:, l], cache_v[:, l], state_ssm[:, l]))
    y_sample = rmsnorm(xs, final_gain)
    return (y_prompt, y_sample, new_cache_k, new_cache_v, new_state_ssm)
```

```python
import math, os, sys
SKIP = set(os.environ.get('KSKIP', '').split(','))
import numpy as np
from contextlib import ExitStack
import ml_dtypes
import concourse.bass as bass
import concourse.mybir as mybir
from concourse.bass_utils import run_bass_kernel_spmd

F32 = mybir.dt.float32
BF16 = mybir.dt.bfloat16
AF = mybir.ActivationFunctionType
ALU = mybir.AluOpType

D = 1024
DEPTH = 4
DFF = 2816
NT = 1536
EPS = 1e-6
IN_W = 9760
OFF_Z, OFF_XBC, OFF_DT, OFF_Q, OFF_K, OFF_V, OFF_U, OFF_G = 0, 1024, 2560, 2592, 3616, 4640, 5664, 6688
PV_BADA, PV_NG, PV_CW, PV_CB, PV_SNG, PV_PS, PV_DV, PV_DNG, PV_LAM, PV_DTB, PV_ALOG = 0, 72, 96, 156, 168, 176, 184, 192, 193, 449, 450
PL = 451
NEG = -30000.0
DEBUG_ANNOT = bool(os.environ.get('KANNOT'))
STOPAT = os.environ.get('KSTOP', '')


class StopBuild(Exception):
    pass


def chk(name):
    if STOPAT and name == STOPAT:
        raise StopBuild(name)


class Buf:
    __slots__ = ("name", "w", "r", "dsem", "dcnt")

    def __init__(self, name):
        self.name = name
        self.w = None
        self.r = []
        self.dsem = None
        self.dcnt = 0


class KB:
    ENGS = ("pe", "dve", "act", "pool", "sp")

    def __init__(self, nc, n_dma_sems=70):
        self.nc = nc
        self.prog = {e: [] for e in self.ENGS}
        self.cnt = {e: 0 for e in self.ENGS}
        self.waited = {}
        self.dma_free = ["d%d" % i for i in range(n_dma_sems)]
        self.sem_names = list(self.ENGS) + list(self.dma_free)
        self.sems = {}
        self.pending = {e: False for e in self.ENGS}
        self.nins = 0
        self.dbufs = []

    def _deps(self, eng, reads, writes):
        toks = []
        for b in reads:
            if b.w is not None:
                toks.append(b.w)
        for b in writes:
            if b.w is not None:
                toks.append(b.w)
            toks.extend(b.r)
        best = {}
        for (sk, v) in toks:
            if sk == "pe" and eng == "pe":
                continue
            if sk in self.cnt and v > self.cnt[sk]:
                if sk == eng:
                    continue
                raise RuntimeError("dep on open nosig group %s (eng %s)" % (sk, eng))
            if v > best.get(sk, 0):
                best[sk] = v
        for sk, v in best.items():
            if self.waited.get((eng, sk), 0) >= v:
                continue
            self.waited[(eng, sk)] = v
            self.prog[eng].append(("wait", sk, v))

    def op(self, eng, fn, reads=(), writes=(), sig=True):
        self._deps(eng, reads, writes)
        tok = (eng, self.cnt[eng] + 1)
        f = sys._getframe(1)
        if f.f_code.co_filename == __file__ and f.f_code.co_name in ("mm", "tr", "act", "tt", "ts", "stt", "cp", "recip", "memset"):
            f = f.f_back
        self.prog[eng].append(("op", fn, sig, "L%d" % f.f_lineno))
        self.nins += 1
        if sig:
            self.cnt[eng] += 1
        self.pending[eng] = not sig
        for b in reads:
            if len(b.r) > 64:
                b.r = b.r[-32:] if False else b.r
            b.r.append(tok)
        for b in writes:
            b.w = tok
            b.r = []
        return tok

    def dma(self, eng, fn, reads=(), writes=(), sbuf=None):
        self._deps(eng, reads, writes)
        b = sbuf if sbuf is not None else (writes[0] if writes else reads[0])
        if b.dsem is None:
            b.dsem = self.dma_free.pop(0)
            self.dbufs.append(b)
        b.dcnt += 1
        tok = (b.dsem, 16 * b.dcnt)
        self.prog[eng].append(("dma", fn, b.dsem))
        self.nins += 1
        for x in reads:
            x.r.append(tok)
        for x in writes:
            x.w = tok
            x.r = []
        return tok

    def barrier(self, engs=("pe", "dve", "act", "sp")):
        for e in self.ENGS:
            assert not self.pending[e]
        targets = [(e2, self.cnt[e2]) for e2 in ("pe", "dve", "act", "pool")]
        targets += [(b.dsem, 16 * b.dcnt) for b in self.dbufs]
        for e in engs:
            for (sk, v) in targets:
                if v == 0 or self.waited.get((e, sk), 0) >= v:
                    continue
                self.waited[(e, sk)] = v
                self.prog[e].append(("wait", sk, v))

    def final_wait(self, eng, bufs):
        self._deps(eng, bufs, bufs)

    def simulate(self):
        sem = {n: 0 for n in self.sem_names}
        pc = {e: 0 for e in self.ENGS}
        progress = True
        while progress:
            progress = False
            for e in self.ENGS:
                prog = self.prog[e]
                while pc[e] < len(prog):
                    it = prog[pc[e]]
                    if it[0] == "wait":
                        if sem[it[1]] < it[2]:
                            break
                    elif it[0] == "op":
                        if it[2]:
                            sem[e] += 1
                    else:
                        sem[it[2]] += 16
                    pc[e] += 1
                    progress = True
        bad = [(e, pc[e], len(self.prog[e]), self.prog[e][pc[e]][:3], sem[self.prog[e][pc[e]][1]] if self.prog[e][pc[e]][0] == "wait" else None)
               for e in self.ENGS if pc[e] < len(self.prog[e])]
        if bad:
            raise RuntimeError("DEADLOCK in emitted program: %s" % (bad,))
        for e in self.ENGS:
            assert sem[e] == self.cnt[e]

    def emit(self, stack):
        self.simulate()
        nc = self.nc
        for nm in self.sem_names:
            self.sems[nm] = stack.enter_context(nc.semaphore("s_" + nm))
        block = stack.enter_context(nc.Block())
        deco = {"pe": block.tensor, "dve": block.vector, "act": block.scalar, "pool": block.gpsimd, "sp": block.sync}
        for e in self.ENGS:
            assert not self.pending[e], "engine %s ends with open group" % e
            prog = self.prog[e]
            sems = self.sems
            esem = sems[e]

            def body(engine, prog=prog, esem=esem, sems=sems):
                for item in prog:
                    if item[0] == "wait":
                        engine.wait_ge(sems[item[1]], item[2])
                    elif item[0] == "op":
                        ins = item[1](engine)
                        if DEBUG_ANNOT:
                            ins.annotate(item[3])
                        if item[2]:
                            ins.then_inc(esem, 1)
                    else:
                        item[1](engine).then_inc(sems[item[2]], 16)

            deco[e](body)

    def mm(self, out, lhsT, rhs, start, stop, reads, writes, sig):
        self.op("pe", lambda e: e.matmul(out, lhsT=lhsT, rhs=rhs, start=start, stop=stop), reads, writes, sig)

    def tr(self, out, in_, ident, reads, writes, sig=True):
        self.op("pe", lambda e: e.transpose(out, in_, ident), reads, writes, sig)

    def act(self, out, in_, func, reads, writes, bias=0.0, scale=1.0):
        self.op("act", lambda e: e.activation(out, in_, func, bias=bias, scale=scale), reads, writes)

    def tt(self, out, in0, in1, op, reads, writes, eng="dve"):
        self.op(eng, lambda e: e.tensor_tensor(out=out, in0=in0, in1=in1, op=op), reads, writes)

    def ts(self, out, in0, s1, s2, op0, op1, reads, writes, eng="dve"):
        if s2 is None:
            self.op(eng, lambda e: e.tensor_single_scalar(out, in0, s1, op0), reads, writes)
        else:
            self.op(eng, lambda e: e.tensor_scalar(out=out, in0=in0, scalar1=s1, scalar2=s2, op0=op0, op1=op1), reads, writes)

    def stt(self, out, in0, scalar, in1, op0, op1, reads, writes, eng="dve"):
        self.op(eng, lambda e: e.scalar_tensor_tensor(out=out, in0=in0, scalar=scalar, in1=in1, op0=op0, op1=op1), reads, writes)

    def cp(self, out, in_, reads, writes, eng="dve"):
        if eng == "act":
            self.op("act", lambda e: e.copy(out, in_), reads, writes)
        else:
            self.op(eng, lambda e: e.tensor_copy(out, in_), reads, writes)

    def recip(self, out, in_, reads, writes):
        self.op("dve", lambda e: e.reciprocal(out, in_), reads, writes)

    def memset(self, ap, val, writes, eng="dve"):
        self.op(eng, lambda e: e.memset(ap, val), (), writes)

    def ld(self, out, in_, writes, eng="sp", sbuf=None):
        self.dma(eng, lambda e: e.dma_start(out=out, in_=in_), (), writes, sbuf=sbuf)

    def st(self, out, in_, reads, eng="sp", sbuf=None):
        self.dma(eng, lambda e: e.dma_start(out=out, in_=in_), reads, (), sbuf=sbuf)


class Rot:
    def __init__(self, items):
        self.items = items
        self.i = 0

    def next(self):
        it = self.items[self.i % len(self.items)]
        self.i += 1
        return it


def host_consts():
    c = {}
    c["identf"] = np.eye(128, dtype=np.float32)
    c["identb"] = np.eye(128, dtype=np.float32).astype(ml_dtypes.bfloat16)
    c["onesb"] = np.ones((128, 128), np.float32).astype(ml_dtypes.bfloat16)
    c["onesf"] = np.ones((128, 128), np.float32)
    s = np.arange(128)[:, None]
    l = np.arange(128)[None, :]
    mf = np.where(l >= s, 0.0, NEG).astype(np.float32)
    mb = np.where(l <= s, 0.0, NEG).astype(np.float32)
    c["maskf"] = np.tile(mf, (1, 4)).astype(ml_dtypes.bfloat16)
    c["maskb"] = np.tile(mb, (1, 4)).astype(ml_dtypes.bfloat16)
    sel = np.zeros((32, 32, 128), np.float32)
    for h in range(32):
        sel[h, h, :] = 1.0
    c["sel"] = sel.reshape(32, 32 * 128)
    sg = np.zeros((32, 4), np.float32)
    sg[:16, 0] = 1.0
    sg[16:, 0] = -1.0
    sg[16:, 1] = 1.0
    sg[:, 2] = -1.0
    c["sgn"] = sg
    rm = np.ones((32, 1024), np.float32)
    rm[:, ::128] = 0.0
    c["rmask"] = rm
    n_freq = 16
    inv_freq = (10000.0 ** (-np.arange(n_freq, dtype=np.float32) / n_freq)).astype(np.float32)
    t = np.arange(1024)
    pos_row = (t // 64).astype(np.float32)
    pos_col = (t % 64).astype(np.float32)
    cos = np.zeros((128, 1024), np.float32)
    sin = np.zeros((128, 1024), np.float32)
    perm = np.zeros((128, 128), np.float32)
    for d in range(128):
        dd = d % 64
        pos = pos_row if dd < 32 else pos_col
        f = inv_freq[dd % 16]
        ang = (pos * f).astype(np.float32)
        cos[d] = np.cos(ang)
        sin[d] = np.sin(ang)
        if (d % 32) < 16:
            perm[d + 16, d] = -1.0
        else:
            perm[d - 16, d] = 1.0
    c["cos"] = cos
    c["sin"] = sin
    c["perm"] = perm.astype(ml_dtypes.bfloat16)
    def invc(seqs, padlen):
        out = np.ones((4, padlen), np.float32)
        for g, w in enumerate((2, 4, 8, 16)):
            for (o, L) in seqs:
                tt = np.arange(L)
                lo = np.clip(tt - w // 2, 0, L)
                hi = np.clip(tt + w - w // 2, 0, L)
                out[g, o:o + L] = 1.0 / (hi - lo).astype(np.float32)
        return np.broadcast_to(out[None], (128, 4, padlen)).reshape(128, 4 * padlen).astype(ml_dtypes.bfloat16)
    c["invc0"] = invc([(8, 256), (280, 256)], 544)
    c["invc1"] = invc([(8, 1024)], 1040)
    return c


CONST_SPECS = [("identf", [128, 128], F32), ("identb", [128, 128], BF16), ("onesb", [128, 128], BF16),
               ("onesf", [128, 128], F32), ("maskf", [128, 512], BF16), ("maskb", [128, 512], BF16),
               ("sel", [32, 4096], F32), ("sgn", [32, 4], F32), ("rmask", [32, 1024], F32),
               ("cos", [128, 1024], F32), ("sin", [128, 1024], F32), ("perm", [128, 128], BF16),
               ("invc0", [128, 4 * 544], BF16), ("invc1", [128, 4 * 1040], BF16)]


def build(n_layers=DEPTH, do_mixer=True, dbg=False):
    nc = bass.Bass("TRN2", target_bir_lowering=False)

    def din(name, shape, dt=F32):
        return nc.dram_tensor(name, shape, dt, kind="ExternalInput").ap()

    def dout(name, shape, dt=F32):
        return nc.dram_tensor(name, shape, dt, kind="ExternalOutput").ap()

    xT_d = din("xT", [D, NT])
    cond_d = din("cond", [128, 16])
    pv_d = din("pv", [128, DEPTH * PL + 8])
    ckT_d = din("ckT", [DEPTH, 8, 128, 512])
    cv_d = din("cv", [DEPTH, 512, 8, 128])
    st0_d = din("st0", [DEPTH, 2, 64, 1024])
    w_ada_d = din("w_ada", [DEPTH, D, 9 * D])
    ffn_in_d = din("ffn_w_in", [DEPTH, 2, D, 2 * DFF])
    ffn_out_d = din("ffn_w_out", [DEPTH, 2, DFF, D])
    w_in_d = din("w_in", [DEPTH, D, IN_W])
    pmap_d = din("pool_map", [DEPTH, 4, 256, 256])
    wbr_d = din("w_branch", [DEPTH, 3, D, D])
    wo_d = din("w_out", [DEPTH, D, D])
    cdram = {n: din("c_" + n, shp, dt) for (n, shp, dt) in CONST_SPECS}
    yT_d = dout("yT", [D, NT])
    nk_d = dout("nk", [2, DEPTH, 256, 8, 128])
    nv_d = dout("nv", [2, DEPTH, 256, 8, 128])
    ns_d = dout("ns", [2, DEPTH, 2, 64, 1024])
    dbg_d = dout("dbg", [8, 128, 1536]) if dbg else None

    st = ExitStack()
    with st:
        def sb(name, shape, dt=F32):
            return st.enter_context(nc.sbuf_tensor("s_" + name, shape, dt))

        kb = KB(nc)
        xT = sb("xT", [128, 8, NT])
        xB = [Buf("x%d" % i) for i in range(3)]
        hB = [Buf("h%d" % i) for i in range(3)]
        NSLOT = 2
        wslots = Rot([(sb("ws%d" % i, [128, 4096], BF16), Buf("ws%d" % i)) for i in range(NSLOT)])
        pv = sb("pv", [128, DEPTH * PL + 8])
        cond = sb("cond", [128, 8, 2])
        sc = sb("sc", [128, 8, 2])
        mod = sb("mod", [128, DEPTH, 72, 2])
        Amod = sb("Amod", [128, DEPTH, 3, 8, 2])
        Gmod = sb("Gmod", [128, DEPTH, 3, 8, 2])
        CB_ = Buf("const")
        MB = Buf("mod")
        identf = sb("identf", [128, 128]); identb = sb("identb", [128, 128], BF16)
        onesb = sb("onesb", [128, 128], BF16); onesf = sb("onesf", [128, 128])
        ps = [st.enter_context(nc.psum_tensor("ps%d" % i, [128, 512], F32)) for i in range(7)]
        pB = [Buf("ps%d" % i) for i in range(7)]
        psb = st.enter_context(nc.psum_tensor("psb", [128, 1024], BF16))
        psbB = Buf("psb")
        PH = 108 * 1024
        phase = sb("phase", [128, PH // 4])

        class Carver:
            def __init__(self):
                self.off = 0
                self.live = []
                self.last = (0, 0)
                self.mark = None

            def reset(self, off=None):
                self.off = HT_END if off is None else off
                self.mark = None

            def mk(self, name):
                lo, hi = (self.mark if self.mark is not None else self.last[0]), self.off
                self.mark = None
                self.last = (lo, hi)
                b = Buf(name)
                toks = []
                for (l2, h2, b2) in self.live:
                    if l2 < hi and lo < h2:
                        if b2.w is not None:
                            toks.append(b2.w)
                        toks.extend(b2.r)
                b.r = list(dict.fromkeys(toks))
                self.live = [(l2, h2, b2) for (l2, h2, b2) in self.live if not (lo <= l2 and h2 <= hi)]
                self.live.append((lo, hi, b))
                return b

            def mks(self, name, n):
                lo, hi = (self.mark if self.mark is not None else self.last[0]), self.off
                self.last = (lo, hi)
                bs = []
                keep = self.live
                for i in range(n):
                    self.live = list(keep)
                    self.mark = lo
                    bs.append(self.mk("%s%d" % (name, i)))
                self.live = [(l2, h2, b2) for (l2, h2, b2) in keep if not (lo <= l2 and h2 <= hi)] + [(lo, hi, b) for b in bs]
                return bs

            def take(self, shape, dt=F32):
                n = int(np.prod(shape[1:]))
                nbytes = n * (4 if dt == F32 else 2)
                nbytes = (nbytes + 63) // 64 * 64
                assert self.off + nbytes <= PH, "phase region overflow %d" % (self.off + nbytes)
                self.peak = max(getattr(self, "peak", 0), self.off + nbytes)
                self.last = (self.off, self.off + nbytes)
                if self.mark is None:
                    self.mark = self.off
                ap = phase[0:shape[0], self.off // 4:(self.off + nbytes) // 4]
                if dt != F32:
                    ap = ap.bitcast(BF16)[:, 0:n]
                else:
                    ap = ap[:, 0:n]
                self.off += nbytes
                if len(shape) > 2:
                    names = " ".join("a%d" % i for i in range(len(shape) - 1))
                    kw = {"a%d" % i: shape[i + 1] for i in range(len(shape) - 1)}
                    ap = ap.rearrange("p (%s) -> p %s" % (names, names), **kw)
                return ap

        cv = Carver()
        hT = cv.take([128, 8, NT], BF16)
        HT_END = cv.off

        for mt in range(3):
            kb.ld(xT[:, :, mt * 512:(mt + 1) * 512], xT_d.rearrange("(c p) t -> p c t", p=128)[:, :, mt * 512:(mt + 1) * 512], [xB[mt]])
        kb.ld(pv[:], pv_d, [CB_])
        kb.ld(cond[:].rearrange("p c j -> p (c j)"), cond_d, [CB_])
        kb.ld(identf[:], cdram["identf"], [CB_]); kb.ld(identb[:], cdram["identb"], [CB_])
        kb.ld(onesb[:], cdram["onesb"], [CB_]); kb.ld(onesf[:], cdram["onesf"], [CB_])

        def pvl(l, off, n=1):
            return pv[:, l * PL + off:l * PL + off + n]

        kb.act(sc[:], cond[:], AF.Silu, [CB_], [MB])
        cv.reset()
        wa = Rot([(cv.take([128, 8, 512], BF16), cv.mk("wa%d" % i)) for i in range(5)])
        scb = sb("scb", [128, 8, 2], BF16)
        kb.cp(scb[:], sc[:], [MB], [MB])
        for l in range(n_layers):
            wv = w_ada_d[l].rearrange("(kc p) n -> p kc n", p=128)
            for blk in range(18):
                wt, wb = wa.next()
                kb.ld(wt, wv[:, :, blk * 512:(blk + 1) * 512], [wb], eng="pool")
                for mi in range(4):
                    m = blk * 4 + mi
                    for kc in range(8):
                        kb.mm(ps[6][:, 2 * m:2 * m + 2], wt[:, kc, mi * 128:(mi + 1) * 128], scb[:, kc, :], kc == 0, kc == 7,
                              [wb, MB], [pB[6]], sig=(kc == 7))
            kb.tt(mod[:, l], ps[6][:, 0:144].rearrange("p (m j) -> p m j", j=2),
                  pvl(l, PV_BADA, 72).unsqueeze(2).to_broadcast([128, 72, 2]), ALU.add, [pB[6], CB_], [MB])
            for i in range(3):
                kb.stt(Amod[:, l, i], mod[:, l, (3 * i + 1) * 8:(3 * i + 2) * 8, :], 1.0,
                       pvl(l, PV_NG + 8 * i, 8).unsqueeze(2).to_broadcast([128, 8, 2]), ALU.add, ALU.mult, [MB, CB_], [MB])
                kb.ts(Gmod[:, l, i], mod[:, l, (3 * i + 2) * 8:(3 * i + 3) * 8, :], 0.5, None, ALU.mult, None, [MB], [MB])

        CND = [0, 1, 1]
        DBGB = Buf("dbg")

        def tap(idx, ap, reads, n):
            if dbg:
                kb.dma("sp", lambda e: e.dma_start(out=dbg_d[idx, 0:ap.shape[0], 0:n], in_=ap), reads, (), sbuf=DBGB)

        tap(0, mod[:, 0].rearrange("p m j -> p (m j)"), [MB], 144)
        tap(1, Amod[:, 0].rearrange("p i c j -> p (i c j)"), [MB], 48)
        tap(2, sc[:].rearrange("p c j -> p (c j)"), [MB], 16)

        def rstd_from_ps(pst, pbuf, n, dfeat, out_ap, outbuf, tmp_ap, tmpbuf):
            kb.ts(tmp_ap, pst, 1.0 / dfeat, EPS, ALU.mult, ALU.add, [pbuf], [tmpbuf])
            kb.act(tmp_ap, tmp_ap, AF.Sqrt, [tmpbuf], [tmpbuf])
            kb.recip(out_ap, tmp_ap, [tmpbuf], [outbuf])

        def norm_to_h(l, i, mts, sq, sqB, rs, rsB, tmp, tmpB):
            for mt in mts:
                cs = slice(mt * 512, (mt + 1) * 512)
                for kc in range(8):
                    kb.act(sq[:, kc, :], xT[:, kc, cs], AF.Square, [xB[mt]], [sqB])
                for kc in range(8):
                    kb.mm(ps[6][:], onesb[:], sq[:, kc, :], kc == 0, kc == 7, [sqB, CB_], [pB[6]], sig=(kc == 7))
                rstd_from_ps(ps[6][:], pB[6], 512, D, rs, rsB, tmp, tmpB)
                if l == 0 and i == 0:
                    tap(3, rs, [rsB], 512) if mt == 0 else None
                    tap(4, rs, [rsB], 512) if mt == 1 else None
                for kc in range(8):
                    kb.stt(tmp, xT[:, kc, cs], Amod[:, l, i, kc, CND[mt]:CND[mt] + 1], rs, ALU.mult, ALU.mult, [xB[mt], MB, rsB], [tmpB])
                    kb.act(hT[:, kc, cs], tmp, AF.Identity, [tmpB, MB], [hB[mt]], bias=mod[:, l, 3 * i * 8 + kc, CND[mt]:CND[mt] + 1])

        def wload(srcs, shape):
            wt, wb = wslots.next()
            n = int(np.prod(shape[1:]))
            names = " ".join("a%d" % i for i in range(len(shape) - 1))
            kw = {"a%d" % i: shape[i + 1] for i in range(len(shape) - 1)}
            view = wt[:, 0:n].rearrange("p (%s) -> p %s" % (names, names), **kw) if len(shape) > 2 else wt[:, 0:n]
            for (fn, src) in srcs:
                kb.dma("pool", (lambda e, o=fn(view), s=src: e.dma_start(out=o, in_=s)), (), [wb])
            return view, wb

        def ffn(l, which):
            cv.reset()
            sq = cv.take([128, 8, 512], BF16); sqB = cv.mk("sq")
            rs = cv.take([128, 512]); rsB = cv.mk("rs")
            tmp = cv.take([128, 512]); tmpB = cv.mk("tmp")
            norm_to_h(l, 0 if which == 0 else 2, [0, 1, 2], sq, sqB, rs, rsB, tmp, tmpB)
            chk("ffn%d_norm" % which)
            cv.reset()
            aT = cv.take([128, 12, NT], BF16)
            aB = cv.mks("a", 12)
            sgs = Rot([(cv.take([128, NT]), cv.mk("sg%d" % i)) for i in range(2)])
            w1 = ffn_in_d[l, which].rearrange("(kc p) n -> p kc n", p=128)
            w2 = ffn_out_d[l, which].rearrange("(kc p) n -> p kc n", p=128)
            gi = 0 if which == 0 else 2
            pset = Rot([(0, 1, 2), (3, 4, 5)])
            for (p0, p1) in ((0, 6), (6, 11)):
                nk = (p1 - p0) * 2
                for pr in range(p0, p1):
                    wv, wb = wload([(lambda v: v[:, :, 0, :], w1[:, :, pr * 256:(pr + 1) * 256]),
                                    (lambda v: v[:, :, 1, :], w1[:, :, DFF + pr * 256:DFF + (pr + 1) * 256])], [128, 8, 2, 256])
                    for ci in range(2):
                        j = (pr - p0) * 2 + ci
                        sg, sgB = sgs.next()
                        bg = pset.next()
                        for kc in range(8):
                            for mt in range(3):
                                kb.mm(ps[bg[mt]][:], wv[:, kc, 0, ci * 128:(ci + 1) * 128], hT[:, kc, mt * 512:(mt + 1) * 512],
                                      kc == 0, kc == 7, [wb, hB[mt]], [pB[bg[mt]]], sig=(kc == 7 and mt == 2))
                        for mt in range(3):
                            kb.act(sg[:, mt * 512:(mt + 1) * 512], ps[bg[mt]][:], AF.Silu, [pB[bg[mt]]], [sgB])
                        bu = pset.next()
                        for kc in range(8):
                            for mt in range(3):
                                kb.mm(ps[bu[mt]][:], wv[:, kc, 1, ci * 128:(ci + 1) * 128], hT[:, kc, mt * 512:(mt + 1) * 512],
                                      kc == 0, kc == 7, [wb, hB[mt]], [pB[bu[mt]]], sig=(kc == 7 and mt == 2))
                        for mt in range(3):
                            kb.tt(aT[:, j, mt * 512:(mt + 1) * 512], ps[bu[mt]][:], sg[:, mt * 512:(mt + 1) * 512], ALU.mult,
                                  [pB[bu[mt]], sgB], [aB[j]])
                chk("ffn%d_in%d" % (which, p0))
                for mp in range(4):
                    wv, wb = wload([(lambda v: v, w2[:, p0 * 2:p0 * 2 + nk, mp * 256:(mp + 1) * 256])], [128, nk, 256])
                    for mi in range(2):
                        m = mp * 2 + mi
                        bo = pset.next()
                        for kc in range(nk):
                            for mt in range(3):
                                kb.mm(ps[bo[mt]][:], wv[:, kc, mi * 128:(mi + 1) * 128], aT[:, kc, mt * 512:(mt + 1) * 512],
                                      kc == 0, kc == nk - 1, [wb, aB[kc]], [pB[bo[mt]]], sig=(kc == nk - 1 and mt == 2))
                        for mt in range(3):
                            cs = slice(mt * 512, (mt + 1) * 512)
                            kb.stt(xT[:, m, cs], ps[bo[mt]][:], Gmod[:, l, gi, m, CND[mt]:CND[mt] + 1], xT[:, m, cs], ALU.mult, ALU.add,
                                   [pB[bo[mt]], MB, xB[mt]], [xB[mt]])


        sel = sb("sel", [32, 4096])
        sgn = sb("sgn", [32, 4])
        rmask = sb("rmask", [32, 1024])
        maskf = sb("maskf", [128, 512], BF16); maskb = sb("maskb", [128, 512], BF16)
        perm = sb("perm", [128, 128], BF16)
        lamt = sb("lamt", [128, DEPTH, 4])
        for (t_, n_) in ((sel, "sel"), (sgn, "sgn"), (rmask, "rmask"), (maskf, "maskf"), (maskb, "maskb"), (perm, "perm")):
            kb.ld(t_[:], cdram[n_], [CB_])
        LAM_INIT = [0.8 - 0.6 * math.exp(-0.3 * l_) for l_ in range(DEPTH)]
        GROUPS = [([0], [(0, 256), (256, 256)]), ([1, 2], [(0, 1024)])]
        pcur = [0]
        OUTB = []

        def pbank():
            b = pcur[0] % 6
            pcur[0] += 1
            return b

        def mixer(l):
            W = w_in_d[l].rearrange("(kc p) n -> p kc n", p=128)
            cv.reset()
            lt = cv.take([128, 128]); ltB = cv.mk("lt")
            la = pv[:, l * PL + PV_LAM:l * PL + PV_LAM + 256].rearrange("p (a d) -> p a d", a=4)
            kb.tt(lt[:, 0:64], la[:, 0, :], la[:, 1, :], ALU.mult, [CB_], [ltB])
            kb.tt(lt[:, 64:128], la[:, 2, :], la[:, 3, :], ALU.mult, [CB_], [ltB])
            kb.op("dve", lambda e: e.reduce_sum(out=lamt[:, l, 0:2], in_=lt[:].rearrange("p (a d) -> p a d", a=2), axis=mybir.AxisListType.X), [ltB], [MB])
            kb.act(lamt[:, l, 0:2], lamt[:, l, 0:2], AF.Exp, [MB], [MB])
            kb.tt(lamt[:, l, 2:3], lamt[:, l, 0:1], lamt[:, l, 1:2], ALU.subtract, [MB], [MB])
            kb.ts(lamt[:, l, 2:3], lamt[:, l, 2:3], LAM_INIT[l], None, ALU.add, None, [MB], [MB])
            kb.ts(lamt[:, l, 3:4], lamt[:, l, 2:3], -1.0, None, ALU.mult, None, [MB], [MB])
            for gi, (mts, seqs) in enumerate(GROUPS):
                mixer_group(l, W, gi, mts, seqs)

        def mixer_group(l, W, gi, mts, seqs):
            g0 = mts[0] * 512
            T = len(mts) * 512
            nmt = len(mts)
            cnd = CND[mts[0]]
            cv.reset()
            sq = cv.take([128, 8, 512], BF16); sqB = cv.mk("sq")
            rs = cv.take([128, 512]); rsB = cv.mk("rs")
            tmp = cv.take([128, 512]); tmpB = cv.mk("tmp")
            norm_to_h(l, 1, mts, sq, sqB, rs, rsB, tmp, tmpB)
            cv.reset()
            hBs = [hB[mt] for mt in mts]

            def proj(srcs, shape, lhs_fn, M, evac):
                wv, wb = wload(srcs, shape)
                for mi, mt in enumerate(mts):
                    b = pbank()
                    for kc in range(8):
                        kb.mm(ps[b][0:M, :], lhs_fn(wv, kc), hT[:, kc, mt * 512:(mt + 1) * 512], kc == 0, kc == 7, [wb, hB[mt]], [pB[b]], sig=(kc == 7))
                    evac(mi, ps[b][0:M, :], pB[b])

            yT = cv.take([128, 8, T], BF16); yB = cv.mks("y", nmt)
            YEND = cv.off
            if 'ssd' in SKIP:
                for mi in range(nmt):
                    kb.memset(yT[:, :, mi * 512:(mi + 1) * 512], 0.0, [yB[mi]])
            dtT = cv.take([32, T]); aT_ = cv.take([32, T]); cumT = cv.take([32, T]); fmB = cv.mk("fm")
            SSD0 = cv.off
            t1 = cv.take([32, T]); t2 = cv.take([32, T]); t12B = cv.mk("t12")
            ac = cv.take([32, 2]); acB = cv.mk("ac")
            kb.act(ac[:, 0:1], pv[0:32, l * PL + PV_ALOG:l * PL + PV_ALOG + 1], AF.Exp, [CB_], [acB])
            kb.ts(ac[:, 1:2], ac[:, 0:1], -1.0, None, ALU.mult, None, [acB], [acB])

            def ev_dt(mi, pst, pb):
                kb.act(t1[:, mi * 512:(mi + 1) * 512], pst, AF.Identity, [pb, CB_], [t12B], bias=pv[0:32, l * PL + PV_DTB:l * PL + PV_DTB + 1])
            proj([(lambda v: v, W[:, :, OFF_DT:OFF_DT + 32])], [128, 8, 32], lambda wv, kc: wv[:, kc, :], 32, ev_dt)
            kb.ts(t2[:], t1[:], -1.0, None, ALU.mult, None, [t12B], [t12B])
            kb.tt(t2[:], t2[:], t1[:], ALU.max, [t12B], [t12B])
            kb.act(t2[:], t2[:], AF.Exp, [t12B], [t12B], scale=-1.0)
            kb.act(t2[:], t2[:], AF.Ln, [t12B], [t12B], bias=1.0)
            kb.ts(t1[:], t1[:], 0.0, None, ALU.max, None, [t12B], [t12B])
            kb.tt(dtT[:], t1[:], t2[:], ALU.add, [t12B], [fmB])
            kb.ts(aT_[:], dtT[:], ac[:, 1:2], None, ALU.mult, None, [fmB, acB], [fmB])
            for c0 in range(0, T, 1024):
                n_ = min(1024, T - c0)
                kb.op("dve", lambda e, c0=c0, n_=n_: e.tensor_tensor_scan(out=cumT[:, c0:c0 + n_], data0=rmask[:, 0:n_], data1=aT_[:, c0:c0 + n_],
                                                                      initial=0.0, op0=ALU.mult, op1=ALU.add), [fmB, CB_], [fmB])

            nseq = len(seqs)
            PW = T + 4 * nseq
            for j in range(0 if 'ssd' in SKIP else 2):
                cv.reset(SSD0)
                xbc = cv.take([128, 6, T], BF16); xbB = cv.mk("xbc")
                PC0 = cv.off
                pcs = Rot([(cv.take([128, PW]), cv.mk("pc%d" % i)) for i in range(1)])
                accs = Rot([(cv.take([128, PW]), cv.mk("acc%d" % i)) for i in range(1)])
                for (pc_, pcb_) in pcs.items:
                    kb.memset(pc_[:], 0.0, [pcb_])
                chunk_ids = [4 * j, 4 * j + 1, 4 * j + 2, 4 * j + 3, 8 + j, 10 + j]
                for ci, c in enumerate(chunk_ids):
                    pc_, pcb_ = pcs.next()
                    acc, accB = accs.next()

                    def ev_pc(mi, pst, pb, pc_=pc_, pcb_=pcb_):
                        for k, (o, L) in enumerate(seqs):
                            lo = max(o, mi * 512); hi = min(o + L, (mi + 1) * 512)
                            if lo < hi:
                                kb.cp(pc_[:, 2 + lo + 4 * k:2 + hi + 4 * k], pst[:, lo - mi * 512:hi - mi * 512], [pb], [pcb_], eng="act")
                    proj([(lambda v: v, W[:, :, OFF_XBC + c * 128:OFF_XBC + (c + 1) * 128])], [128, 8, 128], lambda wv, kc: wv[:, kc, :], 128, ev_pc)
                    n_ = PW - 4
                    cw = pv[:, l * PL + PV_CW + c * 5:l * PL + PV_CW + c * 5 + 5]
                    kb.ts(acc[:, 0:n_], pc_[:, 0:n_], cw[:, 0:1], None, ALU.mult, None, [pcb_, CB_], [accB])
                    for tp in range(1, 5):
                        kb.stt(acc[:, 0:n_], pc_[:, tp:tp + n_], cw[:, tp:tp + 1], acc[:, 0:n_], ALU.mult, ALU.add, [pcb_, CB_, accB], [accB])
                    for k, (o, L) in enumerate(seqs):
                        kb.act(xbc[:, ci, o:o + L], acc[:, o + 4 * k:o + 4 * k + L], AF.Silu, [accB, CB_], [xbB],
                               bias=pv[:, l * PL + PV_CB + c:l * PL + PV_CB + c + 1])
                cv.reset(PC0)
                Hf = cv.take([128, 512]); Hb = cv.take([128, 512]); Hfb = cv.take([128, 512], BF16); HB_ = cv.mk("H")
                ntmax = max(L for (_, L) in seqs) // 128
                Hbin = cv.take([128, ntmax, 512], BF16); HbinB = cv.mk("Hbin")
                sets = []
                for si in range(2):
                    S = {}
                    S["tok"] = cv.take([128, 96]); S["arg"] = cv.take([128, 32]); S["dte"] = cv.take([128, 32]); S["cd"] = cv.take([128, 32]); S["s2"] = cv.take([128, 32]); S["tokB"] = cv.mk("tok%d" % si)
                    S["btok"] = cv.take([128, 128], BF16); S["btB"] = cv.mk("btok%d" % si)
                    S["xdf"] = cv.take([128, 512], BF16); S["xdb"] = cv.take([128, 512], BF16); S["xdd"] = cv.take([128, 512], BF16); S["xdB"] = cv.mk("xd%d" % si)
                    S["Rt"] = cv.take([32, 128]); S["Cn"] = cv.take([32, 128]); S["EL"] = cv.take([32, 128]); S["tb_"] = cv.take([32, 128]); S["rB"] = cv.mk("R%d" % si)
                    S["ty"] = cv.take([128, 4, 128]); S["tyB"] = cv.mk("ty%d" % si)
                    sets.append(S)
                par = [0]
                tok = arg = dte = cd = s2 = tokB = btok = btB = xdf = xdb = xdd = xdB = Rt = Cn = EL = tb_ = rB = ty = tyB = None

                def nextset():
                    nonlocal tok, arg, dte, cd, s2, tokB, btok, btB, xdf, xdb, xdd, xdB, Rt, Cn, EL, tb_, rB, ty, tyB
                    S = sets[par[0] % 2]
                    par[0] += 1
                    tok, arg, dte, cd, s2, tokB = S["tok"], S["arg"], S["dte"], S["cd"], S["s2"], S["tokB"]
                    btok, btB = S["btok"], S["btB"]
                    xdf, xdb, xdd, xdB = S["xdf"], S["xdb"], S["xdd"], S["xdB"]
                    Rt, Cn, EL, tb_, rB = S["Rt"], S["Cn"], S["EL"], S["tb_"], S["rB"]
                    ty, tyB = S["ty"], S["tyB"]

                cbsR = Rot([(cv.take([128, 128]), cv.mk("cb%d" % i)) for i in range(2)])
                decs = Rot([(cv.take([128, 512]), cv.mk("dec%d" % i)) for i in range(2)])
                scs = [(cv.take([128, 4, 128], BF16), cv.mk("sc%d" % i)) for i in range(2)]
                ebcR = Rot([(cv.take([128, 512]), cv.mk("ebc%d" % i)) for i in range(2)])
                ces = [(cv.take([128, 4, 128], BF16), cv.mk("ce%d" % i)) for i in range(2)]
                segb = Rot([3, 6])
                PS_T, PS_S, PS_CB, PS_SEG, PS_E, PS_Y = 0, 1, 2, 3, 4, 5

                for k, (o, L) in enumerate(seqs):
                    nt = L // 128
                    kb.memset(Hf[:], 0.0, [HB_]); kb.memset(Hb[:], 0.0, [HB_])
                    if gi == 1:
                        for d_, H_ in ((0, Hf), (1, Hb)):
                            for gg in range(2):
                                kb.ld(H_[gg * 64:(gg + 1) * 64, gg * 256:(gg + 1) * 256],
                                      st0_d[l, d_][:, (8 * j + 4 * gg) * 64:(8 * j + 4 * gg + 4) * 64], [HB_])
                    kb.cp(Hfb[:], Hf[:], [HB_], [HB_])

                    def prep(i):
                        tsl = slice(o + i * 128, o + (i + 1) * 128)
                        kb.tr(ps[PS_T][:, 0:32], dtT[:, tsl], identf[0:32, 0:32], [fmB, CB_], [pB[PS_T]], sig=False)
                        kb.tr(ps[PS_T][:, 32:64], aT_[:, tsl], identf[0:32, 0:32], [fmB, CB_], [pB[PS_T]], sig=False)
                        kb.tr(ps[PS_T][:, 64:96], cumT[:, tsl], identf[0:32, 0:32], [fmB, CB_], [pB[PS_T]], sig=True)
                        kb.cp(tok[:], ps[PS_T][:, 0:96], [pB[PS_T]], [tokB], eng="act")
                        kb.mm(ps[PS_T][:, 96:128], onesf[:], tok[:, 32:64], True, True, [tokB, CB_], [pB[PS_T]], sig=True)
                        kb.tt(arg[:, 0:16], ps[PS_T][:, 96:112], tok[:, 64:80], ALU.subtract, [pB[PS_T], tokB], [tokB])
                        kb.tt(arg[:, 16:32], tok[:, 80:96], tok[:, 48:64], ALU.subtract, [tokB], [tokB])
                        kb.act(dte[:], arg[:], AF.Exp, [tokB], [tokB])
                        kb.act(cd[:], ps[PS_T][:, 96:128], AF.Exp, [pB[PS_T]], [tokB])
                        kb.tt(s2[:], tok[:, 0:32], dte[:], ALU.mult, [tokB], [tokB])
                        for ci in range(4):
                            kb.tr(psb[:, ci * 128:(ci + 1) * 128], xbc[:, ci, tsl], identb[:], [xbB, CB_], [psbB], sig=False)
                        kb.tr(psb[:, 512:640], xbc[:, 4, tsl], identb[:], [xbB, CB_], [psbB], sig=True)
                        kb.cp(btok[:], psb[:, 512:640], [psbB], [btB])
                        return tsl

                    def xprod(out, col0):
                        kb.tt(out[:].rearrange("p (h d) -> p h d", h=8), psb[:, 0:512].rearrange("p (h d) -> p h d", h=8),
                              col0.unsqueeze(2).to_broadcast([128, 8, 64]), ALU.mult, [psbB, tokB], [xdB])

                    def state_update(H_, xsrc, cdcol):
                        kb.mm(ps[PS_S][:], btok[:], xsrc[:], True, True, [btB, xdB], [pB[PS_S]], sig=True)
                        kb.tt(H_[:].rearrange("p (h d) -> p h d", h=8), H_[:].rearrange("p (h d) -> p h d", h=8),
                              cdcol.unsqueeze(2).to_broadcast([128, 8, 64]), ALU.mult, [HB_, tokB], [HB_])
                        kb.tt(H_[:], H_[:], ps[PS_S][:], ALU.add, [HB_, pB[PS_S]], [HB_])

                    for i in range(nt - 1, -1, -1):
                        nextset()
                        prep(i)
                        kb.cp(Hbin[:, i, :], Hb[:], [HB_], [HbinB])
                        xprod(xdd, s2[:, 16 + 8 * j:16 + 8 * j + 8])
                        state_update(Hb, xdd, cd[:, 16 + 8 * j:16 + 8 * j + 8])
                    for i in range(nt):
                        nextset()
                        tsl = prep(i)
                        xprod(xdf, tok[:, 8 * j:8 * j + 8])
                        xprod(xdb, tok[:, 16 + 8 * j:16 + 8 * j + 8])
                        xprod(xdd, s2[:, 8 * j:8 * j + 8])
                        kb.ts(tb_[:], aT_[:, tsl], sgn[:, 1:2], None, ALU.mult, None, [fmB, CB_], [rB])
                        kb.stt(Rt[:], cumT[:, tsl], sgn[:, 0:1], tb_[:], ALU.mult, ALU.add, [fmB, CB_, rB], [rB])
                        kb.ts(Cn[:], Rt[:], -1.0, None, ALU.mult, None, [rB], [rB])
                        last = o + i * 128 + 127
                        kb.stt(EL[:], cumT[:, last:last + 1].to_broadcast([32, 128]), sgn[:, 1:2], Rt[:], ALU.mult, ALU.add, [fmB, CB_, rB], [rB])
                        for gg in range(2):
                            r0 = gg * 64
                            cbs, cbB = cbsR.next()
                            kb.mm(ps[PS_CB][:, 0:128], xbc[r0:r0 + 64, 4, tsl], xbc[r0:r0 + 64, 5, tsl], True, True, [xbB], [pB[PS_CB]], sig=True)
                            kb.cp(cbs[:], ps[PS_CB][:, 0:128], [pB[PS_CB]], [cbB], eng="act")
                            for d_ in range(2):
                                msk = maskf if d_ == 0 else maskb
                                sc_, scB = scs[d_]
                                ce_, ceB = ces[d_]
                                dec, decB = decs.next()
                                PS_SEG = segb.next()
                                ebc, ebB = ebcR.next()
                                kb.mm(ps[PS_SEG][:], identb[:], msk[:], True, False, [CB_], [pB[PS_SEG]], sig=False)
                                for hh in range(4):
                                    dh = d_ * 16 + 8 * j + 4 * gg + hh
                                    kb.mm(ps[PS_SEG][:, hh * 128:(hh + 1) * 128], sel[:, dh * 128:(dh + 1) * 128], Rt[:], False, False, [CB_, rB], [pB[PS_SEG]], sig=False)
                                    kb.mm(ps[PS_SEG][:, hh * 128:(hh + 1) * 128], Cn[:], sel[:, dh * 128:(dh + 1) * 128], False, hh == 3, [CB_, rB], [pB[PS_SEG]], sig=(hh == 3))
                                kb.act(dec[:], ps[PS_SEG][:], AF.Exp, [pB[PS_SEG]], [decB])
                                kb.tt(sc_[:], dec[:].rearrange("p (h s) -> p h s", h=4), cbs[:].unsqueeze(1).to_broadcast([128, 4, 128]), ALU.mult, [decB, cbB], [scB])
                                for hh in range(4):
                                    dh = d_ * 16 + 8 * j + 4 * gg + hh
                                    kb.mm(ps[PS_E][:, hh * 128:(hh + 1) * 128], sel[:, dh * 128:(dh + 1) * 128], EL[:], True, True, [CB_, rB], [pB[PS_E]], sig=(hh == 3))
                                kb.act(ebc[:], ps[PS_E][:], AF.Exp, [pB[PS_E]], [ebB])
                                kb.tt(ce_[r0:r0 + 64], ebc[r0:r0 + 64, :].rearrange("p (h s) -> p h s", h=4),
                                      xbc[r0:r0 + 64, 5, tsl].unsqueeze(1).to_broadcast([64, 4, 128]), ALU.mult, [ebB, xbB], [ceB])
                            for hh in range(4):
                                hl = gg * 4 + hh
                                cl = hl // 2
                                yr = (hl % 2) * 64
                                out = ps[PS_Y][yr:yr + 64, cl * 128:(cl + 1) * 128]
                                hs = slice(hl * 64, (hl + 1) * 64)
                                kb.mm(out, xdf[:, hs], scs[0][0][:, hh, :], True, False, [xdB, scs[0][1]], [pB[PS_Y]], sig=False)
                                kb.mm(out, xdb[:, hs], scs[1][0][:, hh, :], False, False, [xdB, scs[1][1]], [pB[PS_Y]], sig=False)
                                kb.mm(out, Hfb[r0:r0 + 64, hs], ces[0][0][r0:r0 + 64, hh, :], False, False, [HB_, ces[0][1]], [pB[PS_Y]], sig=False)
                                kb.mm(out, Hbin[r0:r0 + 64, i, hs], ces[1][0][r0:r0 + 64, hh, :], False, True, [HbinB, ces[1][1]], [pB[PS_Y]], sig=True)
                        dv = pv[:, l * PL + PV_DV + 4 * j:l * PL + PV_DV + 4 * j + 4]
                        kb.tt(ty[:], xbc[:, 0:4, tsl], dv.unsqueeze(2).to_broadcast([128, 4, 128]), ALU.mult, [xbB, CB_], [tyB])
                        kb.tt(yT[:, 4 * j:4 * j + 4, tsl], ty[:], ps[PS_Y][:].rearrange("p (c s) -> p c s", c=4), ALU.add, [tyB, pB[PS_Y]], yB)
                        state_update(Hf, xdd, cd[:, 8 * j:8 * j + 8])
                        kb.cp(Hfb[:], Hf[:], [HB_], [HB_])
                    if gi == 0:
                        for d_, H_ in ((0, Hf), (1, Hb)):
                            for gg in range(2):
                                kb.st(ns_d[k, l, d_][:, (8 * j + 4 * gg) * 64:(8 * j + 4 * gg + 4) * 64],
                                      H_[gg * 64:(gg + 1) * 64, gg * 256:(gg + 1) * 256], [HB_])
                        kb.final_wait("dve", [HB_])
                        OUTB.append(HB_)

            cv.reset(YEND)
            merged = cv.take([128, 8, T], BF16); mgB = cv.mks("mg", nmt)
            mergedb = merged; mgbB = mgB
            tgs = Rot([(cv.take([128, 512]), cv.mk("tg%d" % i)) for i in range(2)])
            tms = Rot([(cv.take([128, 512]), cv.mk("tm%d" % i)) for i in range(2)])
            MRG2 = cv.off
            szs = Rot([(cv.take([128, T]), cv.mk("sz%d" % i)) for i in range(2)])
            for c in range(8):
                sz, szB = szs.next()

                def ev_z(mi, pst, pb, sz=sz, szB=szB):
                    kb.act(sz[:, mi * 512:(mi + 1) * 512], pst, AF.Silu, [pb], [szB])
                proj([(lambda v: v, W[:, :, OFF_Z + c * 128:OFF_Z + (c + 1) * 128])], [128, 8, 128], lambda wv, kc: wv[:, kc, :], 128, ev_z)
                kb.tt(yT[:, c, :], yT[:, c, :], sz[:], ALU.mult, yB + [szB], yB)
            sq = cv.take([128, 8, 512], BF16); sqB = cv.mk("sq")
            rs = cv.take([128, 512]); rsB = cv.mk("rs")
            tmp = cv.take([128, 512]); tmpB = cv.mk("tmp")
            for mi in range(nmt):
                cs = slice(mi * 512, (mi + 1) * 512)
                for kc in range(8):
                    kb.act(sq[:, kc, :], yT[:, kc, cs], AF.Square, yB, [sqB])
                for kc in range(8):
                    kb.mm(ps[6][:], onesb[:], sq[:, kc, :], kc == 0, kc == 7, [sqB, CB_], [pB[6]], sig=(kc == 7))
                rstd_from_ps(ps[6][:], pB[6], 512, D, rs, rsB, tmp, tmpB)
                for kc in range(8):
                    kb.stt(yT[:, kc, cs], yT[:, kc, cs], pv[:, l * PL + PV_SNG + kc:l * PL + PV_SNG + kc + 1], rs, ALU.mult, ALU.mult, yB + [CB_, rsB], yB)


            def merge(n):
                for m in range(8):
                    wv, wb = wload([(lambda v: v[:, :, 0, :], wbr_d[l, n].rearrange("(kc p) n -> p kc n", p=128)[:, :, m * 128:(m + 1) * 128]),
                                    (lambda v: v[:, :, 1, :], W[:, :, OFF_G + n * 1024 + m * 128:OFF_G + n * 1024 + (m + 1) * 128])], [128, 8, 2, 128])
                    for mi, mt in enumerate(mts):
                        cs = slice(mi * 512, (mi + 1) * 512)
                        bp = pbank(); bq = pbank()
                        for kc in range(8):
                            kb.mm(ps[bp][:], wv[:, kc, 0, :], yT[:, kc, cs], kc == 0, kc == 7, [wb] + yB, [pB[bp]], sig=(kc == 7))
                        for kc in range(8):
                            kb.mm(ps[bq][:], wv[:, kc, 1, :], hT[:, kc, mt * 512:(mt + 1) * 512], kc == 0, kc == 7, [wb, hB[mt]], [pB[bq]], sig=(kc == 7))
                        tg, tgB = tgs.next()
                        kb.act(tg[:], ps[bq][:], AF.Tanh, [pB[bq]], [tgB], scale=0.5)
                        if n == 0:
                            kb.stt(merged[:, m, cs], tg[:], 1.0, ps[bp][:], ALU.add, ALU.mult, [tgB, pB[bp]], [mgB[mi]])
                        else:
                            tm, tmB = tms.next()
                            kb.stt(tm[:], tg[:], 1.0, ps[bp][:], ALU.add, ALU.mult, [tgB, pB[bp]], [tmB])
                            if n == 1:
                                kb.tt(merged[:, m, cs], merged[:, m, cs], tm[:], ALU.add, [mgB[mi], tmB], [mgB[mi]])
                            else:
                                kb.tt(mergedb[:, m, cs], merged[:, m, cs], tm[:], ALU.add, [mgB[mi], tmB], [mgbB[mi]])

            if 'ssd' in SKIP:
                for mi in range(nmt):
                    kb.memset(yT[:, :, mi * 512:(mi + 1) * 512], 0.0, [yB[mi]])
            merge(0)

            cv.reset(MRG2)
            Tk = T + (512 if gi == 1 else 0)
            ntk = Tk // 128
            qh = cv.take([128, T], BF16); qB = cv.mk("qh")
            kh = cv.take([128, Tk], BF16); kB_ = cv.mk("kh")
            vh = cv.take([128, ntk, 128], BF16); vB = cv.mk("vh")
            stg = cv.take([128, 4, 128]); stgB = cv.mk("stg")
            OUTB.append(stgB)
            qraw = cv.take([128, 512], BF16); qrB = cv.mk("qraw")
            r1 = cv.take([128, 512]); r2 = cv.take([128, 512]); rrB = cv.mk("rr")
            Pt = Rot([(cv.take([128, 2, 256], BF16), cv.mk("P%d" % i)) for i in range(3)])
            rden = cv.take([128, 2, 256]); on_ = cv.take([128, 2, 256]); od = cv.take([128, 256]); odsq = cv.take([128, 256], BF16); nB = cv.mk("nrm")
            rs2 = cv.take([128, 256]); tmp2 = cv.take([128, 256]); rs2B = cv.mk("rs2")
            gn = cv.take([128, 2]); gnB = cv.mk("gn")
            kb.ts(gn[:, 0:1], pv[:, l * PL + PV_DNG:l * PL + PV_DNG + 1], 1.0 - LAM_INIT[l], None, ALU.mult, None, [CB_], [gnB])
            if gi == 1:
                cos = cv.take([128, 1024]); sin = cv.take([128, 1024]); csB = cv.mk("cs")
                kb.ld(cos[:], cdram["cos"], [csB]); kb.ld(sin[:], cdram["sin"], [csB])
            PSC = [0, 1]
            PO, PD = 4, 5

            chk("att%d_pre" % gi)
            for h in range(0 if ('att' in SKIP or ('att%d' % gi) in SKIP) else 8):
                wv, wb = wload([(lambda v: v[:, :, 0, :], W[:, :, OFF_Q + h * 128:OFF_Q + (h + 1) * 128]),
                                (lambda v: v[:, :, 1, :], W[:, :, OFF_K + h * 128:OFF_K + (h + 1) * 128]),
                                (lambda v: v[:, :, 2, :], W[:, :, OFF_V + h * 128:OFF_V + (h + 1) * 128])], [128, 8, 3, 128])
                koff = Tk - T
                if gi == 1:
                    kb.dma("pool", lambda e, h=h: e.dma_start(out=kh[:, 0:512], in_=ckT_d[l, h]), (), [kB_])
                    kb.dma("pool", lambda e, h=h: e.dma_start(out=vh[:, 0:4, :], in_=cv_d[l][:, h, :].rearrange("(t p) e -> p t e", p=128)), (), [vB])
                for which, dst, dB, off in ((0, qh, qB, 0), (1, kh, kB_, koff)):
                    for mi, mt in enumerate(mts):
                        b = 4 + (pcur[0] % 2); pcur[0] += 1
                        for kc in range(8):
                            kb.mm(ps[b][:], wv[:, kc, which, :], hT[:, kc, mt * 512:(mt + 1) * 512], kc == 0, kc == 7, [wb, hB[mt]], [pB[b]], sig=(kc == 7))
                        dcs = slice(off + mi * 512, off + (mi + 1) * 512)
                        if gi == 0:
                            kb.cp(dst[:, dcs], ps[b][:], [pB[b]], [dB], eng="act")
                        else:
                            tcs = slice(mi * 512, (mi + 1) * 512)
                            kb.cp(qraw[:], ps[b][:], [pB[b]], [qrB], eng="act")
                            kb.mm(ps[6][:], perm[:], qraw[:], True, True, [CB_, qrB], [pB[6]], sig=True)
                            kb.tt(r1[:], ps[b][:], cos[:, tcs], ALU.mult, [pB[b], csB, qrB], [rrB])
                            kb.tt(r2[:], ps[6][:], sin[:, tcs], ALU.mult, [pB[6], csB], [rrB])
                            kb.tt(dst[:, dcs], r1[:], r2[:], ALU.add, [rrB], [dB])
                chk("att%d_h%d_qk" % (gi, h))
                for t0 in range(0, 0 if 'nov' in SKIP else T // 128, 4):
                    b = 4 + (pcur[0] % 2); pcur[0] += 1
                    for tt_ in range(4):
                        tl = t0 + tt_
                        for kc in range(8):
                            kb.mm(ps[b][:, tt_ * 128:(tt_ + 1) * 128], hT[:, kc, g0 + tl * 128:g0 + (tl + 1) * 128], wv[:, kc, 2, :], kc == 0, kc == 7,
                                  [wb] + hBs, [pB[b]], sig=(kc == 7 and tt_ == 3))
                    if gi == 0 and 'nost' not in SKIP:
                        kb.cp(stg[:], ps[b][:].rearrange("p (t e) -> p t e", t=4), [pB[b]], [stgB])
                        kb.cp(vh[:, koff // 128 + t0:koff // 128 + t0 + 4, :], stg[:], [stgB], [vB], eng="act")
                    else:
                        kb.cp(vh[:, koff // 128 + t0:koff // 128 + t0 + 4, :], ps[b][:].rearrange("p (t e) -> p t e", t=4), [pB[b]], [vB], eng="act")
                    if gi == 0 and 'nost' not in SKIP:
                        for k in range(0 if 'nodma' in SKIP else 2):
                            kb.st(nv_d[k, l][:, h, :].rearrange("(t p) e -> p t e", p=128), stg[:, 2 * k:2 * k + 2, :], [stgB])
                        if 'nokst' in SKIP:
                            continue
                        b2 = 4 + (pcur[0] % 2); pcur[0] += 1
                        for tt_ in range(4):
                            tl = t0 + tt_
                            for kc in range(8):
                                kb.mm(ps[b2][:, tt_ * 128:(tt_ + 1) * 128], hT[:, kc, g0 + tl * 128:g0 + (tl + 1) * 128], wv[:, kc, 1, :], kc == 0, kc == 7,
                                      [wb] + hBs, [pB[b2]], sig=(kc == 7 and tt_ == 3))
                        kb.cp(stg[:], ps[b2][:].rearrange("p (t e) -> p t e", t=4), [pB[b2]], [stgB])
                        for k in range(0 if 'nodma' in SKIP else 2):
                            kb.st(nk_d[k, l][:, h, :].rearrange("(t p) e -> p t e", p=128), stg[:, 2 * k:2 * k + 2, :], [stgB])
                for k, (o, L) in enumerate(seqs if 'nocore' not in SKIP else []):
                    if gi == 0:
                        ktiles = list(range(o // 128, (o + L) // 128))
                    else:
                        ktiles = list(range(ntk))
                    for qb in range(L // 256):
                        qs = slice(o + qb * 256, o + (qb + 1) * 256)
                        nk_ = len(ktiles)
                        Ps = {}

                        def score(ki):
                            kt = ktiles[ki]
                            for c_ in range(2):
                                bs = (ki % 2) * 2 + c_
                                kb.mm(ps[bs][:, 0:256], kh[c_ * 64:(c_ + 1) * 64, kt * 128:(kt + 1) * 128], qh[c_ * 64:(c_ + 1) * 64, qs],
                                      True, True, [kB_, qB], [pB[bs]], sig=True)
                            P_, PB_ = Pt.next()
                            Ps[ki] = (P_, PB_)
                            for c_ in range(2):
                                bs = (ki % 2) * 2 + c_
                                kb.act(P_[:, c_, :], ps[bs][:, 0:256], AF.Exp, [pB[bs]], [PB_], scale=0.125)

                        score(0)
                        for ki, kt in enumerate(ktiles):
                            if ki + 1 < nk_:
                                score(ki + 1)
                            P_, PB_ = Ps.pop(ki)
                            kb.mm(ps[PO][:], vh[:, kt, :], P_[:].rearrange("p c q -> p (c q)"), ki == 0, ki == nk_ - 1, [vB, PB_], [pB[PO]], sig=False)
                            kb.mm(ps[PD][:], onesb[:], P_[:].rearrange("p c q -> p (c q)"), ki == 0, ki == nk_ - 1, [CB_, PB_], [pB[PD]], sig=True)
                        if 'nonorm' in SKIP:
                            kb.cp(rden[:].rearrange("p c q -> p (c q)"), ps[PD][:], [pB[PD]], [nB])
                            kb.cp(on_[:].rearrange("p c q -> p (c q)"), ps[PO][:], [pB[PO]], [nB])
                            continue
                        kb.recip(rden[:].rearrange("p c q -> p (c q)"), ps[PD][:], [pB[PD]], [nB])
                        kb.tt(on_[:].rearrange("p c q -> p (c q)"), ps[PO][:], rden[:].rearrange("p c q -> p (c q)"), ALU.mult, [pB[PO], nB], [nB])
                        kb.stt(od[:], on_[:, 1, :], lamt[:, l, 3:4], on_[:, 0, :], ALU.mult, ALU.add, [nB, MB], [nB])
                        kb.act(odsq[:], od[:], AF.Square, [nB], [nB])
                        kb.mm(ps[6][:, 0:256], onesb[:], odsq[:], True, True, [CB_, nB], [pB[6]], sig=True)
                        rstd_from_ps(ps[6][:, 0:256], pB[6], 256, 128, rs2, rs2B, tmp2, rs2B)
                        kb.stt(yT[:, h, qs], od[:], gn[:, 0:1], rs2[:], ALU.mult, ALU.mult, [nB, gnB, rs2B], yB)
            if 'att' in SKIP:
                for mi in range(nmt):
                    kb.memset(yT[:, :, mi * 512:(mi + 1) * 512], 0.0, [yB[mi]])
            chk("g%d_att" % gi)
            merge(1)
            chk("g%d_m1" % gi)

            cv.reset(MRG2)
            PP = T + 16 * nseq
            pu = Rot([(cv.take([128, PP]), cv.mk("pu%d" % i)) for i in range(2)])
            for (p_, pb_) in pu.items:
                kb.memset(p_[:], 0.0, [pb_])
            lv = [(cv.take([128, PP]), cv.mk("lv%d" % i)) for i in range(2)]
            pooled = cv.take([128, 2, T], BF16); plB = cv.mk("pooled")
            invc = cv.take([128, 4, PP], BF16); ivB = cv.mk("invc")
            kb.ld(invc[:].rearrange("p g n -> p (g n)"), cdram["invc%d" % gi], [ivB])
            for g in range(0 if 'pool' in SKIP else 4):
                for ci in range(2):
                    c = 2 * g + ci
                    pu_, puB = pu.next()

                    def ev_u(mi, pst, pb, pu_=pu_, puB=puB):
                        for k, (o, L) in enumerate(seqs):
                            lo = max(o, mi * 512); hi = min(o + L, (mi + 1) * 512)
                            if lo < hi:
                                kb.cp(pu_[:, 8 + lo + 16 * k:8 + hi + 16 * k], pst[:, lo - mi * 512:hi - mi * 512], [pb], [puB], eng="act")
                    proj([(lambda v: v, W[:, :, OFF_U + c * 128:OFF_U + (c + 1) * 128])], [128, 8, 128], lambda wv, kc: wv[:, kc, :], 128, ev_u)
                    src, srcB = pu_, puB
                    A_, AB_ = lv[0]
                    kb.tt(A_[:, 1:PP], src[:, 0:PP - 1], src[:, 1:PP], ALU.add, [srcB], [AB_])
                    kb.memset(A_[:, 0:1], 0.0, [AB_])
                    cur, curB = A_, AB_
                    for step, sh in enumerate((1, 2, 4)[:g]):
                        nx, nxB = lv[(step + 1) % 2]
                        kb.memset(nx[:, 0:sh], 0.0, [nxB]); kb.memset(nx[:, PP - sh:PP], 0.0, [nxB])
                        kb.tt(nx[:, sh:PP - sh], cur[:, 0:PP - 2 * sh], cur[:, 2 * sh:PP], ALU.add, [curB], [nxB])
                        cur, curB = nx, nxB
                    oth, othB = lv[0] if cur is lv[1][0] else lv[1]
                    kb.tt(oth[:], cur[:], invc[:, g, :], ALU.mult, [curB, ivB], [othB])
                    for k, (o, L) in enumerate(seqs):
                        kb.tt(pooled[:, ci, o:o + L], oth[:, 8 + o + 16 * k:8 + o + 16 * k + L], pu_[:, 8 + o + 16 * k:8 + o + 16 * k + L], ALU.subtract, [othB, puB], [plB])
                wv, wb = wload([(lambda v: v, pmap_d[l, g].rearrange("(kc p) n -> p kc n", p=128))], [128, 2, 256])
                for e_ in range(2):
                    for mi in range(nmt):
                        b = pbank()
                        for kc in range(2):
                            kb.mm(ps[b][:], wv[:, kc, e_ * 128:(e_ + 1) * 128], pooled[:, kc, mi * 512:(mi + 1) * 512], kc == 0, kc == 1, [wb, plB], [pB[b]], sig=(kc == 1))
                        kb.ts(yT[:, 2 * g + e_, mi * 512:(mi + 1) * 512], ps[b][:], pv[:, l * PL + PV_PS + 2 * g + e_:l * PL + PV_PS + 2 * g + e_ + 1], None, ALU.mult, None,
                              [pB[b], CB_], yB)
            chk("g%d_pool" % gi)
            merge(2)
            chk("g%d_m2" % gi)

            wo = wo_d[l].rearrange("(kc p) n -> p kc n", p=128)
            for m in range(8):
                wv, wb = wload([(lambda v: v, wo[:, :, m * 128:(m + 1) * 128])], [128, 8, 128])
                for mi, mt in enumerate(mts):
                    b = pbank()
                    for kc in range(8):
                        kb.mm(ps[b][:], wv[:, kc, :], mergedb[:, kc, mi * 512:(mi + 1) * 512], kc == 0, kc == 7, [wb, mgbB[mi]], [pB[b]], sig=(kc == 7))
                    cs = slice(mt * 512, (mt + 1) * 512)
                    kb.stt(xT[:, m, cs], ps[b][:], Gmod[:, l, 1, m, cnd:cnd + 1], xT[:, m, cs], ALU.mult, ALU.add, [pB[b], MB, xB[mt]], [xB[mt]])
            chk("g%d_out" % gi)

        try:
            for l in range(n_layers):
                ffn(l, 0)
                if do_mixer:
                    mixer(l)
                ffn(l, 1)
        except StopBuild as ex:
            print("STOPPED at", ex)

        cv.reset()
        sq = cv.take([128, 8, 512], BF16); sqB = cv.mk("sq")
        rs = cv.take([128, 512]); rsB = cv.mk("rs")
        tmp = cv.take([128, 512]); tmpB = cv.mk("tmp")
        yo = cv.take([128, 8, NT]); yoB = cv.mks("yo", 3)
        fg = pv[:, DEPTH * PL:DEPTH * PL + 8]
        for mt in range(3):
            cs = slice(mt * 512, (mt + 1) * 512)
            for kc in range(8):
                kb.act(sq[:, kc, :], xT[:, kc, cs], AF.Square, [xB[mt]], [sqB])
            for kc in range(8):
                kb.mm(ps[6][:], onesb[:], sq[:, kc, :], kc == 0, kc == 7, [sqB, CB_], [pB[6]], sig=(kc == 7))
            rstd_from_ps(ps[6][:], pB[6], 512, D, rs, rsB, tmp, tmpB)
            for kc in range(8):
                kb.stt(yo[:, kc, cs], xT[:, kc, cs], fg[:, kc:kc + 1], rs, ALU.mult, ALU.mult, [xB[mt], CB_, rsB], [yoB[mt]])
            kb.st(yT_d.rearrange("(c p) t -> p c t", p=128)[:, :, cs], yo[:, :, cs], [yoB[mt]])
        if os.environ.get("KVERB"):
            print("phase peak bytes", cv.peak, "of", PH, "nins", kb.nins, {e: len(kb.prog[e]) for e in kb.ENGS})
        kb.final_wait("sp", yoB + [DBGB] + OUTB)
        kb.emit(st)
    return nc


_NC_CACHE = {}


def prep_inputs(inp):
    f = lambda a: np.ascontiguousarray(np.asarray(a, dtype=np.float32))
    consts = host_consts()
    shared = {"w_ada": f(inp["w_ada"]), "ffn_w_in": f(inp["ffn_w_in"]), "ffn_w_out": f(inp["ffn_w_out"]),
              "w_in": f(inp["w_in"]), "pool_map": f(inp["pool_map"]), "w_branch": f(inp["w_branch"]), "w_out": f(inp["w_out"])}
    for k, v in consts.items():
        shared["c_" + k] = np.ascontiguousarray(v)
    pvv = np.zeros((128, DEPTH * PL + 8), np.float32)
    for l in range(DEPTH):
        o = l * PL
        pvv[:, o + PV_BADA:o + PV_BADA + 72] = f(inp["b_ada"])[l].reshape(72, 128).T
        pvv[:, o + PV_NG:o + PV_NG + 24] = f(inp["norm_gain"])[l].reshape(24, 128).T
        cw = f(inp["ssd_conv_w"])[l].reshape(5, 12, 128)
        pvv[:, o + PV_CW:o + PV_CW + 60] = cw.transpose(2, 1, 0).reshape(128, 60)
        pvv[:, o + PV_CB:o + PV_CB + 12] = f(inp["ssd_conv_b"])[l].reshape(12, 128).T
        pvv[:, o + PV_SNG:o + PV_SNG + 8] = f(inp["ssd_norm_gain"])[l].reshape(8, 128).T
        pvv[:, o + PV_PS:o + PV_PS + 8] = f(inp["pool_scale"])[l].reshape(8, 128).T
        pvv[:, o + PV_DV:o + PV_DV + 8] = np.repeat(f(inp["ssd_d"])[l], 64).reshape(8, 128).T
        pvv[:, o + PV_DNG] = f(inp["diff_norm_gain"])[l]
        pvv[:, o + PV_LAM:o + PV_LAM + 256] = f(inp["diff_lambda"])[l].reshape(1, 256)
        pvv[:32, o + PV_DTB] = f(inp["ssd_dt_bias"])[l].reshape(32)
        pvv[:32, o + PV_ALOG] = f(inp["ssd_a_log"])[l].reshape(32)
    pvv[:, DEPTH * PL:] = f(inp["final_gain"]).reshape(8, 128).T
    shared["pv"] = pvv
    xp = f(inp["x_prompt"]); xs = f(inp["x_sample"])
    ck = f(inp["cache_k"]); cvv = f(inp["cache_v"]); s0 = f(inp["state_ssm"])
    cc = f(inp["c"]); cctx = f(inp["c_ctx"])
    in_maps = []
    for c in range(8):
        m = dict(shared)
        xt = np.concatenate([xp[2 * c], xp[2 * c + 1], xs[c]], axis=0)
        m["xT"] = np.ascontiguousarray(xt.T)
        cd = np.zeros((128, 8, 2), np.float32)
        cd[:, :, 0] = cctx.reshape(8, 128).T
        cd[:, :, 1] = cc[c].reshape(8, 128).T
        m["cond"] = cd.reshape(128, 16)
        m["ckT"] = np.ascontiguousarray(ck[c].transpose(0, 2, 3, 1))
        m["cv"] = np.ascontiguousarray(cvv[c])
        m["st0"] = np.ascontiguousarray(s0[c].transpose(0, 1, 4, 2, 3).reshape(DEPTH, 2, 64, 1024))
        in_maps.append(m)
    return in_maps


def kernel(**inputs):
    in_maps = prep_inputs(inputs)
    key = "full"
    if key not in _NC_CACHE:
        _NC_CACHE[key] = build()
    nc = _NC_CACHE[key]
    res = run_bass_kernel_spmd(nc, in_maps, core_ids=list(range(8)))
    y_prompt = np.zeros((16, 256, D), np.float32)
    y_sample = np.zeros((8, 1024, D), np.float32)
    nk = np.zeros((16, DEPTH, 256, 8, 128), np.float32)
    nv = np.zeros((16, DEPTH, 256, 8, 128), np.float32)
    ns = np.zeros((16, DEPTH, 2, 16, 64, 64), np.float32)
    for c in range(8):
        r = res.results[c]
        y = np.asarray(r["yT"]).T
        y_prompt[2 * c] = y[0:256]
        y_prompt[2 * c + 1] = y[256:512]
        y_sample[c] = y[512:1536]
        nk[2 * c:2 * c + 2] = np.asarray(r["nk"])
        nv[2 * c:2 * c + 2] = np.asarray(r["nv"])
        s = np.asarray(r["ns"]).reshape(2, DEPTH, 2, 64, 16, 64)
        ns[2 * c:2 * c + 2] = s.transpose(0, 1, 2, 4, 5, 3)
    return (y_prompt, y_sample, nk, nv, ns)
```

```python
import math, os, sys
SKIP = set(os.environ.get('KSKIP', '').split(','))
import numpy as np
from contextlib import ExitStack
import ml_dtypes
import concourse.bass as bass
import concourse.mybir as mybir
from concourse.bass_utils import run_bass_kernel_spmd

F32 = mybir.dt.float32
BF16 = mybir.dt.bfloat16
AF = mybir.ActivationFunctionType
ALU = mybir.AluOpType

D = 1024
DEPTH = 4
DFF = 2816
NT = 1536
EPS = 1e-6
IN_W = 9760
OFF_Z, OFF_XBC, OFF_DT, OFF_Q, OFF_K, OFF_V, OFF_U, OFF_G = 0, 1024, 2560, 2592, 3616, 4640, 5664, 6688
PV_BADA, PV_NG, PV_CW, PV_CB, PV_SNG, PV_PS, PV_DV, PV_DNG, PV_LAM, PV_DTB, PV_ALOG = 0, 72, 96, 156, 168, 176, 184, 192, 193, 449, 450
PL = 451
NEG = -30000.0
DEBUG_ANNOT = bool(os.environ.get('KANNOT'))
STOPAT = os.environ.get('KSTOP', '')


class StopBuild(Exception):
    pass


def chk(name):
    if STOPAT and name == STOPAT:
        raise StopBuild(name)


class Buf:
    __slots__ = ("name", "w", "r", "dsem", "dcnt")

    def __init__(self, name):
        self.name = name
        self.w = None
        self.r = []
        self.dsem = None
        self.dcnt = 0


class KB:
    ENGS = ("pe", "dve", "act", "pool", "sp")

    def __init__(self, nc, n_dma_sems=70):
        self.nc = nc
        self.prog = {e: [] for e in self.ENGS}
        self.cnt = {e: 0 for e in self.ENGS}
        self.waited = {}
        self.dma_free = ["d%d" % i for i in range(n_dma_sems)]
        self.sem_names = list(self.ENGS) + list(self.dma_free)
        self.sems = {}
        self.pending = {e: False for e in self.ENGS}
        self.nins = 0
        self.dbufs = []

    def _deps(self, eng, reads, writes):
        toks = []
        for b in reads:
            if b.w is not None:
                toks.append(b.w)
        for b in writes:
            if b.w is not None:
                toks.append(b.w)
            toks.extend(b.r)
        best = {}
        for (sk, v) in toks:
            if sk == "pe" and eng == "pe":
                continue
            if sk in self.cnt and v > self.cnt[sk]:
                if sk == eng:
                    continue
                raise RuntimeError("dep on open nosig group %s (eng %s)" % (sk, eng))
            if v > best.get(sk, 0):
                best[sk] = v
        for sk, v in best.items():
            if self.waited.get((eng, sk), 0) >= v:
                continue
            self.waited[(eng, sk)] = v
            self.prog[eng].append(("wait", sk, v))

    def op(self, eng, fn, reads=(), writes=(), sig=True):
        self._deps(eng, reads, writes)
        tok = (eng, self.cnt[eng] + 1)
        f = sys._getframe(1)
        if f.f_code.co_filename == __file__ and f.f_code.co_name in ("mm", "tr", "act", "tt", "ts", "stt", "cp", "recip", "memset"):
            f = f.f_back
        self.prog[eng].append(("op", fn, sig, "L%d" % f.f_lineno))
        self.nins += 1
        if sig:
            self.cnt[eng] += 1
        self.pending[eng] = not sig
        for b in reads:
            if len(b.r) > 64:
                b.r = b.r[-32:] if False else b.r
            b.r.append(tok)
        for b in writes:
            b.w = tok
            b.r = []
        return tok

    def dma(self, eng, fn, reads=(), writes=(), sbuf=None):
        self._deps(eng, reads, writes)
        b = sbuf if sbuf is not None else (writes[0] if writes else reads[0])
        if b.dsem is None:
            b.dsem = self.dma_free.pop(0)
            self.dbufs.append(b)
        b.dcnt += 1
        tok = (b.dsem, 16 * b.dcnt)
        self.prog[eng].append(("dma", fn, b.dsem))
        self.nins += 1
        for x in reads:
            x.r.append(tok)
        for x in writes:
            x.w = tok
            x.r = []
        return tok

    def barrier(self, engs=("pe", "dve", "act", "sp")):
        for e in self.ENGS:
            assert not self.pending[e]
        targets = [(e2, self.cnt[e2]) for e2 in ("pe", "dve", "act", "pool")]
        targets += [(b.dsem, 16 * b.dcnt) for b in self.dbufs]
        for e in engs:
            for (sk, v) in targets:
                if v == 0 or self.waited.get((e, sk), 0) >= v:
                    continue
                self.waited[(e, sk)] = v
                self.prog[e].append(("wait", sk, v))

    def final_wait(self, eng, bufs):
        self._deps(eng, bufs, bufs)

    def simulate(self):
        sem = {n: 0 for n in self.sem_names}
        pc = {e: 0 for e in self.ENGS}
        progress = True
        while progress:
            progress = False
            for e in self.ENGS:
                prog = self.prog[e]
                while pc[e] < len(prog):
                    it = prog[pc[e]]
                    if it[0] == "wait":
                        if sem[it[1]] < it[2]:
                            break
                    elif it[0] == "op":
                        if it[2]:
                            sem[e] += 1
                    else:
                        sem[it[2]] += 16
                    pc[e] += 1
                    progress = True
        bad = [(e, pc[e], len(self.prog[e]), self.prog[e][pc[e]][:3], sem[self.prog[e][pc[e]][1]] if self.prog[e][pc[e]][0] == "wait" else None)
               for e in self.ENGS if pc[e] < len(self.prog[e])]
        if bad:
            raise RuntimeError("DEADLOCK in emitted program: %s" % (bad,))
        for e in self.ENGS:
            assert sem[e] == self.cnt[e]

    def emit(self, stack):
        self.simulate()
        nc = self.nc
        for nm in self.sem_names:
            self.sems[nm] = stack.enter_context(nc.semaphore("s_" + nm))
        block = stack.enter_context(nc.Block())
        deco = {"pe": block.tensor, "dve": block.vector, "act": block.scalar, "pool": block.gpsimd, "sp": block.sync}
        for e in self.ENGS:
            assert not self.pending[e], "engine %s ends with open group" % e
            prog = self.prog[e]
            sems = self.sems
            esem = sems[e]

            def body(engine, prog=prog, esem=esem, sems=sems):
                for item in prog:
                    if item[0] == "wait":
                        engine.wait_ge(sems[item[1]], item[2])
                    elif item[0] == "op":
                        ins = item[1](engine)
                        if DEBUG_ANNOT:
                            ins.annotate(item[3])
                        if item[2]:
                            ins.then_inc(esem, 1)
                    else:
                        item[1](engine).then_inc(sems[item[2]], 16)

            deco[e](body)

    def mm(self, out, lhsT, rhs, start, stop, reads, writes, sig):
        self.op("pe", lambda e: e.matmul(out, lhsT=lhsT, rhs=rhs, start=start, stop=stop), reads, writes, sig)

    def tr(self, out, in_, ident, reads, writes, sig=True):
        self.op("pe", lambda e: e.transpose(out, in_, ident), reads, writes, sig)

    def act(self, out, in_, func, reads, writes, bias=0.0, scale=1.0):
        self.op("act", lambda e: e.activation(out, in_, func, bias=bias, scale=scale), reads, writes)

    def tt(self, out, in0, in1, op, reads, writes, eng="dve"):
        self.op(eng, lambda e: e.tensor_tensor(out=out, in0=in0, in1=in1, op=op), reads, writes)

    def ts(self, out, in0, s1, s2, op0, op1, reads, writes, eng="dve"):
        if s2 is None:
            self.op(eng, lambda e: e.tensor_single_scalar(out, in0, s1, op0), reads, writes)
        else:
            self.op(eng, lambda e: e.tensor_scalar(out=out, in0=in0, scalar1=s1, scalar2=s2, op0=op0, op1=op1), reads, writes)

    def stt(self, out, in0, scalar, in1, op0, op1, reads, writes, eng="dve"):
        self.op(eng, lambda e: e.scalar_tensor_tensor(out=out, in0=in0, scalar=scalar, in1=in1, op0=op0, op1=op1), reads, writes)

    def cp(self, out, in_, reads, writes, eng="dve"):
        if eng == "act":
            self.op("act", lambda e: e.copy(out, in_), reads, writes)
        else:
            self.op(eng, lambda e: e.tensor_copy(out, in_), reads, writes)

    def recip(self, out, in_, reads, writes):
        self.op("dve", lambda e: e.reciprocal(out, in_), reads, writes)

    def memset(self, ap, val, writes, eng="dve"):
        self.op(eng, lambda e: e.memset(ap, val), (), writes)

    def ld(self, out, in_, writes, eng="sp", sbuf=None):
        self.dma(eng, lambda e: e.dma_start(out=out, in_=in_), (), writes, sbuf=sbuf)

    def st(self, out, in_, reads, eng="sp", sbuf=None):
        self.dma(eng, lambda e: e.dma_start(out=out, in_=in_), reads, (), sbuf=sbuf)


class Rot:
    def __init__(self, items):
        self.items = items
        self.i = 0

    def next(self):
        it = self.items[self.i % len(self.items)]
        self.i += 1
        return it


def host_consts():
    c = {}
    c["identf"] = np.eye(128, dtype=np.float32)
    c["identb"] = np.eye(128, dtype=np.float32).astype(ml_dtypes.bfloat16)
    c["onesb"] = np.ones((128, 128), np.float32).astype(ml_dtypes.bfloat16)
    c["onesf"] = np.ones((128, 128), np.float32)
    s = np.arange(128)[:, None]
    l = np.arange(128)[None, :]
    mf = np.where(l >= s, 0.0, NEG).astype(np.float32)
    mb = np.where(l <= s, 0.0, NEG).astype(np.float32)
    c["maskf"] = np.tile(mf, (1, 4)).astype(ml_dtypes.bfloat16)
    c["maskb"] = np.tile(mb, (1, 4)).astype(ml_dtypes.bfloat16)
    sel = np.zeros((32, 32, 128), np.float32)
    for h in range(32):
        sel[h, h, :] = 1.0
    c["sel"] = sel.reshape(32, 32 * 128)
    sg = np.zeros((32, 4), np.float32)
    sg[:16, 0] = 1.0
    sg[16:, 0] = -1.0
    sg[16:, 1] = 1.0
    sg[:, 2] = -1.0
    c["sgn"] = sg
    rm = np.ones((32, 1024), np.float32)
    rm[:, ::128] = 0.0
    c["rmask"] = rm
    n_freq = 16
    inv_freq = (10000.0 ** (-np.arange(n_freq, dtype=np.float32) / n_freq)).astype(np.float32)
    t = np.arange(1024)
    pos_row = (t // 64).astype(np.float32)
    pos_col = (t % 64).astype(np.float32)
    cos = np.zeros((128, 1024), np.float32)
    sin = np.zeros((128, 1024), np.float32)
    perm = np.zeros((128, 128), np.float32)
    for d in range(128):
        dd = d % 64
        pos = pos_row if dd < 32 else pos_col
        f = inv_freq[dd % 16]
        ang = (pos * f).astype(np.float32)
        cos[d] = np.cos(ang)
        sin[d] = np.sin(ang)
        if (d % 32) < 16:
            perm[d + 16, d] = -1.0
        else:
            perm[d - 16, d] = 1.0
    c["cos"] = cos
    c["sin"] = sin
    c["perm"] = perm.astype(ml_dtypes.bfloat16)
    def invc(seqs, padlen):
        out = np.ones((4, padlen), np.float32)
        for g, w in enumerate((2, 4, 8, 16)):
            for (o, L) in seqs:
                tt = np.arange(L)
                lo = np.clip(tt - w // 2, 0, L)
                hi = np.clip(tt + w - w // 2, 0, L)
                out[g, o:o + L] = 1.0 / (hi - lo).astype(np.float32)
        return np.broadcast_to(out[None], (128, 4, padlen)).reshape(128, 4 * padlen).astype(ml_dtypes.bfloat16)
    c["invc0"] = invc([(8, 256), (280, 256)], 544)
    c["invc1"] = invc([(8, 1024)], 1040)
    return c


CONST_SPECS = [("identf", [128, 128], F32), ("identb", [128, 128], BF16), ("onesb", [128, 128], BF16),
               ("onesf", [128, 128], F32), ("maskf", [128, 512], BF16), ("maskb", [128, 512], BF16),
               ("sel", [32, 4096], F32), ("sgn", [32, 4], F32), ("rmask", [32, 1024], F32),
               ("cos", [128, 1024], F32), ("sin", [128, 1024], F32), ("perm", [128, 128], BF16),
               ("invc0", [128, 4 * 544], BF16), ("invc1", [128, 4 * 1040], BF16)]


def build(n_layers=DEPTH, do_mixer=True, dbg=False):
    nc = bass.Bass("TRN2", target_bir_lowering=False)

    def din(name, shape, dt=F32):
        return nc.dram_tensor(name, shape, dt, kind="ExternalInput").ap()

    def dout(name, shape, dt=F32):
        return nc.dram_tensor(name, shape, dt, kind="ExternalOutput").ap()

    xT_d = din("xT", [D, NT])
    cond_d = din("cond", [128, 16])
    pv_d = din("pv", [128, DEPTH * PL + 8])
    ckT_d = din("ckT", [DEPTH, 8, 128, 512])
    cv_d = din("cv", [DEPTH, 512, 8, 128])
    st0_d = din("st0", [DEPTH, 2, 64, 1024])
    w_ada_d = din("w_ada", [DEPTH, D, 9 * D])
    ffn_in_d = din("ffn_w_in", [DEPTH, 2, D, 2 * DFF])
    ffn_out_d = din("ffn_w_out", [DEPTH, 2, DFF, D])
    w_in_d = din("w_in", [DEPTH, D, IN_W])
    pmap_d = din("pool_map", [DEPTH, 4, 256, 256])
    wbr_d = din("w_branch", [DEPTH, 3, D, D])
    wo_d = din("w_out", [DEPTH, D, D])
    cdram = {n: din("c_" + n, shp, dt) for (n, shp, dt) in CONST_SPECS}
    yT_d = dout("yT", [D, NT])
    nk_d = dout("nk", [2, DEPTH, 256, 8, 128])
    nv_d = dout("nv", [2, DEPTH, 256, 8, 128])
    ns_d = dout("ns", [2, DEPTH, 2, 64, 1024])
    dbg_d = dout("dbg", [8, 128, 1536]) if dbg else None

    st = ExitStack()
    with st:
        def sb(name, shape, dt=F32):
            return st.enter_context(nc.sbuf_tensor("s_" + name, shape, dt))

        kb = KB(nc)
        xT = sb("xT", [128, 8, NT])
        xB = [Buf("x%d" % i) for i in range(3)]
        hB = [Buf("h%d" % i) for i in range(3)]
        NSLOT = 2
        wslots = Rot([(sb("ws%d" % i, [128, 4096], BF16), Buf("ws%d" % i)) for i in range(NSLOT)])
        pv = sb("pv", [128, DEPTH * PL + 8])
        cond = sb("cond", [128, 8, 2])
        sc = sb("sc", [128, 8, 2])
        mod = sb("mod", [128, DEPTH, 72, 2])
        Amod = sb("Amod", [128, DEPTH, 3, 8, 2])
        Gmod = sb("Gmod", [128, DEPTH, 3, 8, 2])
        CB_ = Buf("const")
        MB = Buf("mod")
        identf = sb("identf", [128, 128]); identb = sb("identb", [128, 128], BF16)
        onesb = sb("onesb", [128, 128], BF16); onesf = sb("onesf", [128, 128])
        ps = [st.enter_context(nc.psum_tensor("ps%d" % i, [128, 512], F32)) for i in range(7)]
        pB = [Buf("ps%d" % i) for i in range(7)]
        psb = st.enter_context(nc.psum_tensor("psb", [128, 1024], BF16))
        psbB = Buf("psb")
        PH = 108 * 1024
        phase = sb("phase", [128, PH // 4])

        class Carver:
            def __init__(self):
                self.off = 0
                self.live = []
                self.last = (0, 0)
                self.mark = None

            def reset(self, off=None):
                self.off = HT_END if off is None else off
                self.mark = None

            def mk(self, name):
                lo, hi = (self.mark if self.mark is not None else self.last[0]), self.off
                self.mark = None
                self.last = (lo, hi)
                b = Buf(name)
                toks = []
                for (l2, h2, b2) in self.live:
                    if l2 < hi and lo < h2:
                        if b2.w is not None:
                            toks.append(b2.w)
                        toks.extend(b2.r)
                b.r = list(dict.fromkeys(toks))
                self.live = [(l2, h2, b2) for (l2, h2, b2) in self.live if not (lo <= l2 and h2 <= hi)]
                self.live.append((lo, hi, b))
                return b

            def mks(self, name, n):
                lo, hi = (self.mark if self.mark is not None else self.last[0]), self.off
                self.last = (lo, hi)
                bs = []
                keep = self.live
                for i in range(n):
                    self.live = list(keep)
                    self.mark = lo
                    bs.append(self.mk("%s%d" % (name, i)))
                self.live = [(l2, h2, b2) for (l2, h2, b2) in keep if not (lo <= l2 and h2 <= hi)] + [(lo, hi, b) for b in bs]
                return bs

            def take(self, shape, dt=F32):
                n = int(np.prod(shape[1:]))
                nbytes = n * (4 if dt == F32 else 2)
                nbytes = (nbytes + 63) // 64 * 64
                assert self.off + nbytes <= PH, "phase region overflow %d" % (self.off + nbytes)
                self.peak = max(getattr(self, "peak", 0), self.off + nbytes)
                self.last = (self.off, self.off + nbytes)
                if self.mark is None:
                    self.mark = self.off
                ap = phase[0:shape[0], self.off // 4:(self.off + nbytes) // 4]
                if dt != F32:
                    ap = ap.bitcast(BF16)[:, 0:n]
                else:
                    ap = ap[:, 0:n]
                self.off += nbytes
                if len(shape) > 2:
                    names = " ".join("a%d" % i for i in range(len(shape) - 1))
                    kw = {"a%d" % i: shape[i + 1] for i in range(len(shape) - 1)}
                    ap = ap.rearrange("p (%s) -> p %s" % (names, names), **kw)
                return ap

        cv = Carver()
        hT = cv.take([128, 8, NT], BF16)
        HT_END = cv.off

        for mt in range(3):
            kb.ld(xT[:, :, mt * 512:(mt + 1) * 512], xT_d.rearrange("(c p) t -> p c t", p=128)[:, :, mt * 512:(mt + 1) * 512], [xB[mt]])
        kb.ld(pv[:], pv_d, [CB_])
        kb.ld(cond[:].rearrange("p c j -> p (c j)"), cond_d, [CB_])
        kb.ld(identf[:], cdram["identf"], [CB_]); kb.ld(identb[:], cdram["identb"], [CB_])
        kb.ld(onesb[:], cdram["onesb"], [CB_]); kb.ld(onesf[:], cdram["onesf"], [CB_])

        def pvl(l, off, n=1):
            return pv[:, l * PL + off:l * PL + off + n]

        kb.act(sc[:], cond[:], AF.Silu, [CB_], [MB])
        cv.reset()
        wa = Rot([(cv.take([128, 8, 512], BF16), cv.mk("wa%d" % i)) for i in range(5)])
        scb = sb("scb", [128, 8, 2], BF16)
        kb.cp(scb[:], sc[:], [MB], [MB])
        for l in range(n_layers):
            wv = w_ada_d[l].rearrange("(kc p) n -> p kc n", p=128)
            for blk in range(18):
                wt, wb = wa.next()
                kb.ld(wt, wv[:, :, blk * 512:(blk + 1) * 512], [wb], eng="pool")
                for mi in range(4):
                    m = blk * 4 + mi
                    for kc in range(8):
                        kb.mm(ps[6][:, 2 * m:2 * m + 2], wt[:, kc, mi * 128:(mi + 1) * 128], scb[:, kc, :], kc == 0, kc == 7,
                              [wb, MB], [pB[6]], sig=(kc == 7))
            kb.tt(mod[:, l], ps[6][:, 0:144].rearrange("p (m j) -> p m j", j=2),
                  pvl(l, PV_BADA, 72).unsqueeze(2).to_broadcast([128, 72, 2]), ALU.add, [pB[6], CB_], [MB])
            for i in range(3):
                kb.stt(Amod[:, l, i], mod[:, l, (3 * i + 1) * 8:(3 * i + 2) * 8, :], 1.0,
                       pvl(l, PV_NG + 8 * i, 8).unsqueeze(2).to_broadcast([128, 8, 2]), ALU.add, ALU.mult, [MB, CB_], [MB])
                kb.ts(Gmod[:, l, i], mod[:, l, (3 * i + 2) * 8:(3 * i + 3) * 8, :], 0.5, None, ALU.mult, None, [MB], [MB])

        CND = [0, 1, 1]
        DBGB = Buf("dbg")

        def tap(idx, ap, reads, n):
            if dbg:
                kb.dma("sp", lambda e: e.dma_start(out=dbg_d[idx, 0:ap.shape[0], 0:n], in_=ap), reads, (), sbuf=DBGB)

        tap(0, mod[:, 0].rearrange("p m j -> p (m j)"), [MB], 144)
        tap(1, Amod[:, 0].rearrange("p i c j -> p (i c j)"), [MB], 48)
        tap(2, sc[:].rearrange("p c j -> p (c j)"), [MB], 16)

        def rstd_from_ps(pst, pbuf, n, dfeat, out_ap, outbuf, tmp_ap, tmpbuf):
            kb.ts(tmp_ap, pst, 1.0 / dfeat, EPS, ALU.mult, ALU.add, [pbuf], [tmpbuf])
            kb.act(tmp_ap, tmp_ap, AF.Sqrt, [tmpbuf], [tmpbuf])
            kb.recip(out_ap, tmp_ap, [tmpbuf], [outbuf])

        def norm_to_h(l, i, mts, sq, sqB, rs, rsB, tmp, tmpB):
            for mt in mts:
                cs = slice(mt * 512, (mt + 1) * 512)
                for kc in range(8):
                    kb.act(sq[:, kc, :], xT[:, kc, cs], AF.Square, [xB[mt]], [sqB])
                for kc in range(8):
                    kb.mm(ps[6][:], onesb[:], sq[:, kc, :], kc == 0, kc == 7, [sqB, CB_], [pB[6]], sig=(kc == 7))
                rstd_from_ps(ps[6][:], pB[6], 512, D, rs, rsB, tmp, tmpB)
                if l == 0 and i == 0:
                    tap(3, rs, [rsB], 512) if mt == 0 else None
                    tap(4, rs, [rsB], 512) if mt == 1 else None
                for kc in range(8):
                    kb.stt(tmp, xT[:, kc, cs], Amod[:, l, i, kc, CND[mt]:CND[mt] + 1], rs, ALU.mult, ALU.mult, [xB[mt], MB, rsB], [tmpB])
                    kb.act(hT[:, kc, cs], tmp, AF.Identity, [tmpB, MB], [hB[mt]], bias=mod[:, l, 3 * i * 8 + kc, CND[mt]:CND[mt] + 1])

        def wload(srcs, shape):
            wt, wb = wslots.next()
            n = int(np.prod(shape[1:]))
            names = " ".join("a%d" % i for i in range(len(shape) - 1))
            kw = {"a%d" % i: shape[i + 1] for i in range(len(shape) - 1)}
            view = wt[:, 0:n].rearrange("p (%s) -> p %s" % (names, names), **kw) if len(shape) > 2 else wt[:, 0:n]
            for (fn, src) in srcs:
                kb.dma("pool", (lambda e, o=fn(view), s=src: e.dma_start(out=o, in_=s)), (), [wb])
            return view, wb

        def ffn(l, which):
            cv.reset()
            sq = cv.take([128, 8, 512], BF16); sqB = cv.mk("sq")
            rs = cv.take([128, 512]); rsB = cv.mk("rs")
            tmp = cv.take([128, 512]); tmpB = cv.mk("tmp")
            norm_to_h(l, 0 if which == 0 else 2, [0, 1, 2], sq, sqB, rs, rsB, tmp, tmpB)
            chk("ffn%d_norm" % which)
            cv.reset()
            aT = cv.take([128, 12, NT], BF16)
            aB = cv.mks("a", 12)
            sgs = Rot([(cv.take([128, NT]), cv.mk("sg%d" % i)) for i in range(2)])
            w1 = ffn_in_d[l, which].rearrange("(kc p) n -> p kc n", p=128)
            w2 = ffn_out_d[l, which].rearrange("(kc p) n -> p kc n", p=128)
            gi = 0 if which == 0 else 2
            pset = Rot([(0, 1, 2), (3, 4, 5)])
            for (p0, p1) in ((0, 6), (6, 11)):
                nk = (p1 - p0) * 2
                for pr in range(p0, p1):
                    wv, wb = wload([(lambda v: v[:, :, 0, :], w1[:, :, pr * 256:(pr + 1) * 256]),
                                    (lambda v: v[:, :, 1, :], w1[:, :, DFF + pr * 256:DFF + (pr + 1) * 256])], [128, 8, 2, 256])
                    for ci in range(2):
                        j = (pr - p0) * 2 + ci
                        sg, sgB = sgs.next()
                        bg = pset.next()
                        for kc in range(8):
                            for mt in range(3):
                                kb.mm(ps[bg[mt]][:], wv[:, kc, 0, ci * 128:(ci + 1) * 128], hT[:, kc, mt * 512:(mt + 1) * 512],
                                      kc == 0, kc == 7, [wb, hB[mt]], [pB[bg[mt]]], sig=(kc == 7 and mt == 2))
                        for mt in range(3):
                            kb.act(sg[:, mt * 512:(mt + 1) * 512], ps[bg[mt]][:], AF.Silu, [pB[bg[mt]]], [sgB])
                        bu = pset.next()
                        for kc in range(8):
                            for mt in range(3):
                                kb.mm(ps[bu[mt]][:], wv[:, kc, 1, ci * 128:(ci + 1) * 128], hT[:, kc, mt * 512:(mt + 1) * 512],
                                      kc == 0, kc == 7, [wb, hB[mt]], [pB[bu[mt]]], sig=(kc == 7 and mt == 2))
                        for mt in range(3):
                            kb.tt(aT[:, j, mt * 512:(mt + 1) * 512], ps[bu[mt]][:], sg[:, mt * 512:(mt + 1) * 512], ALU.mult,
                                  [pB[bu[mt]], sgB], [aB[j]])
                chk("ffn%d_in%d" % (which, p0))
                for mp in range(4):
                    wv, wb = wload([(lambda v: v, w2[:, p0 * 2:p0 * 2 + nk, mp * 256:(mp + 1) * 256])], [128, nk, 256])
                    for mi in range(2):
                        m = mp * 2 + mi
                        bo = pset.next()
                        for kc in range(nk):
                            for mt in range(3):
                                kb.mm(ps[bo[mt]][:], wv[:, kc, mi * 128:(mi + 1) * 128], aT[:, kc, mt * 512:(mt + 1) * 512],
                                      kc == 0, kc == nk - 1, [wb, aB[kc]], [pB[bo[mt]]], sig=(kc == nk - 1 and mt == 2))
                        for mt in range(3):
                            cs = slice(mt * 512, (mt + 1) * 512)
                            kb.stt(xT[:, m, cs], ps[bo[mt]][:], Gmod[:, l, gi, m, CND[mt]:CND[mt] + 1], xT[:, m, cs], ALU.mult, ALU.add,
                                   [pB[bo[mt]], MB, xB[mt]], [xB[mt]])


        sel = sb("sel", [32, 4096])
        sgn = sb("sgn", [32, 4])
        rmask = sb("rmask", [32, 1024])
        maskf = sb("maskf", [128, 512], BF16); maskb = sb("maskb", [128, 512], BF16)
        perm = sb("perm", [128, 128], BF16)
        lamt = sb("lamt", [128, DEPTH, 4])
        for (t_, n_) in ((sel, "sel"), (sgn, "sgn"), (rmask, "rmask"), (maskf, "maskf"), (maskb, "maskb"), (perm, "perm")):
            kb.ld(t_[:], cdram[n_], [CB_])
        LAM_INIT = [0.8 - 0.6 * math.exp(-0.3 * l_) for l_ in range(DEPTH)]
        GROUPS = [([0], [(0, 256), (256, 256)]), ([1, 2], [(0, 1024)])]
        pcur = [0]
        OUTB = []

        def pbank():
            b = pcur[0] % 6
            pcur[0] += 1
            return b

        def mixer(l):
            W = w_in_d[l].rearrange("(kc p) n -> p kc n", p=128)
            cv.reset()
            lt = cv.take([128, 128]); ltB = cv.mk("lt")
            la = pv[:, l * PL + PV_LAM:l * PL + PV_LAM + 256].rearrange("p (a d) -> p a d", a=4)
            kb.tt(lt[:, 0:64], la[:, 0, :], la[:, 1, :], ALU.mult, [CB_], [ltB])
            kb.tt(lt[:, 64:128], la[:, 2, :], la[:, 3, :], ALU.mult, [CB_], [ltB])
            kb.op("dve", lambda e: e.reduce_sum(out=lamt[:, l, 0:2], in_=lt[:].rearrange("p (a d) -> p a d", a=2), axis=mybir.AxisListType.X), [ltB], [MB])
            kb.act(lamt[:, l, 0:2], lamt[:, l, 0:2], AF.Exp, [MB], [MB])
            kb.tt(lamt[:, l, 2:3], lamt[:, l, 0:1], lamt[:, l, 1:2], ALU.subtract, [MB], [MB])
            kb.ts(lamt[:, l, 2:3], lamt[:, l, 2:3], LAM_INIT[l], None, ALU.add, None, [MB], [MB])
            kb.ts(lamt[:, l, 3:4], lamt[:, l, 2:3], -1.0, None, ALU.mult, None, [MB], [MB])
            for gi, (mts, seqs) in enumerate(GROUPS):
                mixer_group(l, W, gi, mts, seqs)

        def mixer_group(l, W, gi, mts, seqs):
            g0 = mts[0] * 512
            T = len(mts) * 512
            nmt = len(mts)
            cnd = CND[mts[0]]
            cv.reset()
            sq = cv.take([128, 8, 512], BF16); sqB = cv.mk("sq")
            rs = cv.take([128, 512]); rsB = cv.mk("rs")
            tmp = cv.take([128, 512]); tmpB = cv.mk("tmp")
            norm_to_h(l, 1, mts, sq, sqB, rs, rsB, tmp, tmpB)
            cv.reset()
            hBs = [hB[mt] for mt in mts]

            def proj(srcs, shape, lhs_fn, M, evac):
                wv, wb = wload(srcs, shape)
                for mi, mt in enumerate(mts):
                    b = pbank()
                    for kc in range(8):
                        kb.mm(ps[b][0:M, :], lhs_fn(wv, kc), hT[:, kc, mt * 512:(mt + 1) * 512], kc == 0, kc == 7, [wb, hB[mt]], [pB[b]], sig=(kc == 7))
                    evac(mi, ps[b][0:M, :], pB[b])

            yT = cv.take([128, 8, T], BF16); yB = cv.mks("y", nmt)
            YEND = cv.off
            if 'ssd' in SKIP:
                for mi in range(nmt):
                    kb.memset(yT[:, :, mi * 512:(mi + 1) * 512], 0.0, [yB[mi]])
            dtT = cv.take([32, T]); aT_ = cv.take([32, T]); cumT = cv.take([32, T]); fmB = cv.mk("fm")
            SSD0 = cv.off
            t1 = cv.take([32, T]); t2 = cv.take([32, T]); t12B = cv.mk("t12")
            ac = cv.take([32, 2]); acB = cv.mk("ac")
            kb.act(ac[:, 0:1], pv[0:32, l * PL + PV_ALOG:l * PL + PV_ALOG + 1], AF.Exp, [CB_], [acB])
            kb.ts(ac[:, 1:2], ac[:, 0:1], -1.0, None, ALU.mult, None, [acB], [acB])

            def ev_dt(mi, pst, pb):
                kb.act(t1[:, mi * 512:(mi + 1) * 512], pst, AF.Identity, [pb, CB_], [t12B], bias=pv[0:32, l * PL + PV_DTB:l * PL + PV_DTB + 1])
            proj([(lambda v: v, W[:, :, OFF_DT:OFF_DT + 32])], [128, 8, 32], lambda wv, kc: wv[:, kc, :], 32, ev_dt)
            kb.ts(t2[:], t1[:], -1.0, None, ALU.mult, None, [t12B], [t12B])
            kb.tt(t2[:], t2[:], t1[:], ALU.max, [t12B], [t12B])
            kb.act(t2[:], t2[:], AF.Exp, [t12B], [t12B], scale=-1.0)
            kb.act(t2[:], t2[:], AF.Ln, [t12B], [t12B], bias=1.0)
            kb.ts(t1[:], t1[:], 0.0, None, ALU.max, None, [t12B], [t12B])
            kb.tt(dtT[:], t1[:], t2[:], ALU.add, [t12B], [fmB])
            kb.ts(aT_[:], dtT[:], ac[:, 1:2], None, ALU.mult, None, [fmB, acB], [fmB])
            for c0 in range(0, T, 1024):
                n_ = min(1024, T - c0)
                kb.op("dve", lambda e, c0=c0, n_=n_: e.tensor_tensor_scan(out=cumT[:, c0:c0 + n_], data0=rmask[:, 0:n_], data1=aT_[:, c0:c0 + n_],
                                                                      initial=0.0, op0=ALU.mult, op1=ALU.add), [fmB, CB_], [fmB])

            nseq = len(seqs)
            PW = T + 4 * nseq
            for j in range(0 if 'ssd' in SKIP else 2):
                cv.reset(SSD0)
                xbc = cv.take([128, 6, T], BF16); xbB = cv.mk("xbc")
                PC0 = cv.off
                pcs = Rot([(cv.take([128, PW]), cv.mk("pc%d" % i)) for i in range(1)])
                accs = Rot([(cv.take([128, PW]), cv.mk("acc%d" % i)) for i in range(1)])
                for (pc_, pcb_) in pcs.items:
                    kb.memset(pc_[:], 0.0, [pcb_])
                chunk_ids = [4 * j, 4 * j + 1, 4 * j + 2, 4 * j + 3, 8 + j, 10 + j]
                for ci, c in enumerate(chunk_ids):
                    pc_, pcb_ = pcs.next()
                    acc, accB = accs.next()

                    def ev_pc(mi, pst, pb, pc_=pc_, pcb_=pcb_):
                        for k, (o, L) in enumerate(seqs):
                            lo = max(o, mi * 512); hi = min(o + L, (mi + 1) * 512)
                            if lo < hi:
                                kb.cp(pc_[:, 2 + lo + 4 * k:2 + hi + 4 * k], pst[:, lo - mi * 512:hi - mi * 512], [pb], [pcb_], eng="act")
                    proj([(lambda v: v, W[:, :, OFF_XBC + c * 128:OFF_XBC + (c + 1) * 128])], [128, 8, 128], lambda wv, kc: wv[:, kc, :], 128, ev_pc)
                    n_ = PW - 4
                    cw = pv[:, l * PL + PV_CW + c * 5:l * PL + PV_CW + c * 5 + 5]
                    kb.ts(acc[:, 0:n_], pc_[:, 0:n_], cw[:, 0:1], None, ALU.mult, None, [pcb_, CB_], [accB])
                    for tp in range(1, 5):
                        kb.stt(acc[:, 0:n_], pc_[:, tp:tp + n_], cw[:, tp:tp + 1], acc[:, 0:n_], ALU.mult, ALU.add, [pcb_, CB_, accB], [accB])
                    for k, (o, L) in enumerate(seqs):
                        kb.act(xbc[:, ci, o:o + L], acc[:, o + 4 * k:o + 4 * k + L], AF.Silu, [accB, CB_], [xbB],
                               bias=pv[:, l * PL + PV_CB + c:l * PL + PV_CB + c + 1])
                cv.reset(PC0)
                Hf = cv.take([128, 512]); Hb = cv.take([128, 512]); Hfb = cv.take([128, 512], BF16); HB_ = cv.mk("H")
                ntmax = max(L for (_, L) in seqs) // 128
                Hbin = cv.take([128, ntmax, 512], BF16); HbinB = cv.mk("Hbin")
                sets = []
                for si in range(2):
                    S = {}
                    S["tok"] = cv.take([128, 96]); S["arg"] = cv.take([128, 32]); S["dte"] = cv.take([128, 32]); S["cd"] = cv.take([128, 32]); S["s2"] = cv.take([128, 32]); S["tokB"] = cv.mk("tok%d" % si)
                    S["btok"] = cv.take([128, 128], BF16); S["btB"] = cv.mk("btok%d" % si)
                    S["xdf"] = cv.take([128, 512], BF16); S["xdb"] = cv.take([128, 512], BF16); S["xdd"] = cv.take([128, 512], BF16); S["xdB"] = cv.mk("xd%d" % si)
                    S["Rt"] = cv.take([32, 128]); S["Cn"] = cv.take([32, 128]); S["EL"] = cv.take([32, 128]); S["tb_"] = cv.take([32, 128]); S["rB"] = cv.mk("R%d" % si)
                    S["ty"] = cv.take([128, 4, 128]); S["tyB"] = cv.mk("ty%d" % si)
                    sets.append(S)
                par = [0]
                tok = arg = dte = cd = s2 = tokB = btok = btB = xdf = xdb = xdd = xdB = Rt = Cn = EL = tb_ = rB = ty = tyB = None

                def nextset():
                    nonlocal tok, arg, dte, cd, s2, tokB, btok, btB, xdf, xdb, xdd, xdB, Rt, Cn, EL, tb_, rB, ty, tyB
                    S = sets[par[0] % 2]
                    par[0] += 1
                    tok, arg, dte, cd, s2, tokB = S["tok"], S["arg"], S["dte"], S["cd"], S["s2"], S["tokB"]
                    btok, btB = S["btok"], S["btB"]
                    xdf, xdb, xdd, xdB = S["xdf"], S["xdb"], S["xdd"], S["xdB"]
                    Rt, Cn, EL, tb_, rB = S["Rt"], S["Cn"], S["EL"], S["tb_"], S["rB"]
                    ty, tyB = S["ty"], S["tyB"]

                cbsR = Rot([(cv.take([128, 128]), cv.mk("cb%d" % i)) for i in range(2)])
                decs = Rot([(cv.take([128, 512]), cv.mk("dec%d" % i)) for i in range(2)])
                scs = [(cv.take([128, 4, 128], BF16), cv.mk("sc%d" % i)) for i in range(2)]
                ebcR = Rot([(cv.take([128, 512]), cv.mk("ebc%d" % i)) for i in range(2)])
                ces = [(cv.take([128, 4, 128], BF16), cv.mk("ce%d" % i)) for i in range(2)]
                segb = Rot([3, 6])
                PS_T, PS_S, PS_CB, PS_SEG, PS_E, PS_Y = 0, 1, 2, 3, 4, 5

                for k, (o, L) in enumerate(seqs):
                    nt = L // 128
                    kb.memset(Hf[:], 0.0, [HB_]); kb.memset(Hb[:], 0.0, [HB_])
                    if gi == 1:
                        for d_, H_ in ((0, Hf), (1, Hb)):
                            for gg in range(2):
                                kb.ld(H_[gg * 64:(gg + 1) * 64, gg * 256:(gg + 1) * 256],
                                      st0_d[l, d_][:, (8 * j + 4 * gg) * 64:(8 * j + 4 * gg + 4) * 64], [HB_])
                    kb.cp(Hfb[:], Hf[:], [HB_], [HB_])

                    def prep(i):
                        tsl = slice(o + i * 128, o + (i + 1) * 128)
                        kb.tr(ps[PS_T][:, 0:32], dtT[:, tsl], identf[0:32, 0:32], [fmB, CB_], [pB[PS_T]], sig=False)
                        kb.tr(ps[PS_T][:, 32:64], aT_[:, tsl], identf[0:32, 0:32], [fmB, CB_], [pB[PS_T]], sig=False)
                        kb.tr(ps[PS_T][:, 64:96], cumT[:, tsl], identf[0:32, 0:32], [fmB, CB_], [pB[PS_T]], sig=True)
                        kb.cp(tok[:], ps[PS_T][:, 0:96], [pB[PS_T]], [tokB], eng="act")
                        kb.mm(ps[PS_T][:, 96:128], onesf[:], tok[:, 32:64], True, True, [tokB, CB_], [pB[PS_T]], sig=True)
                        kb.tt(arg[:, 0:16], ps[PS_T][:, 96:112], tok[:, 64:80], ALU.subtract, [pB[PS_T], tokB], [tokB])
                        kb.tt(arg[:, 16:32], tok[:, 80:96], tok[:, 48:64], ALU.subtract, [tokB], [tokB])
                        kb.act(dte[:], arg[:], AF.Exp, [tokB], [tokB])
                        kb.act(cd[:], ps[PS_T][:, 96:128], AF.Exp, [pB[PS_T]], [tokB])
                        kb.tt(s2[:], tok[:, 0:32], dte[:], ALU.mult, [tokB], [tokB])
                        for ci in range(4):
                            kb.tr(psb[:, ci * 128:(ci + 1) * 128], xbc[:, ci, tsl], identb[:], [xbB, CB_], [psbB], sig=False)
                        kb.tr(psb[:, 512:640], xbc[:, 4, tsl], identb[:], [xbB, CB_], [psbB], sig=True)
                        kb.cp(btok[:], psb[:, 512:640], [psbB], [btB])
                        return tsl

                    def xprod(out, col0):
                        kb.tt(out[:].rearrange("p (h d) -> p h d", h=8), psb[:, 0:512].rearrange("p (h d) -> p h d", h=8),
                              col0.unsqueeze(2).to_broadcast([128, 8, 64]), ALU.mult, [psbB, tokB], [xdB])

                    def state_update(H_, xsrc, cdcol):
                        kb.mm(ps[PS_S][:], btok[:], xsrc[:], True, True, [btB, xdB], [pB[PS_S]], sig=True)
                        kb.tt(H_[:].rearrange("p (h d) -> p h d", h=8), H_[:].rearrange("p (h d) -> p h d", h=8),
                              cdcol.unsqueeze(2).to_broadcast([128, 8, 64]), ALU.mult, [HB_, tokB], [HB_])
                        kb.tt(H_[:], H_[:], ps[PS_S][:], ALU.add, [HB_, pB[PS_S]], [HB_])

                    for i in range(nt - 1, -1, -1):
                        nextset()
                        prep(i)
                        kb.cp(Hbin[:, i, :], Hb[:], [HB_], [HbinB])
                        xprod(xdd, s2[:, 16 + 8 * j:16 + 8 * j + 8])
                        state_update(Hb, xdd, cd[:, 16 + 8 * j:16 + 8 * j + 8])
                    for i in range(nt):
                        nextset()
                        tsl = prep(i)
                        xprod(xdf, tok[:, 8 * j:8 * j + 8])
                        xprod(xdb, tok[:, 16 + 8 * j:16 + 8 * j + 8])
                        xprod(xdd, s2[:, 8 * j:8 * j + 8])
                        kb.ts(tb_[:], aT_[:, tsl], sgn[:, 1:2], None, ALU.mult, None, [fmB, CB_], [rB])
                        kb.stt(Rt[:], cumT[:, tsl], sgn[:, 0:1], tb_[:], ALU.mult, ALU.add, [fmB, CB_, rB], [rB])
                        kb.ts(Cn[:], Rt[:], -1.0, None, ALU.mult, None, [rB], [rB])
                        last = o + i * 128 + 127
                        kb.stt(EL[:], cumT[:, last:last + 1].to_broadcast([32, 128]), sgn[:, 1:2], Rt[:], ALU.mult, ALU.add, [fmB, CB_, rB], [rB])
                        for gg in range(2):
                            r0 = gg * 64
                            cbs, cbB = cbsR.next()
                            kb.mm(ps[PS_CB][:, 0:128], xbc[r0:r0 + 64, 4, tsl], xbc[r0:r0 + 64, 5, tsl], True, True, [xbB], [pB[PS_CB]], sig=True)
                            kb.cp(cbs[:], ps[PS_CB][:, 0:128], [pB[PS_CB]], [cbB], eng="act")
                            for d_ in range(2):
                                msk = maskf if d_ == 0 else maskb
                                sc_, scB = scs[d_]
                                ce_, ceB = ces[d_]
                                dec, decB = decs.next()
                                PS_SEG = segb.next()
                                ebc, ebB = ebcR.next()
                                kb.mm(ps[PS_SEG][:], identb[:], msk[:], True, False, [CB_], [pB[PS_SEG]], sig=False)
                                for hh in range(4):
                                    dh = d_ * 16 + 8 * j + 4 * gg + hh
                                    kb.mm(ps[PS_SEG][:, hh * 128:(hh + 1) * 128], sel[:, dh * 128:(dh + 1) * 128], Rt[:], False, False, [CB_, rB], [pB[PS_SEG]], sig=False)
                                    kb.mm(ps[PS_SEG][:, hh * 128:(hh + 1) * 128], Cn[:], sel[:, dh * 128:(dh + 1) * 128], False, hh == 3, [CB_, rB], [pB[PS_SEG]], sig=(hh == 3))
                                kb.act(dec[:], ps[PS_SEG][:], AF.Exp, [pB[PS_SEG]], [decB])
                                kb.tt(sc_[:], dec[:].rearrange("p (h s) -> p h s", h=4), cbs[:].unsqueeze(1).to_broadcast([128, 4, 128]), ALU.mult, [decB, cbB], [scB])
                                for hh in range(4):
                                    dh = d_ * 16 + 8 * j + 4 * gg + hh
                                    kb.mm(ps[PS_E][:, hh * 128:(hh + 1) * 128], sel[:, dh * 128:(dh + 1) * 128], EL[:], True, True, [CB_, rB], [pB[PS_E]], sig=(hh == 3))
                                kb.act(ebc[:], ps[PS_E][:], AF.Exp, [pB[PS_E]], [ebB])
                                kb.tt(ce_[r0:r0 + 64], ebc[r0:r0 + 64, :].rearrange("p (h s) -> p h s", h=4),
                                      xbc[r0:r0 + 64, 5, tsl].unsqueeze(1).to_broadcast([64, 4, 128]), ALU.mult, [ebB, xbB], [ceB])
                            for hh in range(4):
                                hl = gg * 4 + hh
                                cl = hl // 2
                                yr = (hl % 2) * 64
                                out = ps[PS_Y][yr:yr + 64, cl * 128:(cl + 1) * 128]
                                hs = slice(hl * 64, (hl + 1) * 64)
                                kb.mm(out, xdf[:, hs], scs[0][0][:, hh, :], True, False, [xdB, scs[0][1]], [pB[PS_Y]], sig=False)
                                kb.mm(out, xdb[:, hs], scs[1][0][:, hh, :], False, False, [xdB, scs[1][1]], [pB[PS_Y]], sig=False)
                                kb.mm(out, Hfb[r0:r0 + 64, hs], ces[0][0][r0:r0 + 64, hh, :], False, False, [HB_, ces[0][1]], [pB[PS_Y]], sig=False)
                                kb.mm(out, Hbin[r0:r0 + 64, i, hs], ces[1][0][r0:r0 + 64, hh, :], False, True, [HbinB, ces[1][1]], [pB[PS_Y]], sig=True)
                        dv = pv[:, l * PL + PV_DV + 4 * j:l * PL + PV_DV + 4 * j + 4]
                        kb.tt(ty[:], xbc[:, 0:4, tsl], dv.unsqueeze(2).to_broadcast([128, 4, 128]), ALU.mult, [xbB, CB_], [tyB])
                        kb.tt(yT[:, 4 * j:4 * j + 4, tsl], ty[:], ps[PS_Y][:].rearrange("p (c s) -> p c s", c=4), ALU.add, [tyB, pB[PS_Y]], yB)
                        state_update(Hf, xdd, cd[:, 8 * j:8 * j + 8])
                        kb.cp(Hfb[:], Hf[:], [HB_], [HB_])
                    if gi == 0:
                        for d_, H_ in ((0, Hf), (1, Hb)):
                            for gg in range(2):
                                kb.st(ns_d[k, l, d_][:, (8 * j + 4 * gg) * 64:(8 * j + 4 * gg + 4) * 64],
                                      H_[gg * 64:(gg + 1) * 64, gg * 256:(gg + 1) * 256], [HB_])
                        kb.final_wait("dve", [HB_])
                        OUTB.append(HB_)

            cv.reset(YEND)
            merged = cv.take([128, 8, T], BF16); mgB = cv.mks("mg", nmt)
            mergedb = merged; mgbB = mgB
            tgs = Rot([(cv.take([128, 512]), cv.mk("tg%d" % i)) for i in range(2)])
            tms = Rot([(cv.take([128, 512]), cv.mk("tm%d" % i)) for i in range(2)])
            MRG2 = cv.off
            szs = Rot([(cv.take([128, T]), cv.mk("sz%d" % i)) for i in range(2)])
            for c in range(8):
                sz, szB = szs.next()

                def ev_z(mi, pst, pb, sz=sz, szB=szB):
                    kb.act(sz[:, mi * 512:(mi + 1) * 512], pst, AF.Silu, [pb], [szB])
                proj([(lambda v: v, W[:, :, OFF_Z + c * 128:OFF_Z + (c + 1) * 128])], [128, 8, 128], lambda wv, kc: wv[:, kc, :], 128, ev_z)
                kb.tt(yT[:, c, :], yT[:, c, :], sz[:], ALU.mult, yB + [szB], yB)
            sq = cv.take([128, 8, 512], BF16); sqB = cv.mk("sq")
            rs = cv.take([128, 512]); rsB = cv.mk("rs")
            tmp = cv.take([128, 512]); tmpB = cv.mk("tmp")
            for mi in range(nmt):
                cs = slice(mi * 512, (mi + 1) * 512)
                for kc in range(8):
                    kb.act(sq[:, kc, :], yT[:, kc, cs], AF.Square, yB, [sqB])
                for kc in range(8):
                    kb.mm(ps[6][:], onesb[:], sq[:, kc, :], kc == 0, kc == 7, [sqB, CB_], [pB[6]], sig=(kc == 7))
                rstd_from_ps(ps[6][:], pB[6], 512, D, rs, rsB, tmp, tmpB)
                for kc in range(8):
                    kb.stt(yT[:, kc, cs], yT[:, kc, cs], pv[:, l * PL + PV_SNG + kc:l * PL + PV_SNG + kc + 1], rs, ALU.mult, ALU.mult, yB + [CB_, rsB], yB)


            def merge(n):
                for m in range(8):
                    wv, wb = wload([(lambda v: v[:, :, 0, :], wbr_d[l, n].rearrange("(kc p) n -> p kc n", p=128)[:, :, m * 128:(m + 1) * 128]),
                                    (lambda v: v[:, :, 1, :], W[:, :, OFF_G + n * 1024 + m * 128:OFF_G + n * 1024 + (m + 1) * 128])], [128, 8, 2, 128])
                    for mi, mt in enumerate(mts):
                        cs = slice(mi * 512, (mi + 1) * 512)
                        bp = pbank(); bq = pbank()
                        for kc in range(8):
                            kb.mm(ps[bp][:], wv[:, kc, 0, :], yT[:, kc, cs], kc == 0, kc == 7, [wb] + yB, [pB[bp]], sig=(kc == 7))
                        for kc in range(8):
                            kb.mm(ps[bq][:], wv[:, kc, 1, :], hT[:, kc, mt * 512:(mt + 1) * 512], kc == 0, kc == 7, [wb, hB[mt]], [pB[bq]], sig=(kc == 7))
                        tg, tgB = tgs.next()
                        kb.act(tg[:], ps[bq][:], AF.Tanh, [pB[bq]], [tgB], scale=0.5)
                        if n == 0:
                            kb.stt(merged[:, m, cs], tg[:], 1.0, ps[bp][:], ALU.add, ALU.mult, [tgB, pB[bp]], [mgB[mi]])
                        else:
                            tm, tmB = tms.next()
                            kb.stt(tm[:], tg[:], 1.0, ps[bp][:], ALU.add, ALU.mult, [tgB, pB[bp]], [tmB])
                            if n == 1:
                                kb.tt(merged[:, m, cs], merged[:, m, cs], tm[:], ALU.add, [mgB[mi], tmB], [mgB[mi]])
                            else:
                                kb.tt(mergedb[:, m, cs], merged[:, m, cs], tm[:], ALU.add, [mgB[mi], tmB], [mgbB[mi]])

            if 'ssd' in SKIP:
                for mi in range(nmt):
                    kb.memset(yT[:, :, mi * 512:(mi + 1) * 512], 0.0, [yB[mi]])
            merge(0)

            cv.reset(MRG2)
            Tk = T + (512 if gi == 1 else 0)
            ntk = Tk // 128
            NQB = T // 256
            qz = cv.take([128, NQB, 2, 256], BF16); qB = cv.mk("qz")
            kb.memset(qz[0:64, :, 1, :], 0.0, [qB])
            kb.memset(qz[64:128, :, 0, :], 0.0, [qB])
            kh = cv.take([128, Tk], BF16); kB_ = cv.mk("kh")
            vh = cv.take([128, ntk, 128], BF16); vB = cv.mk("vh")
            stg = cv.take([128, 4, 128]); stgB = cv.mk("stg")
            OUTB.append(stgB)
            qraw = cv.take([128, 512], BF16); qrB = cv.mk("qraw")
            r1 = cv.take([128, 512]); r2 = cv.take([128, 512]); rrB = cv.mk("rr")
            Pt = Rot([(cv.take([128, 2, 256], BF16), cv.mk("P%d" % i)) for i in range(3)])
            rden = cv.take([128, 2, 256]); on_ = cv.take([128, 2, 256]); od = cv.take([128, 256]); odsq = cv.take([128, 256], BF16); nB = cv.mk("nrm")
            rs2 = cv.take([128, 256]); tmp2 = cv.take([128, 256]); rs2B = cv.mk("rs2")
            gn = cv.take([128, 2]); gnB = cv.mk("gn")
            kb.ts(gn[:, 0:1], pv[:, l * PL + PV_DNG:l * PL + PV_DNG + 1], 1.0 - LAM_INIT[l], None, ALU.mult, None, [CB_], [gnB])
            if gi == 1:
                cos = cv.take([128, 1024]); sin = cv.take([128, 1024]); csB = cv.mk("cs")
                kb.ld(cos[:], cdram["cos"], [csB]); kb.ld(sin[:], cdram["sin"], [csB])
            PSC = [0, 1]
            qbc = [0]

            chk("att%d_pre" % gi)
            for h in range(0 if ('att' in SKIP or ('att%d' % gi) in SKIP) else 8):
                wv, wb = wload([(lambda v: v[:, :, 0, :], W[:, :, OFF_Q + h * 128:OFF_Q + (h + 1) * 128]),
                                (lambda v: v[:, :, 1, :], W[:, :, OFF_K + h * 128:OFF_K + (h + 1) * 128]),
                                (lambda v: v[:, :, 2, :], W[:, :, OFF_V + h * 128:OFF_V + (h + 1) * 128])], [128, 8, 3, 128])
                koff = Tk - T
                if gi == 1:
                    kb.dma("pool", lambda e, h=h: e.dma_start(out=kh[:, 0:512], in_=ckT_d[l, h]), (), [kB_])
                    kb.dma("pool", lambda e, h=h: e.dma_start(out=vh[:, 0:4, :], in_=cv_d[l][:, h, :].rearrange("(t p) e -> p t e", p=128)), (), [vB])
                for which, dst, dB, off in ((0, None, qB, 0), (1, kh, kB_, koff)):
                    for mi, mt in enumerate(mts):
                        b = 4 + (pcur[0] % 2); pcur[0] += 1
                        for kc in range(8):
                            kb.mm(ps[b][:], wv[:, kc, which, :], hT[:, kc, mt * 512:(mt + 1) * 512], kc == 0, kc == 7, [wb, hB[mt]], [pB[b]], sig=(kc == 7))
                        dcs = slice(off + mi * 512, off + (mi + 1) * 512)
                        if gi == 0:
                            if which == 0:
                                for c_ in range(2):
                                    kb.cp(qz[c_ * 64:(c_ + 1) * 64, 2 * mi:2 * mi + 2, c_, :], ps[b][c_ * 64:(c_ + 1) * 64, :].rearrange("p (a q) -> p a q", a=2),
                                          [pB[b]], [dB], eng="act")
                            else:
                                kb.cp(dst[:, dcs], ps[b][:], [pB[b]], [dB], eng="act")
                        else:
                            tcs = slice(mi * 512, (mi + 1) * 512)
                            kb.cp(qraw[:], ps[b][:], [pB[b]], [qrB], eng="act")
                            kb.mm(ps[6][:], perm[:], qraw[:], True, True, [CB_, qrB], [pB[6]], sig=True)
                            kb.tt(r1[:], ps[b][:], cos[:, tcs], ALU.mult, [pB[b], csB, qrB], [rrB])
                            kb.tt(r2[:], ps[6][:], sin[:, tcs], ALU.mult, [pB[6], csB], [rrB])
                            if which == 0:
                                for c_ in range(2):
                                    rs_ = slice(c_ * 64, (c_ + 1) * 64)
                                    kb.tt(qz[rs_, 2 * mi:2 * mi + 2, c_, :], r1[rs_, :].rearrange("p (a q) -> p a q", a=2),
                                          r2[rs_, :].rearrange("p (a q) -> p a q", a=2), ALU.add, [rrB], [dB])
                            else:
                                kb.tt(dst[:, dcs], r1[:], r2[:], ALU.add, [rrB], [dB])
                chk("att%d_h%d_qk" % (gi, h))
                for t0 in range(0, 0 if 'nov' in SKIP else T // 128, 4):
                    b = 4 + (pcur[0] % 2); pcur[0] += 1
                    for tt_ in range(4):
                        tl = t0 + tt_
                        for kc in range(8):
                            kb.mm(ps[b][:, tt_ * 128:(tt_ + 1) * 128], hT[:, kc, g0 + tl * 128:g0 + (tl + 1) * 128], wv[:, kc, 2, :], kc == 0, kc == 7,
                                  [wb] + hBs, [pB[b]], sig=(kc == 7 and tt_ == 3))
                    if gi == 0 and 'nost' not in SKIP:
                        kb.cp(stg[:], ps[b][:].rearrange("p (t e) -> p t e", t=4), [pB[b]], [stgB])
                        kb.cp(vh[:, koff // 128 + t0:koff // 128 + t0 + 4, :], stg[:], [stgB], [vB], eng="act")
                    else:
                        kb.cp(vh[:, koff // 128 + t0:koff // 128 + t0 + 4, :], ps[b][:].rearrange("p (t e) -> p t e", t=4), [pB[b]], [vB], eng="act")
                    if gi == 0 and 'nost' not in SKIP:
                        for k in range(0 if 'nodma' in SKIP else 2):
                            kb.st(nv_d[k, l][:, h, :].rearrange("(t p) e -> p t e", p=128), stg[:, 2 * k:2 * k + 2, :], [stgB])
                        if 'nokst' in SKIP:
                            continue
                        b2 = 4 + (pcur[0] % 2); pcur[0] += 1
                        for tt_ in range(4):
                            tl = t0 + tt_
                            for kc in range(8):
                                kb.mm(ps[b2][:, tt_ * 128:(tt_ + 1) * 128], hT[:, kc, g0 + tl * 128:g0 + (tl + 1) * 128], wv[:, kc, 1, :], kc == 0, kc == 7,
                                      [wb] + hBs, [pB[b2]], sig=(kc == 7 and tt_ == 3))
                        kb.cp(stg[:], ps[b2][:].rearrange("p (t e) -> p t e", t=4), [pB[b2]], [stgB])
                        for k in range(0 if 'nodma' in SKIP else 2):
                            kb.st(nk_d[k, l][:, h, :].rearrange("(t p) e -> p t e", p=128), stg[:, 2 * k:2 * k + 2, :], [stgB])
                for k, (o, L) in enumerate(seqs if 'nocore' not in SKIP else []):
                    if gi == 0:
                        ktiles = list(range(o // 128, (o + L) // 128))
                    else:
                        ktiles = list(range(ntk))
                    for qb in range(L // 256):
                        qs = slice(o + qb * 256, o + (qb + 1) * 256)
                        gq = (o + qb * 256) // 256
                        PO, PD = ((2, 3), (4, 5))[qbc[0] % 2]
                        qbc[0] += 1
                        nk_ = len(ktiles)
                        Ps = {}

                        def score(ki):
                            kt = ktiles[ki]
                            bs = ki % 2
                            kb.mm(ps[bs][:], kh[:, kt * 128:(kt + 1) * 128], qz[:, gq, :, :].rearrange("p c q -> p (c q)"), True, True, [kB_, qB], [pB[bs]], sig=True)
                            P_, PB_ = Pt.next()
                            Ps[ki] = (P_, PB_)
                            kb.act(P_[:].rearrange("p c q -> p (c q)"), ps[bs][:], AF.Exp, [pB[bs]], [PB_], scale=0.125)

                        score(0)
                        for ki, kt in enumerate(ktiles):
                            if ki + 1 < nk_:
                                score(ki + 1)
                            P_, PB_ = Ps.pop(ki)
                            kb.mm(ps[PO][:], vh[:, kt, :], P_[:].rearrange("p c q -> p (c q)"), ki == 0, ki == nk_ - 1, [vB, PB_], [pB[PO]], sig=False)
                            kb.mm(ps[PD][:], onesb[:], P_[:].rearrange("p c q -> p (c q)"), ki == 0, ki == nk_ - 1, [CB_, PB_], [pB[PD]], sig=True)
                        if 'nonorm' in SKIP:
                            kb.cp(rden[:].rearrange("p c q -> p (c q)"), ps[PD][:], [pB[PD]], [nB])
                            kb.cp(on_[:].rearrange("p c q -> p (c q)"), ps[PO][:], [pB[PO]], [nB])
                            continue
                        kb.recip(rden[:].rearrange("p c q -> p (c q)"), ps[PD][:], [pB[PD]], [nB])
                        kb.tt(on_[:].rearrange("p c q -> p (c q)"), ps[PO][:], rden[:].rearrange("p c q -> p (c q)"), ALU.mult, [pB[PO], nB], [nB])
                        kb.stt(od[:], on_[:, 1, :], lamt[:, l, 3:4], on_[:, 0, :], ALU.mult, ALU.add, [nB, MB], [nB])
                        kb.act(odsq[:], od[:], AF.Square, [nB], [nB])
                        kb.mm(ps[6][:, 0:256], onesb[:], odsq[:], True, True, [CB_, nB], [pB[6]], sig=True)
                        rstd_from_ps(ps[6][:, 0:256], pB[6], 256, 128, rs2, rs2B, tmp2, rs2B)
                        kb.stt(yT[:, h, qs], od[:], gn[:, 0:1], rs2[:], ALU.mult, ALU.mult, [nB, gnB, rs2B], yB)
            if 'att' in SKIP:
                for mi in range(nmt):
                    kb.memset(yT[:, :, mi * 512:(mi + 1) * 512], 0.0, [yB[mi]])
            chk("g%d_att" % gi)
            merge(1)
            chk("g%d_m1" % gi)

            cv.reset(MRG2)
            PP = T + 16 * nseq
            pu = Rot([(cv.take([128, PP]), cv.mk("pu%d" % i)) for i in range(2)])
            for (p_, pb_) in pu.items:
                kb.memset(p_[:], 0.0, [pb_])
            lv = [(cv.take([128, PP]), cv.mk("lv%d" % i)) for i in range(2)]
            pooled = cv.take([128, 2, T], BF16); plB = cv.mk("pooled")
            invc = cv.take([128, 4, PP], BF16); ivB = cv.mk("invc")
            kb.ld(invc[:].rearrange("p g n -> p (g n)"), cdram["invc%d" % gi], [ivB])
            for g in range(0 if 'pool' in SKIP else 4):
                for ci in range(2):
                    c = 2 * g + ci
                    pu_, puB = pu.next()

                    def ev_u(mi, pst, pb, pu_=pu_, puB=puB):
                        for k, (o, L) in enumerate(seqs):
                            lo = max(o, mi * 512); hi = min(o + L, (mi + 1) * 512)
                            if lo < hi:
                                kb.cp(pu_[:, 8 + lo + 16 * k:8 + hi + 16 * k], pst[:, lo - mi * 512:hi - mi * 512], [pb], [puB], eng="act")
                    proj([(lambda v: v, W[:, :, OFF_U + c * 128:OFF_U + (c + 1) * 128])], [128, 8, 128], lambda wv, kc: wv[:, kc, :], 128, ev_u)
                    src, srcB = pu_, puB
                    A_, AB_ = lv[0]
                    kb.tt(A_[:, 1:PP], src[:, 0:PP - 1], src[:, 1:PP], ALU.add, [srcB], [AB_])
                    kb.memset(A_[:, 0:1], 0.0, [AB_])
                    cur, curB = A_, AB_
                    for step, sh in enumerate((1, 2, 4)[:g]):
                        nx, nxB = lv[(step + 1) % 2]
                        kb.memset(nx[:, 0:sh], 0.0, [nxB]); kb.memset(nx[:, PP - sh:PP], 0.0, [nxB])
                        kb.tt(nx[:, sh:PP - sh], cur[:, 0:PP - 2 * sh], cur[:, 2 * sh:PP], ALU.add, [curB], [nxB])
                        cur, curB = nx, nxB
                    oth, othB = lv[0] if cur is lv[1][0] else lv[1]
                    kb.tt(oth[:], cur[:], invc[:, g, :], ALU.mult, [curB, ivB], [othB])
                    for k, (o, L) in enumerate(seqs):
                        kb.tt(pooled[:, ci, o:o + L], oth[:, 8 + o + 16 * k:8 + o + 16 * k + L], pu_[:, 8 + o + 16 * k:8 + o + 16 * k + L], ALU.subtract, [othB, puB], [plB])
                wv, wb = wload([(lambda v: v, pmap_d[l, g].rearrange("(kc p) n -> p kc n", p=128))], [128, 2, 256])
                for e_ in range(2):
                    for mi in range(nmt):
                        b = pbank()
                        for kc in range(2):
                            kb.mm(ps[b][:], wv[:, kc, e_ * 128:(e_ + 1) * 128], pooled[:, kc, mi * 512:(mi + 1) * 512], kc == 0, kc == 1, [wb, plB], [pB[b]], sig=(kc == 1))
                        kb.ts(yT[:, 2 * g + e_, mi * 512:(mi + 1) * 512], ps[b][:], pv[:, l * PL + PV_PS + 2 * g + e_:l * PL + PV_PS + 2 * g + e_ + 1], None, ALU.mult, None,
                              [pB[b], CB_], yB)
            chk("g%d_pool" % gi)
            merge(2)
            chk("g%d_m2" % gi)

            wo = wo_d[l].rearrange("(kc p) n -> p kc n", p=128)
            for m in range(8):
                wv, wb = wload([(lambda v: v, wo[:, :, m * 128:(m + 1) * 128])], [128, 8, 128])
                for mi, mt in enumerate(mts):
                    b = pbank()
                    for kc in range(8):
                        kb.mm(ps[b][:], wv[:, kc, :], mergedb[:, kc, mi * 512:(mi + 1) * 512], kc == 0, kc == 7, [wb, mgbB[mi]], [pB[b]], sig=(kc == 7))
                    cs = slice(mt * 512, (mt + 1) * 512)
                    kb.stt(xT[:, m, cs], ps[b][:], Gmod[:, l, 1, m, cnd:cnd + 1], xT[:, m, cs], ALU.mult, ALU.add, [pB[b], MB, xB[mt]], [xB[mt]])
            chk("g%d_out" % gi)

        try:
            for l in range(n_layers):
                ffn(l, 0)
                if do_mixer:
                    mixer(l)
                ffn(l, 1)
        except StopBuild as ex:
            print("STOPPED at", ex)

        cv.reset()
        sq = cv.take([128, 8, 512], BF16); sqB = cv.mk("sq")
        rs = cv.take([128, 512]); rsB = cv.mk("rs")
        tmp = cv.take([128, 512]); tmpB = cv.mk("tmp")
        yo = cv.take([128, 8, NT]); yoB = cv.mks("yo", 3)
        fg = pv[:, DEPTH * PL:DEPTH * PL + 8]
        for mt in range(3):
            cs = slice(mt * 512, (mt + 1) * 512)
            for kc in range(8):
                kb.act(sq[:, kc, :], xT[:, kc, cs], AF.Square, [xB[mt]], [sqB])
            for kc in range(8):
                kb.mm(ps[6][:], onesb[:], sq[:, kc, :], kc == 0, kc == 7, [sqB, CB_], [pB[6]], sig=(kc == 7))
            rstd_from_ps(ps[6][:], pB[6], 512, D, rs, rsB, tmp, tmpB)
            for kc in range(8):
                kb.stt(yo[:, kc, cs], xT[:, kc, cs], fg[:, kc:kc + 1], rs, ALU.mult, ALU.mult, [xB[mt], CB_, rsB], [yoB[mt]])
            kb.st(yT_d.rearrange("(c p) t -> p c t", p=128)[:, :, cs], yo[:, :, cs], [yoB[mt]])
        if os.environ.get("KVERB"):
            print("phase peak bytes", cv.peak, "of", PH, "nins", kb.nins, {e: len(kb.prog[e]) for e in kb.ENGS})
        kb.final_wait("sp", yoB + [DBGB] + OUTB)
        kb.emit(st)
    return nc


_NC_CACHE = {}


def prep_inputs(inp):
    f = lambda a: np.ascontiguousarray(np.asarray(a, dtype=np.float32))
    consts = host_consts()
    shared = {"w_ada": f(inp["w_ada"]), "ffn_w_in": f(inp["ffn_w_in"]), "ffn_w_out": f(inp["ffn_w_out"]),
              "w_in": f(inp["w_in"]), "pool_map": f(inp["pool_map"]), "w_branch": f(inp["w_branch"]), "w_out": f(inp["w_out"])}
    for k, v in consts.items():
        shared["c_" + k] = np.ascontiguousarray(v)
    pvv = np.zeros((128, DEPTH * PL + 8), np.float32)
    for l in range(DEPTH):
        o = l * PL
        pvv[:, o + PV_BADA:o + PV_BADA + 72] = f(inp["b_ada"])[l].reshape(72, 128).T
        pvv[:, o + PV_NG:o + PV_NG + 24] = f(inp["norm_gain"])[l].reshape(24, 128).T
        cw = f(inp["ssd_conv_w"])[l].reshape(5, 12, 128)
        pvv[:, o + PV_CW:o + PV_CW + 60] = cw.transpose(2, 1, 0).reshape(128, 60)
        pvv[:, o + PV_CB:o + PV_CB + 12] = f(inp["ssd_conv_b"])[l].reshape(12, 128).T
        pvv[:, o + PV_SNG:o + PV_SNG + 8] = f(inp["ssd_norm_gain"])[l].reshape(8, 128).T
        pvv[:, o + PV_PS:o + PV_PS + 8] = f(inp["pool_scale"])[l].reshape(8, 128).T
        pvv[:, o + PV_DV:o + PV_DV + 8] = np.repeat(f(inp["ssd_d"])[l], 64).reshape(8, 128).T
        pvv[:, o + PV_DNG] = f(inp["diff_norm_gain"])[l]
        pvv[:, o + PV_LAM:o + PV_LAM + 256] = f(inp["diff_lambda"])[l].reshape(1, 256)
        pvv[:32, o + PV_DTB] = f(inp["ssd_dt_bias"])[l].reshape(32)
        pvv[:32, o + PV_ALOG] = f(inp["ssd_a_log"])[l].reshape(32)
    pvv[:, DEPTH * PL:] = f(inp["final_gain"]).reshape(8, 128).T
    shared["pv"] = pvv
    xp = f(inp["x_prompt"]); xs = f(inp["x_sample"])
    ck = f(inp["cache_k"]); cvv = f(inp["cache_v"]); s0 = f(inp["state_ssm"])
    cc = f(inp["c"]); cctx = f(inp["c_ctx"])
    in_maps = []
    for c in range(8):
        m = dict(shared)
        xt = np.concatenate([xp[2 * c], xp[2 * c + 1], xs[c]], axis=0)
        m["xT"] = np.ascontiguousarray(xt.T)
        cd = np.zeros((128, 8, 2), np.float32)
        cd[:, :, 0] = cctx.reshape(8, 128).T
        cd[:, :, 1] = cc[c].reshape(8, 128).T
        m["cond"] = cd.reshape(128, 16)
        m["ckT"] = np.ascontiguousarray(ck[c].transpose(0, 2, 3, 1))
        m["cv"] = np.ascontiguousarray(cvv[c])
        m["st0"] = np.ascontiguousarray(s0[c].transpose(0, 1, 4, 2, 3).reshape(DEPTH, 2, 64, 1024))
        in_maps.append(m)
    return in_maps


def kernel(**inputs):
    in_maps = prep_inputs(inputs)
    key = "full"
    if key not in _NC_CACHE:
        _NC_CACHE[key] = build()
    nc = _NC_CACHE[key]
    res = run_bass_kernel_spmd(nc, in_maps, core_ids=list(range(8)))
    y_prompt = np.zeros((16, 256, D), np.float32)
    y_sample = np.zeros((8, 1024, D), np.float32)
    nk = np.zeros((16, DEPTH, 256, 8, 128), np.float32)
    nv = np.zeros((16, DEPTH, 256, 8, 128), np.float32)
    ns = np.zeros((16, DEPTH, 2, 16, 64, 64), np.float32)
    for c in range(8):
        r = res.results[c]
        y = np.asarray(r["yT"]).T
        y_prompt[2 * c] = y[0:256]
        y_prompt[2 * c + 1] = y[256:512]
        y_sample[c] = y[512:1536]
        nk[2 * c:2 * c + 2] = np.asarray(r["nk"])
        nv[2 * c:2 * c + 2] = np.asarray(r["nv"])
        s = np.asarray(r["ns"]).reshape(2, DEPTH, 2, 64, 16, 64)
        ns[2 * c:2 * c + 2] = s.transpose(0, 1, 2, 4, 5, 3)
    return (y_prompt, y_sample, nk, nv, ns)
```

```python
import math, os, sys
SKIP = set(os.environ.get('KSKIP', '').split(','))
import numpy as np
from contextlib import ExitStack
import ml_dtypes
import concourse.bass as bass
import concourse.mybir as mybir
from concourse.bass_utils import run_bass_kernel_spmd

F32 = mybir.dt.float32
BF16 = mybir.dt.bfloat16
AF = mybir.ActivationFunctionType
ALU = mybir.AluOpType

D = 1024
DEPTH = 4
DFF = 2816
NT = 1536
EPS = 1e-6
IN_W = 9760
OFF_Z, OFF_XBC, OFF_DT, OFF_Q, OFF_K, OFF_V, OFF_U, OFF_G = 0, 1024, 2560, 2592, 3616, 4640, 5664, 6688
PV_BADA, PV_NG, PV_CW, PV_CB, PV_SNG, PV_PS, PV_DV, PV_DNG, PV_LAM, PV_DTB, PV_ALOG = 0, 72, 96, 156, 168, 176, 184, 192, 193, 449, 450
PL = 451
NEG = -30000.0
DEBUG_ANNOT = bool(os.environ.get('KANNOT'))
STOPAT = os.environ.get('KSTOP', '')


class StopBuild(Exception):
    pass


def chk(name):
    if STOPAT and name == STOPAT:
        raise StopBuild(name)


class Buf:
    __slots__ = ("name", "w", "r", "dsem", "dcnt")

    def __init__(self, name):
        self.name = name
        self.w = None
        self.r = []
        self.dsem = None
        self.dcnt = 0


class KB:
    ENGS = ("pe", "dve", "act", "pool", "sp")

    def __init__(self, nc, n_dma_sems=70):
        self.nc = nc
        self.prog = {e: [] for e in self.ENGS}
        self.cnt = {e: 0 for e in self.ENGS}
        self.waited = {}
        self.dma_free = ["d%d" % i for i in range(n_dma_sems)]
        self.sem_names = list(self.ENGS) + list(self.dma_free)
        self.sems = {}
        self.pending = {e: False for e in self.ENGS}
        self.nins = 0
        self.dbufs = []

    def _deps(self, eng, reads, writes):
        toks = []
        for b in reads:
            if b.w is not None:
                toks.append(b.w)
        for b in writes:
            if b.w is not None:
                toks.append(b.w)
            toks.extend(b.r)
        best = {}
        for (sk, v) in toks:
            if sk == "pe" and eng == "pe":
                continue
            if sk in self.cnt and v > self.cnt[sk]:
                if sk == eng:
                    continue
                raise RuntimeError("dep on open nosig group %s (eng %s)" % (sk, eng))
            if v > best.get(sk, 0):
                best[sk] = v
        for sk, v in best.items():
            if self.waited.get((eng, sk), 0) >= v:
                continue
            self.waited[(eng, sk)] = v
            self.prog[eng].append(("wait", sk, v))

    def op(self, eng, fn, reads=(), writes=(), sig=True):
        self._deps(eng, reads, writes)
        tok = (eng, self.cnt[eng] + 1)
        f = sys._getframe(1)
        if f.f_code.co_filename == __file__ and f.f_code.co_name in ("mm", "tr", "act", "tt", "ts", "stt", "cp", "recip", "memset"):
            f = f.f_back
        self.prog[eng].append(("op", fn, sig, "L%d" % f.f_lineno))
        self.nins += 1
        if sig:
            self.cnt[eng] += 1
        self.pending[eng] = not sig
        for b in reads:
            if len(b.r) > 64:
                b.r = b.r[-32:] if False else b.r
            b.r.append(tok)
        for b in writes:
            b.w = tok
            b.r = []
        return tok

    def dma(self, eng, fn, reads=(), writes=(), sbuf=None):
        self._deps(eng, reads, writes)
        b = sbuf if sbuf is not None else (writes[0] if writes else reads[0])
        if b.dsem is None:
            b.dsem = self.dma_free.pop(0)
            self.dbufs.append(b)
        b.dcnt += 1
        tok = (b.dsem, 16 * b.dcnt)
        self.prog[eng].append(("dma", fn, b.dsem))
        self.nins += 1
        for x in reads:
            x.r.append(tok)
        for x in writes:
            x.w = tok
            x.r = []
        return tok

    def barrier(self, engs=("pe", "dve", "act", "sp")):
        for e in self.ENGS:
            assert not self.pending[e]
        targets = [(e2, self.cnt[e2]) for e2 in ("pe", "dve", "act", "pool")]
        targets += [(b.dsem, 16 * b.dcnt) for b in self.dbufs]
        for e in engs:
            for (sk, v) in targets:
                if v == 0 or self.waited.get((e, sk), 0) >= v:
                    continue
                self.waited[(e, sk)] = v
                self.prog[e].append(("wait", sk, v))

    def final_wait(self, eng, bufs):
        self._deps(eng, bufs, bufs)

    def simulate(self):
        sem = {n: 0 for n in self.sem_names}
        pc = {e: 0 for e in self.ENGS}
        progress = True
        while progress:
            progress = False
            for e in self.ENGS:
                prog = self.prog[e]
                while pc[e] < len(prog):
                    it = prog[pc[e]]
                    if it[0] == "wait":
                        if sem[it[1]] < it[2]:
                            break
                    elif it[0] == "op":
                        if it[2]:
                            sem[e] += 1
                    else:
                        sem[it[2]] += 16
                    pc[e] += 1
                    progress = True
        bad = [(e, pc[e], len(self.prog[e]), self.prog[e][pc[e]][:3], sem[self.prog[e][pc[e]][1]] if self.prog[e][pc[e]][0] == "wait" else None)
               for e in self.ENGS if pc[e] < len(self.prog[e])]
        if bad:
            raise RuntimeError("DEADLOCK in emitted program: %s" % (bad,))
        for e in self.ENGS:
            assert sem[e] == self.cnt[e]

    def emit(self, stack):
        self.simulate()
        nc = self.nc
        for nm in self.sem_names:
            self.sems[nm] = stack.enter_context(nc.semaphore("s_" + nm))
        block = stack.enter_context(nc.Block())
        deco = {"pe": block.tensor, "dve": block.vector, "act": block.scalar, "pool": block.gpsimd, "sp": block.sync}
        for e in self.ENGS:
            assert not self.pending[e], "engine %s ends with open group" % e
            prog = self.prog[e]
            sems = self.sems
            esem = sems[e]

            def body(engine, prog=prog, esem=esem, sems=sems):
                for item in prog:
                    if item[0] == "wait":
                        engine.wait_ge(sems[item[1]], item[2])
                    elif item[0] == "op":
                        ins = item[1](engine)
                        if DEBUG_ANNOT:
                            ins.annotate(item[3])
                        if item[2]:
                            ins.then_inc(esem, 1)
                    else:
                        item[1](engine).then_inc(sems[item[2]], 16)

            deco[e](body)

    def mm(self, out, lhsT, rhs, start, stop, reads, writes, sig):
        self.op("pe", lambda e: e.matmul(out, lhsT=lhsT, rhs=rhs, start=start, stop=stop), reads, writes, sig)

    def tr(self, out, in_, ident, reads, writes, sig=True):
        self.op("pe", lambda e: e.transpose(out, in_, ident), reads, writes, sig)

    def act(self, out, in_, func, reads, writes, bias=0.0, scale=1.0):
        self.op("act", lambda e: e.activation(out, in_, func, bias=bias, scale=scale), reads, writes)

    def tt(self, out, in0, in1, op, reads, writes, eng="dve"):
        self.op(eng, lambda e: e.tensor_tensor(out=out, in0=in0, in1=in1, op=op), reads, writes)

    def ts(self, out, in0, s1, s2, op0, op1, reads, writes, eng="dve"):
        if s2 is None:
            self.op(eng, lambda e: e.tensor_single_scalar(out, in0, s1, op0), reads, writes)
        else:
            self.op(eng, lambda e: e.tensor_scalar(out=out, in0=in0, scalar1=s1, scalar2=s2, op0=op0, op1=op1), reads, writes)

    def stt(self, out, in0, scalar, in1, op0, op1, reads, writes, eng="dve"):
        self.op(eng, lambda e: e.scalar_tensor_tensor(out=out, in0=in0, scalar=scalar, in1=in1, op0=op0, op1=op1), reads, writes)

    def cp(self, out, in_, reads, writes, eng="dve"):
        if eng == "act":
            self.op("act", lambda e: e.copy(out, in_), reads, writes)
        else:
            self.op(eng, lambda e: e.tensor_copy(out, in_), reads, writes)

    def recip(self, out, in_, reads, writes):
        self.op("dve", lambda e: e.reciprocal(out, in_), reads, writes)

    def memset(self, ap, val, writes, eng="dve"):
        self.op(eng, lambda e: e.memset(ap, val), (), writes)

    def ld(self, out, in_, writes, eng="sp", sbuf=None):
        self.dma(eng, lambda e: e.dma_start(out=out, in_=in_), (), writes, sbuf=sbuf)

    def st(self, out, in_, reads, eng="sp", sbuf=None):
        self.dma(eng, lambda e: e.dma_start(out=out, in_=in_), reads, (), sbuf=sbuf)


class Rot:
    def __init__(self, items):
        self.items = items
        self.i = 0

    def next(self):
        it = self.items[self.i % len(self.items)]
        self.i += 1
        return it


def host_consts():
    c = {}
    c["identf"] = np.eye(128, dtype=np.float32)
    c["identb"] = np.eye(128, dtype=np.float32).astype(ml_dtypes.bfloat16)
    c["onesb"] = np.ones((128, 128), np.float32).astype(ml_dtypes.bfloat16)
    c["onesf"] = np.ones((128, 128), np.float32)
    s = np.arange(128)[:, None]
    l = np.arange(128)[None, :]
    mf = np.where(l >= s, 0.0, NEG).astype(np.float32)
    mb = np.where(l <= s, 0.0, NEG).astype(np.float32)
    c["maskf"] = np.tile(mf, (1, 4)).astype(ml_dtypes.bfloat16)
    c["maskb"] = np.tile(mb, (1, 4)).astype(ml_dtypes.bfloat16)
    sel = np.zeros((32, 32, 128), np.float32)
    for h in range(32):
        sel[h, h, :] = 1.0
    c["sel"] = sel.reshape(32, 32 * 128)
    sg = np.zeros((32, 4), np.float32)
    sg[:16, 0] = 1.0
    sg[16:, 0] = -1.0
    sg[16:, 1] = 1.0
    sg[:, 2] = -1.0
    c["sgn"] = sg
    rm = np.ones((32, 1024), np.float32)
    rm[:, ::128] = 0.0
    c["rmask"] = rm
    n_freq = 16
    inv_freq = (10000.0 ** (-np.arange(n_freq, dtype=np.float32) / n_freq)).astype(np.float32)
    t = np.arange(1024)
    pos_row = (t // 64).astype(np.float32)
    pos_col = (t % 64).astype(np.float32)
    cos = np.zeros((128, 1024), np.float32)
    sin = np.zeros((128, 1024), np.float32)
    perm = np.zeros((128, 128), np.float32)
    for d in range(128):
        dd = d % 64
        pos = pos_row if dd < 32 else pos_col
        f = inv_freq[dd % 16]
        ang = (pos * f).astype(np.float32)
        cos[d] = np.cos(ang)
        sin[d] = np.sin(ang)
        if (d % 32) < 16:
            perm[d + 16, d] = -1.0
        else:
            perm[d - 16, d] = 1.0
    c["cos"] = cos
    c["sin"] = sin
    c["perm"] = perm.astype(ml_dtypes.bfloat16)
    def invc(seqs, padlen):
        out = np.ones((4, padlen), np.float32)
        for g, w in enumerate((2, 4, 8, 16)):
            for (o, L) in seqs:
                tt = np.arange(L)
                lo = np.clip(tt - w // 2, 0, L)
                hi = np.clip(tt + w - w // 2, 0, L)
                out[g, o:o + L] = 1.0 / (hi - lo).astype(np.float32)
        return np.broadcast_to(out[None], (128, 4, padlen)).reshape(128, 4 * padlen).astype(ml_dtypes.bfloat16)
    c["invc0"] = invc([(8, 256), (280, 256)], 544)
    c["invc1"] = invc([(8, 1024)], 1040)
    return c


CONST_SPECS = [("identf", [128, 128], F32), ("identb", [128, 128], BF16), ("onesb", [128, 128], BF16),
               ("onesf", [128, 128], F32), ("maskf", [128, 512], BF16), ("maskb", [128, 512], BF16),
               ("sel", [32, 4096], F32), ("sgn", [32, 4], F32), ("rmask", [32, 1024], F32),
               ("cos", [128, 1024], F32), ("sin", [128, 1024], F32), ("perm", [128, 128], BF16),
               ("invc0", [128, 4 * 544], BF16), ("invc1", [128, 4 * 1040], BF16)]


def build(n_layers=DEPTH, do_mixer=True, dbg=False):
    nc = bass.Bass("TRN2", target_bir_lowering=False)

    def din(name, shape, dt=F32):
        return nc.dram_tensor(name, shape, dt, kind="ExternalInput").ap()

    def dout(name, shape, dt=F32):
        return nc.dram_tensor(name, shape, dt, kind="ExternalOutput").ap()

    xT_d = din("xT", [D, NT])
    cond_d = din("cond", [128, 16])
    pv_d = din("pv", [128, DEPTH * PL + 8])
    ckT_d = din("ckT", [DEPTH, 8, 128, 512])
    cv_d = din("cv", [DEPTH, 512, 8, 128])
    st0_d = din("st0", [DEPTH, 2, 64, 1024])
    w_ada_d = din("w_ada", [DEPTH, D, 9 * D])
    ffn_in_d = din("ffn_w_in", [DEPTH, 2, D, 2 * DFF])
    ffn_out_d = din("ffn_w_out", [DEPTH, 2, DFF, D])
    w_in_d = din("w_in", [DEPTH, D, IN_W])
    pmap_d = din("pool_map", [DEPTH, 4, 256, 256])
    wbr_d = din("w_branch", [DEPTH, 3, D, D])
    wo_d = din("w_out", [DEPTH, D, D])
    cdram = {n: din("c_" + n, shp, dt) for (n, shp, dt) in CONST_SPECS}
    yT_d = dout("yT", [D, NT])
    nk_d = dout("nk", [2, DEPTH, 256, 8, 128])
    nv_d = dout("nv", [2, DEPTH, 256, 8, 128])
    ns_d = dout("ns", [2, DEPTH, 2, 64, 1024])
    dbg_d = dout("dbg", [8, 128, 1536]) if dbg else None

    st = ExitStack()
    with st:
        def sb(name, shape, dt=F32):
            return st.enter_context(nc.sbuf_tensor("s_" + name, shape, dt))

        kb = KB(nc)
        xT = sb("xT", [128, 8, NT])
        xB = [Buf("x%d" % i) for i in range(3)]
        hB = [Buf("h%d" % i) for i in range(3)]
        NSLOT = 2
        wslots = Rot([(sb("ws%d" % i, [128, 4096], BF16), Buf("ws%d" % i)) for i in range(NSLOT)])
        pv = sb("pv", [128, DEPTH * PL + 8])
        cond = sb("cond", [128, 8, 2])
        sc = sb("sc", [128, 8, 2])
        mod = sb("mod", [128, DEPTH, 72, 2])
        Amod = sb("Amod", [128, DEPTH, 3, 8, 2])
        Gmod = sb("Gmod", [128, DEPTH, 3, 8, 2])
        CB_ = Buf("const")
        MB = Buf("mod")
        identf = sb("identf", [128, 128]); identb = sb("identb", [128, 128], BF16)
        onesb = sb("onesb", [128, 128], BF16); onesf = sb("onesf", [128, 128])
        ps = [st.enter_context(nc.psum_tensor("ps%d" % i, [128, 512], F32)) for i in range(7)]
        pB = [Buf("ps%d" % i) for i in range(7)]
        psb = st.enter_context(nc.psum_tensor("psb", [128, 1024], BF16))
        psbB = Buf("psb")
        PH = 108 * 1024
        phase = sb("phase", [128, PH // 4])

        class Carver:
            def __init__(self):
                self.off = 0
                self.live = []
                self.last = (0, 0)
                self.mark = None

            def reset(self, off=None):
                self.off = HT_END if off is None else off
                self.mark = None

            def mk(self, name):
                lo, hi = (self.mark if self.mark is not None else self.last[0]), self.off
                self.mark = None
                self.last = (lo, hi)
                b = Buf(name)
                toks = []
                for (l2, h2, b2) in self.live:
                    if l2 < hi and lo < h2:
                        if b2.w is not None:
                            toks.append(b2.w)
                        toks.extend(b2.r)
                b.r = list(dict.fromkeys(toks))
                self.live = [(l2, h2, b2) for (l2, h2, b2) in self.live if not (lo <= l2 and h2 <= hi)]
                self.live.append((lo, hi, b))
                return b

            def mks(self, name, n):
                lo, hi = (self.mark if self.mark is not None else self.last[0]), self.off
                self.last = (lo, hi)
                bs = []
                keep = self.live
                for i in range(n):
                    self.live = list(keep)
                    self.mark = lo
                    bs.append(self.mk("%s%d" % (name, i)))
                self.live = [(l2, h2, b2) for (l2, h2, b2) in keep if not (lo <= l2 and h2 <= hi)] + [(lo, hi, b) for b in bs]
                return bs

            def take(self, shape, dt=F32):
                n = int(np.prod(shape[1:]))
                nbytes = n * (4 if dt == F32 else 2)
                nbytes = (nbytes + 63) // 64 * 64
                assert self.off + nbytes <= PH, "phase region overflow %d" % (self.off + nbytes)
                self.peak = max(getattr(self, "peak", 0), self.off + nbytes)
                self.last = (self.off, self.off + nbytes)
                if self.mark is None:
                    self.mark = self.off
                ap = phase[0:shape[0], self.off // 4:(self.off + nbytes) // 4]
                if dt != F32:
                    ap = ap.bitcast(BF16)[:, 0:n]
                else:
                    ap = ap[:, 0:n]
                self.off += nbytes
                if len(shape) > 2:
                    names = " ".join("a%d" % i for i in range(len(shape) - 1))
                    kw = {"a%d" % i: shape[i + 1] for i in range(len(shape) - 1)}
                    ap = ap.rearrange("p (%s) -> p %s" % (names, names), **kw)
                return ap

        cv = Carver()
        hT = cv.take([128, 8, NT], BF16)
        HT_END = cv.off

        for mt in range(3):
            kb.ld(xT[:, :, mt * 512:(mt + 1) * 512], xT_d.rearrange("(c p) t -> p c t", p=128)[:, :, mt * 512:(mt + 1) * 512], [xB[mt]])
        kb.ld(pv[:], pv_d, [CB_])
        kb.ld(cond[:].rearrange("p c j -> p (c j)"), cond_d, [CB_])
        kb.ld(identf[:], cdram["identf"], [CB_]); kb.ld(identb[:], cdram["identb"], [CB_])
        kb.ld(onesb[:], cdram["onesb"], [CB_]); kb.ld(onesf[:], cdram["onesf"], [CB_])

        def pvl(l, off, n=1):
            return pv[:, l * PL + off:l * PL + off + n]

        kb.act(sc[:], cond[:], AF.Silu, [CB_], [MB])
        cv.reset()
        wa = Rot([(cv.take([128, 8, 512], BF16), cv.mk("wa%d" % i)) for i in range(5)])
        scb = sb("scb", [128, 8, 2], BF16)
        kb.cp(scb[:], sc[:], [MB], [MB])
        for l in range(n_layers):
            wv = w_ada_d[l].rearrange("(kc p) n -> p kc n", p=128)
            for blk in range(18):
                wt, wb = wa.next()
                kb.ld(wt, wv[:, :, blk * 512:(blk + 1) * 512], [wb], eng="pool")
                for mi in range(4):
                    m = blk * 4 + mi
                    for kc in range(8):
                        kb.mm(ps[6][:, 2 * m:2 * m + 2], wt[:, kc, mi * 128:(mi + 1) * 128], scb[:, kc, :], kc == 0, kc == 7,
                              [wb, MB], [pB[6]], sig=(kc == 7))
            kb.tt(mod[:, l], ps[6][:, 0:144].rearrange("p (m j) -> p m j", j=2),
                  pvl(l, PV_BADA, 72).unsqueeze(2).to_broadcast([128, 72, 2]), ALU.add, [pB[6], CB_], [MB])
            for i in range(3):
                kb.stt(Amod[:, l, i], mod[:, l, (3 * i + 1) * 8:(3 * i + 2) * 8, :], 1.0,
                       pvl(l, PV_NG + 8 * i, 8).unsqueeze(2).to_broadcast([128, 8, 2]), ALU.add, ALU.mult, [MB, CB_], [MB])
                kb.ts(Gmod[:, l, i], mod[:, l, (3 * i + 2) * 8:(3 * i + 3) * 8, :], 0.5, None, ALU.mult, None, [MB], [MB])

        CND = [0, 1, 1]
        DBGB = Buf("dbg")

        def tap(idx, ap, reads, n):
            if dbg:
                kb.dma("sp", lambda e: e.dma_start(out=dbg_d[idx, 0:ap.shape[0], 0:n], in_=ap), reads, (), sbuf=DBGB)

        tap(0, mod[:, 0].rearrange("p m j -> p (m j)"), [MB], 144)
        tap(1, Amod[:, 0].rearrange("p i c j -> p (i c j)"), [MB], 48)
        tap(2, sc[:].rearrange("p c j -> p (c j)"), [MB], 16)

        def rstd_from_ps(pst, pbuf, n, dfeat, out_ap, outbuf, tmp_ap, tmpbuf):
            kb.ts(tmp_ap, pst, 1.0 / dfeat, EPS, ALU.mult, ALU.add, [pbuf], [tmpbuf])
            kb.act(tmp_ap, tmp_ap, AF.Sqrt, [tmpbuf], [tmpbuf])
            kb.recip(out_ap, tmp_ap, [tmpbuf], [outbuf])

        def norm_to_h(l, i, mts, sq, sqB, rs, rsB, tmp, tmpB):
            for mt in mts:
                cs = slice(mt * 512, (mt + 1) * 512)
                for kc in range(8):
                    kb.act(sq[:, kc, :], xT[:, kc, cs], AF.Square, [xB[mt]], [sqB])
                for kc in range(8):
                    kb.mm(ps[6][:], onesb[:], sq[:, kc, :], kc == 0, kc == 7, [sqB, CB_], [pB[6]], sig=(kc == 7))
                rstd_from_ps(ps[6][:], pB[6], 512, D, rs, rsB, tmp, tmpB)
                if l == 0 and i == 0:
                    tap(3, rs, [rsB], 512) if mt == 0 else None
                    tap(4, rs, [rsB], 512) if mt == 1 else None
                for kc in range(8):
                    kb.stt(tmp, xT[:, kc, cs], Amod[:, l, i, kc, CND[mt]:CND[mt] + 1], rs, ALU.mult, ALU.mult, [xB[mt], MB, rsB], [tmpB])
                    kb.act(hT[:, kc, cs], tmp, AF.Identity, [tmpB, MB], [hB[mt]], bias=mod[:, l, 3 * i * 8 + kc, CND[mt]:CND[mt] + 1])

        def wload(srcs, shape):
            wt, wb = wslots.next()
            n = int(np.prod(shape[1:]))
            names = " ".join("a%d" % i for i in range(len(shape) - 1))
            kw = {"a%d" % i: shape[i + 1] for i in range(len(shape) - 1)}
            view = wt[:, 0:n].rearrange("p (%s) -> p %s" % (names, names), **kw) if len(shape) > 2 else wt[:, 0:n]
            for (fn, src) in srcs:
                kb.dma("pool", (lambda e, o=fn(view), s=src: e.dma_start(out=o, in_=s)), (), [wb])
            return view, wb

        def ffn(l, which):
            cv.reset()
            sq = cv.take([128, 8, 512], BF16); sqB = cv.mk("sq")
            rs = cv.take([128, 512]); rsB = cv.mk("rs")
            tmp = cv.take([128, 512]); tmpB = cv.mk("tmp")
            norm_to_h(l, 0 if which == 0 else 2, [0, 1, 2], sq, sqB, rs, rsB, tmp, tmpB)
            chk("ffn%d_norm" % which)
            cv.reset()
            aT = cv.take([128, 12, NT], BF16)
            aB = cv.mks("a", 12)
            sgs = Rot([(cv.take([128, NT]), cv.mk("sg%d" % i)) for i in range(2)])
            w1 = ffn_in_d[l, which].rearrange("(kc p) n -> p kc n", p=128)
            w2 = ffn_out_d[l, which].rearrange("(kc p) n -> p kc n", p=128)
            gi = 0 if which == 0 else 2
            pset = Rot([(0, 1, 2), (3, 4, 5)])
            for (p0, p1) in ((0, 6), (6, 11)):
                nk = (p1 - p0) * 2
                for pr in range(p0, p1):
                    wv, wb = wload([(lambda v: v[:, :, 0, :], w1[:, :, pr * 256:(pr + 1) * 256]),
                                    (lambda v: v[:, :, 1, :], w1[:, :, DFF + pr * 256:DFF + (pr + 1) * 256])], [128, 8, 2, 256])
                    for ci in range(2):
                        j = (pr - p0) * 2 + ci
                        sg, sgB = sgs.next()
                        bg = pset.next()
                        for kc in range(8):
                            for mt in range(3):
                                kb.mm(ps[bg[mt]][:], wv[:, kc, 0, ci * 128:(ci + 1) * 128], hT[:, kc, mt * 512:(mt + 1) * 512],
                                      kc == 0, kc == 7, [wb, hB[mt]], [pB[bg[mt]]], sig=(kc == 7 and mt == 2))
                        for mt in range(3):
                            kb.act(sg[:, mt * 512:(mt + 1) * 512], ps[bg[mt]][:], AF.Silu, [pB[bg[mt]]], [sgB])
                        bu = pset.next()
                        for kc in range(8):
                            for mt in range(3):
                                kb.mm(ps[bu[mt]][:], wv[:, kc, 1, ci * 128:(ci + 1) * 128], hT[:, kc, mt * 512:(mt + 1) * 512],
                                      kc == 0, kc == 7, [wb, hB[mt]], [pB[bu[mt]]], sig=(kc == 7 and mt == 2))
                        for mt in range(3):
                            kb.tt(aT[:, j, mt * 512:(mt + 1) * 512], ps[bu[mt]][:], sg[:, mt * 512:(mt + 1) * 512], ALU.mult,
                                  [pB[bu[mt]], sgB], [aB[j]])
                chk("ffn%d_in%d" % (which, p0))
                for mp in range(4):
                    wv, wb = wload([(lambda v: v, w2[:, p0 * 2:p0 * 2 + nk, mp * 256:(mp + 1) * 256])], [128, nk, 256])
                    for mi in range(2):
                        m = mp * 2 + mi
                        bo = pset.next()
                        for kc in range(nk):
                            for mt in range(3):
                                kb.mm(ps[bo[mt]][:], wv[:, kc, mi * 128:(mi + 1) * 128], aT[:, kc, mt * 512:(mt + 1) * 512],
                                      kc == 0, kc == nk - 1, [wb, aB[kc]], [pB[bo[mt]]], sig=(kc == nk - 1 and mt == 2))
                        for mt in range(3):
                            cs = slice(mt * 512, (mt + 1) * 512)
                            kb.stt(xT[:, m, cs], ps[bo[mt]][:], Gmod[:, l, gi, m, CND[mt]:CND[mt] + 1], xT[:, m, cs], ALU.mult, ALU.add,
                                   [pB[bo[mt]], MB, xB[mt]], [xB[mt]])


        sel = sb("sel", [32, 4096])
        sgn = sb("sgn", [32, 4])
        rmask = sb("rmask", [32, 1024])
        maskf = sb("maskf", [128, 512], BF16); maskb = sb("maskb", [128, 512], BF16)
        perm = sb("perm", [128, 128], BF16)
        lamt = sb("lamt", [128, DEPTH, 4])
        for (t_, n_) in ((sel, "sel"), (sgn, "sgn"), (rmask, "rmask"), (maskf, "maskf"), (maskb, "maskb"), (perm, "perm")):
            kb.ld(t_[:], cdram[n_], [CB_])
        LAM_INIT = [0.8 - 0.6 * math.exp(-0.3 * l_) for l_ in range(DEPTH)]
        GROUPS = [([0], [(0, 256), (256, 256)]), ([1, 2], [(0, 1024)])]
        pcur = [0]
        OUTB = []

        def pbank():
            b = pcur[0] % 6
            pcur[0] += 1
            return b

        def mixer(l):
            W = w_in_d[l].rearrange("(kc p) n -> p kc n", p=128)
            cv.reset()
            lt = cv.take([128, 128]); ltB = cv.mk("lt")
            la = pv[:, l * PL + PV_LAM:l * PL + PV_LAM + 256].rearrange("p (a d) -> p a d", a=4)
            kb.tt(lt[:, 0:64], la[:, 0, :], la[:, 1, :], ALU.mult, [CB_], [ltB])
            kb.tt(lt[:, 64:128], la[:, 2, :], la[:, 3, :], ALU.mult, [CB_], [ltB])
            kb.op("dve", lambda e: e.reduce_sum(out=lamt[:, l, 0:2], in_=lt[:].rearrange("p (a d) -> p a d", a=2), axis=mybir.AxisListType.X), [ltB], [MB])
            kb.act(lamt[:, l, 0:2], lamt[:, l, 0:2], AF.Exp, [MB], [MB])
            kb.tt(lamt[:, l, 2:3], lamt[:, l, 0:1], lamt[:, l, 1:2], ALU.subtract, [MB], [MB])
            kb.ts(lamt[:, l, 2:3], lamt[:, l, 2:3], LAM_INIT[l], None, ALU.add, None, [MB], [MB])
            kb.ts(lamt[:, l, 3:4], lamt[:, l, 2:3], -1.0, None, ALU.mult, None, [MB], [MB])
            for gi, (mts, seqs) in enumerate(GROUPS):
                mixer_group(l, W, gi, mts, seqs)

        def mixer_group(l, W, gi, mts, seqs):
            g0 = mts[0] * 512
            T = len(mts) * 512
            nmt = len(mts)
            cnd = CND[mts[0]]
            cv.reset()
            sq = cv.take([128, 8, 512], BF16); sqB = cv.mk("sq")
            rs = cv.take([128, 512]); rsB = cv.mk("rs")
            tmp = cv.take([128, 512]); tmpB = cv.mk("tmp")
            norm_to_h(l, 1, mts, sq, sqB, rs, rsB, tmp, tmpB)
            cv.reset()
            hBs = [hB[mt] for mt in mts]

            def proj(srcs, shape, lhs_fn, M, evac):
                wv, wb = wload(srcs, shape)
                for mi, mt in enumerate(mts):
                    b = pbank()
                    for kc in range(8):
                        kb.mm(ps[b][0:M, :], lhs_fn(wv, kc), hT[:, kc, mt * 512:(mt + 1) * 512], kc == 0, kc == 7, [wb, hB[mt]], [pB[b]], sig=(kc == 7))
                    evac(mi, ps[b][0:M, :], pB[b])

            yT = cv.take([128, 8, T], BF16); yB = cv.mks("y", nmt)
            YEND = cv.off
            if 'ssd' in SKIP:
                for mi in range(nmt):
                    kb.memset(yT[:, :, mi * 512:(mi + 1) * 512], 0.0, [yB[mi]])
            dtT = cv.take([32, T]); aT_ = cv.take([32, T]); cumT = cv.take([32, T]); fmB = cv.mk("fm")
            SSD0 = cv.off
            t1 = cv.take([32, T]); t2 = cv.take([32, T]); t12B = cv.mk("t12")
            ac = cv.take([32, 2]); acB = cv.mk("ac")
            kb.act(ac[:, 0:1], pv[0:32, l * PL + PV_ALOG:l * PL + PV_ALOG + 1], AF.Exp, [CB_], [acB])
            kb.ts(ac[:, 1:2], ac[:, 0:1], -1.0, None, ALU.mult, None, [acB], [acB])

            def ev_dt(mi, pst, pb):
                kb.act(t1[:, mi * 512:(mi + 1) * 512], pst, AF.Identity, [pb, CB_], [t12B], bias=pv[0:32, l * PL + PV_DTB:l * PL + PV_DTB + 1])
            proj([(lambda v: v, W[:, :, OFF_DT:OFF_DT + 32])], [128, 8, 32], lambda wv, kc: wv[:, kc, :], 32, ev_dt)
            kb.ts(t2[:], t1[:], -1.0, None, ALU.mult, None, [t12B], [t12B])
            kb.tt(t2[:], t2[:], t1[:], ALU.max, [t12B], [t12B])
            kb.act(t2[:], t2[:], AF.Exp, [t12B], [t12B], scale=-1.0)
            kb.act(t2[:], t2[:], AF.Ln, [t12B], [t12B], bias=1.0)
            kb.ts(t1[:], t1[:], 0.0, None, ALU.max, None, [t12B], [t12B])
            kb.tt(dtT[:], t1[:], t2[:], ALU.add, [t12B], [fmB])
            kb.ts(aT_[:], dtT[:], ac[:, 1:2], None, ALU.mult, None, [fmB, acB], [fmB])
            for c0 in range(0, T, 1024):
                n_ = min(1024, T - c0)
                kb.op("dve", lambda e, c0=c0, n_=n_: e.tensor_tensor_scan(out=cumT[:, c0:c0 + n_], data0=rmask[:, 0:n_], data1=aT_[:, c0:c0 + n_],
                                                                      initial=0.0, op0=ALU.mult, op1=ALU.add), [fmB, CB_], [fmB])

            nseq = len(seqs)
            PW = T + 4 * nseq
            for j in range(0 if 'ssd' in SKIP else 2):
                cv.reset(SSD0)
                xbc = cv.take([128, 6, T], BF16); xbB = cv.mk("xbc")
                PC0 = cv.off
                pcs = Rot([(cv.take([128, PW]), cv.mk("pc%d" % i)) for i in range(1)])
                accs = Rot([(cv.take([128, PW]), cv.mk("acc%d" % i)) for i in range(1)])
                for (pc_, pcb_) in pcs.items:
                    kb.memset(pc_[:], 0.0, [pcb_])
                chunk_ids = [4 * j, 4 * j + 1, 4 * j + 2, 4 * j + 3, 8 + j, 10 + j]
                for ci, c in enumerate(chunk_ids):
                    pc_, pcb_ = pcs.next()
                    acc, accB = accs.next()

                    def ev_pc(mi, pst, pb, pc_=pc_, pcb_=pcb_):
                        for k, (o, L) in enumerate(seqs):
                            lo = max(o, mi * 512); hi = min(o + L, (mi + 1) * 512)
                            if lo < hi:
                                kb.cp(pc_[:, 2 + lo + 4 * k:2 + hi + 4 * k], pst[:, lo - mi * 512:hi - mi * 512], [pb], [pcb_], eng="act")
                    proj([(lambda v: v, W[:, :, OFF_XBC + c * 128:OFF_XBC + (c + 1) * 128])], [128, 8, 128], lambda wv, kc: wv[:, kc, :], 128, ev_pc)
                    n_ = PW - 4
                    cw = pv[:, l * PL + PV_CW + c * 5:l * PL + PV_CW + c * 5 + 5]
                    kb.ts(acc[:, 0:n_], pc_[:, 0:n_], cw[:, 0:1], None, ALU.mult, None, [pcb_, CB_], [accB])
                    for tp in range(1, 5):
                        kb.stt(acc[:, 0:n_], pc_[:, tp:tp + n_], cw[:, tp:tp + 1], acc[:, 0:n_], ALU.mult, ALU.add, [pcb_, CB_, accB], [accB])
                    for k, (o, L) in enumerate(seqs):
                        kb.act(xbc[:, ci, o:o + L], acc[:, o + 4 * k:o + 4 * k + L], AF.Silu, [accB, CB_], [xbB],
                               bias=pv[:, l * PL + PV_CB + c:l * PL + PV_CB + c + 1])
                cv.reset(PC0)
                Hf = cv.take([128, 512]); Hb = cv.take([128, 512]); Hfb = cv.take([128, 512], BF16); HB_ = cv.mk("H")
                ntmax = max(L for (_, L) in seqs) // 128
                Hbin = cv.take([128, ntmax, 512], BF16); HbinB = cv.mk("Hbin")
                sets = []
                for si in range(2):
                    S = {}
                    S["tok"] = cv.take([128, 96]); S["arg"] = cv.take([128, 32]); S["dte"] = cv.take([128, 32]); S["cd"] = cv.take([128, 32]); S["s2"] = cv.take([128, 32]); S["ct"] = cv.take([128, 16]); S["tokB"] = cv.mk("tok%d" % si)
                    S["btok"] = cv.take([128, 128], BF16); S["btB"] = cv.mk("btok%d" % si)
                    S["xdf"] = cv.take([128, 512], BF16); S["xdb"] = cv.take([128, 512], BF16); S["xdd"] = cv.take([128, 512], BF16); S["xdB"] = cv.mk("xd%d" % si)
                    S["Rt"] = cv.take([32, 128]); S["Cn"] = cv.take([32, 128]); S["EL"] = cv.take([32, 128]); S["tb_"] = cv.take([32, 128]); S["rB"] = cv.mk("R%d" % si)
                    S["ty"] = cv.take([128, 4, 128]); S["tyB"] = cv.mk("ty%d" % si)
                    sets.append(S)
                par = [0]
                ct = None
                tok = arg = dte = cd = s2 = tokB = btok = btB = xdf = xdb = xdd = xdB = Rt = Cn = EL = tb_ = rB = ty = tyB = None

                def nextset():
                    nonlocal tok, arg, dte, cd, s2, tokB, btok, btB, xdf, xdb, xdd, xdB, Rt, Cn, EL, tb_, rB, ty, tyB, ct
                    S = sets[par[0] % 2]
                    par[0] += 1
                    tok, arg, dte, cd, s2, tokB = S["tok"], S["arg"], S["dte"], S["cd"], S["s2"], S["tokB"]
                    ct = S["ct"]
                    btok, btB = S["btok"], S["btB"]
                    xdf, xdb, xdd, xdB = S["xdf"], S["xdb"], S["xdd"], S["xdB"]
                    Rt, Cn, EL, tb_, rB = S["Rt"], S["Cn"], S["EL"], S["tb_"], S["rB"]
                    ty, tyB = S["ty"], S["tyB"]

                cbsR = Rot([(cv.take([128, 128]), cv.mk("cb%d" % i)) for i in range(2)])
                decs = Rot([(cv.take([128, 512]), cv.mk("dec%d" % i)) for i in range(2)])
                scs = [(cv.take([128, 4, 128], BF16), cv.mk("sc%d" % i)) for i in range(2)]
                ebcR = Rot([(cv.take([128, 512]), cv.mk("ebc%d" % i)) for i in range(2)])
                ces = [(cv.take([128, 4, 128], BF16), cv.mk("ce%d" % i)) for i in range(2)]
                segb = Rot([3, 6])
                PS_T, PS_S, PS_CB, PS_SEG, PS_E, PS_Y = 0, 1, 2, 3, 4, 5

                for k, (o, L) in enumerate(seqs):
                    nt = L // 128
                    kb.memset(Hf[:], 0.0, [HB_]); kb.memset(Hb[:], 0.0, [HB_])
                    if gi == 1:
                        for d_, H_ in ((0, Hf), (1, Hb)):
                            for gg in range(2):
                                kb.ld(H_[gg * 64:(gg + 1) * 64, gg * 256:(gg + 1) * 256],
                                      st0_d[l, d_][:, (8 * j + 4 * gg) * 64:(8 * j + 4 * gg + 4) * 64], [HB_])
                    kb.cp(Hfb[:], Hf[:], [HB_], [HB_])

                    def prep(i):
                        tsl = slice(o + i * 128, o + (i + 1) * 128)
                        kb.tr(ps[PS_T][:, 0:32], dtT[:, tsl], identf[0:32, 0:32], [fmB, CB_], [pB[PS_T]], sig=False)
                        kb.tr(ps[PS_T][:, 32:64], aT_[:, tsl], identf[0:32, 0:32], [fmB, CB_], [pB[PS_T]], sig=False)
                        kb.tr(ps[PS_T][:, 64:96], cumT[:, tsl], identf[0:32, 0:32], [fmB, CB_], [pB[PS_T]], sig=True)
                        kb.cp(tok[:], ps[PS_T][:, 0:96], [pB[PS_T]], [tokB], eng="act")
                        kb.mm(ps[PS_T][:, 96:128], onesf[:], tok[:, 32:64], True, True, [tokB, CB_], [pB[PS_T]], sig=True)
                        kb.tt(arg[:, 0:16], ps[PS_T][:, 96:112], tok[:, 64:80], ALU.subtract, [pB[PS_T], tokB], [tokB])
                        kb.tt(arg[:, 16:32], tok[:, 80:96], tok[:, 48:64], ALU.subtract, [tokB], [tokB])
                        kb.ts(ct[:], tok[:, 64:80], -1.0, None, ALU.mult, None, [tokB], [tokB])
                        kb.act(dte[:], arg[:], AF.Exp, [tokB], [tokB])
                        kb.act(cd[:], ps[PS_T][:, 96:128], AF.Exp, [pB[PS_T]], [tokB])
                        kb.tt(s2[:], tok[:, 0:32], dte[:], ALU.mult, [tokB], [tokB])
                        for ci in range(4):
                            kb.tr(psb[:, ci * 128:(ci + 1) * 128], xbc[:, ci, tsl], identb[:], [xbB, CB_], [psbB], sig=False)
                        kb.tr(psb[:, 512:640], xbc[:, 4, tsl], identb[:], [xbB, CB_], [psbB], sig=True)
                        kb.cp(btok[:], psb[:, 512:640], [psbB], [btB])
                        return tsl

                    def xprod(out, col0):
                        kb.tt(out[:].rearrange("p (h d) -> p h d", h=8), psb[:, 0:512].rearrange("p (h d) -> p h d", h=8),
                              col0.unsqueeze(2).to_broadcast([128, 8, 64]), ALU.mult, [psbB, tokB], [xdB])

                    def state_update(H_, xsrc, cdcol):
                        kb.mm(ps[PS_S][:], btok[:], xsrc[:], True, True, [btB, xdB], [pB[PS_S]], sig=True)
                        kb.tt(H_[:].rearrange("p (h d) -> p h d", h=8), H_[:].rearrange("p (h d) -> p h d", h=8),
                              cdcol.unsqueeze(2).to_broadcast([128, 8, 64]), ALU.mult, [HB_, tokB], [HB_])
                        kb.tt(H_[:], H_[:], ps[PS_S][:], ALU.add, [HB_, pB[PS_S]], [HB_])

                    for i in range(nt - 1, -1, -1):
                        nextset()
                        prep(i)
                        kb.cp(Hbin[:, i, :], Hb[:], [HB_], [HbinB])
                        xprod(xdd, s2[:, 16 + 8 * j:16 + 8 * j + 8])
                        state_update(Hb, xdd, cd[:, 16 + 8 * j:16 + 8 * j + 8])
                    for i in range(nt):
                        nextset()
                        tsl = prep(i)
                        xprod(xdf, tok[:, 8 * j:8 * j + 8])
                        xprod(xdb, tok[:, 16 + 8 * j:16 + 8 * j + 8])
                        xprod(xdd, s2[:, 8 * j:8 * j + 8])
                        kb.ts(tb_[:], aT_[:, tsl], sgn[:, 1:2], None, ALU.mult, None, [fmB, CB_], [rB])
                        kb.stt(Rt[:], cumT[:, tsl], sgn[:, 0:1], tb_[:], ALU.mult, ALU.add, [fmB, CB_, rB], [rB])
                        last = o + i * 128 + 127
                        kb.stt(EL[:], cumT[:, last:last + 1].to_broadcast([32, 128]), sgn[:, 1:2], Rt[:], ALU.mult, ALU.add, [fmB, CB_, rB], [rB])
                        for gg in range(2):
                            r0 = gg * 64
                            cbs, cbB = cbsR.next()
                            kb.mm(ps[PS_CB][:, 0:128], xbc[r0:r0 + 64, 4, tsl], xbc[r0:r0 + 64, 5, tsl], True, True, [xbB], [pB[PS_CB]], sig=True)
                            kb.cp(cbs[:], ps[PS_CB][:, 0:128], [pB[PS_CB]], [cbB], eng="act")
                            for d_ in range(2):
                                msk = maskf if d_ == 0 else maskb
                                sc_, scB = scs[d_]
                                ce_, ceB = ces[d_]
                                dec, decB = decs.next()
                                PS_SEG = segb.next()
                                ebc, ebB = ebcR.next()
                                kb.mm(ps[PS_SEG][:], identb[:], msk[:], True, False, [CB_], [pB[PS_SEG]], sig=False)
                                for hh in range(4):
                                    dh = d_ * 16 + 8 * j + 4 * gg + hh
                                    kb.mm(ps[PS_SEG][:, hh * 128:(hh + 1) * 128], sel[:, dh * 128:(dh + 1) * 128], Rt[:], False, hh == 3, [CB_, rB], [pB[PS_SEG]], sig=(hh == 3))
                                for hh in range(4):
                                    hx = 8 * j + 4 * gg + hh
                                    bcol = ct[:, hx:hx + 1] if d_ == 0 else arg[:, 16 + hx:16 + hx + 1]
                                    kb.act(dec[:, hh * 128:(hh + 1) * 128], ps[PS_SEG][:, hh * 128:(hh + 1) * 128], AF.Exp, [pB[PS_SEG], tokB], [decB], bias=bcol)
                                kb.tt(sc_[:], dec[:].rearrange("p (h s) -> p h s", h=4), cbs[:].unsqueeze(1).to_broadcast([128, 4, 128]), ALU.mult, [decB, cbB], [scB])
                                for hh in range(4):
                                    dh = d_ * 16 + 8 * j + 4 * gg + hh
                                    kb.mm(ps[PS_E][:, hh * 128:(hh + 1) * 128], sel[:, dh * 128:(dh + 1) * 128], EL[:], True, True, [CB_, rB], [pB[PS_E]], sig=(hh == 3))
                                kb.act(ebc[:], ps[PS_E][:], AF.Exp, [pB[PS_E]], [ebB])
                                kb.tt(ce_[r0:r0 + 64], ebc[r0:r0 + 64, :].rearrange("p (h s) -> p h s", h=4),
                                      xbc[r0:r0 + 64, 5, tsl].unsqueeze(1).to_broadcast([64, 4, 128]), ALU.mult, [ebB, xbB], [ceB])
                            for hh in range(4):
                                hl = gg * 4 + hh
                                cl = hl // 2
                                yr = (hl % 2) * 64
                                out = ps[PS_Y][yr:yr + 64, cl * 128:(cl + 1) * 128]
                                hs = slice(hl * 64, (hl + 1) * 64)
                                kb.mm(out, xdf[:, hs], scs[0][0][:, hh, :], True, False, [xdB, scs[0][1]], [pB[PS_Y]], sig=False)
                                kb.mm(out, xdb[:, hs], scs[1][0][:, hh, :], False, False, [xdB, scs[1][1]], [pB[PS_Y]], sig=False)
                                kb.mm(out, Hfb[r0:r0 + 64, hs], ces[0][0][r0:r0 + 64, hh, :], False, False, [HB_, ces[0][1]], [pB[PS_Y]], sig=False)
                                kb.mm(out, Hbin[r0:r0 + 64, i, hs], ces[1][0][r0:r0 + 64, hh, :], False, True, [HbinB, ces[1][1]], [pB[PS_Y]], sig=True)
                        dv = pv[:, l * PL + PV_DV + 4 * j:l * PL + PV_DV + 4 * j + 4]
                        kb.tt(ty[:], xbc[:, 0:4, tsl], dv.unsqueeze(2).to_broadcast([128, 4, 128]), ALU.mult, [xbB, CB_], [tyB])
                        kb.tt(yT[:, 4 * j:4 * j + 4, tsl], ty[:], ps[PS_Y][:].rearrange("p (c s) -> p c s", c=4), ALU.add, [tyB, pB[PS_Y]], yB)
                        state_update(Hf, xdd, cd[:, 8 * j:8 * j + 8])
                        kb.cp(Hfb[:], Hf[:], [HB_], [HB_])
                    if gi == 0:
                        for d_, H_ in ((0, Hf), (1, Hb)):
                            for gg in range(2):
                                kb.st(ns_d[k, l, d_][:, (8 * j + 4 * gg) * 64:(8 * j + 4 * gg + 4) * 64],
                                      H_[gg * 64:(gg + 1) * 64, gg * 256:(gg + 1) * 256], [HB_])
                        kb.final_wait("dve", [HB_])
                        OUTB.append(HB_)

            cv.reset(YEND)
            merged = cv.take([128, 8, T], BF16); mgB = cv.mks("mg", nmt)
            mergedb = merged; mgbB = mgB
            tgs = Rot([(cv.take([128, 512]), cv.mk("tg%d" % i)) for i in range(2)])
            tms = Rot([(cv.take([128, 512]), cv.mk("tm%d" % i)) for i in range(2)])
            MRG2 = cv.off
            szs = Rot([(cv.take([128, T]), cv.mk("sz%d" % i)) for i in range(2)])
            for c in range(8):
                sz, szB = szs.next()

                def ev_z(mi, pst, pb, sz=sz, szB=szB):
                    kb.act(sz[:, mi * 512:(mi + 1) * 512], pst, AF.Silu, [pb], [szB])
                proj([(lambda v: v, W[:, :, OFF_Z + c * 128:OFF_Z + (c + 1) * 128])], [128, 8, 128], lambda wv, kc: wv[:, kc, :], 128, ev_z)
                kb.tt(yT[:, c, :], yT[:, c, :], sz[:], ALU.mult, yB + [szB], yB)
            sq = cv.take([128, 8, 512], BF16); sqB = cv.mk("sq")
            rs = cv.take([128, 512]); rsB = cv.mk("rs")
            tmp = cv.take([128, 512]); tmpB = cv.mk("tmp")
            for mi in range(nmt):
                cs = slice(mi * 512, (mi + 1) * 512)
                for kc in range(8):
                    kb.act(sq[:, kc, :], yT[:, kc, cs], AF.Square, yB, [sqB])
                for kc in range(8):
                    kb.mm(ps[6][:], onesb[:], sq[:, kc, :], kc == 0, kc == 7, [sqB, CB_], [pB[6]], sig=(kc == 7))
                rstd_from_ps(ps[6][:], pB[6], 512, D, rs, rsB, tmp, tmpB)
                for kc in range(8):
                    kb.stt(yT[:, kc, cs], yT[:, kc, cs], pv[:, l * PL + PV_SNG + kc:l * PL + PV_SNG + kc + 1], rs, ALU.mult, ALU.mult, yB + [CB_, rsB], yB)


            def merge(n):
                for m in range(8):
                    wv, wb = wload([(lambda v: v[:, :, 0, :], wbr_d[l, n].rearrange("(kc p) n -> p kc n", p=128)[:, :, m * 128:(m + 1) * 128]),
                                    (lambda v: v[:, :, 1, :], W[:, :, OFF_G + n * 1024 + m * 128:OFF_G + n * 1024 + (m + 1) * 128])], [128, 8, 2, 128])
                    for mi, mt in enumerate(mts):
                        cs = slice(mi * 512, (mi + 1) * 512)
                        bp = pbank(); bq = pbank()
                        for kc in range(8):
                            kb.mm(ps[bp][:], wv[:, kc, 0, :], yT[:, kc, cs], kc == 0, kc == 7, [wb] + yB, [pB[bp]], sig=(kc == 7))
                        for kc in range(8):
                            kb.mm(ps[bq][:], wv[:, kc, 1, :], hT[:, kc, mt * 512:(mt + 1) * 512], kc == 0, kc == 7, [wb, hB[mt]], [pB[bq]], sig=(kc == 7))
                        tg, tgB = tgs.next()
                        kb.act(tg[:], ps[bq][:], AF.Tanh, [pB[bq]], [tgB], scale=0.5)
                        if n == 0:
                            kb.stt(merged[:, m, cs], tg[:], 1.0, ps[bp][:], ALU.add, ALU.mult, [tgB, pB[bp]], [mgB[mi]])
                        else:
                            tm, tmB = tms.next()
                            kb.stt(tm[:], tg[:], 1.0, ps[bp][:], ALU.add, ALU.mult, [tgB, pB[bp]], [tmB])
                            if n == 1:
                                kb.tt(merged[:, m, cs], merged[:, m, cs], tm[:], ALU.add, [mgB[mi], tmB], [mgB[mi]])
                            else:
                                kb.tt(mergedb[:, m, cs], merged[:, m, cs], tm[:], ALU.add, [mgB[mi], tmB], [mgbB[mi]])

            if 'ssd' in SKIP:
                for mi in range(nmt):
                    kb.memset(yT[:, :, mi * 512:(mi + 1) * 512], 0.0, [yB[mi]])
            merge(0)

            cv.reset(MRG2)
            Tk = T + (512 if gi == 1 else 0)
            ntk = Tk // 128
            NQB = T // 256
            qz = cv.take([128, NQB, 2, 256], BF16); qB = cv.mk("qz")
            kb.memset(qz[0:64, :, 1, :], 0.0, [qB])
            kb.memset(qz[64:128, :, 0, :], 0.0, [qB])
            kh = cv.take([128, Tk], BF16); kB_ = cv.mk("kh")
            vh = cv.take([128, ntk, 128], BF16); vB = cv.mk("vh")
            stg = cv.take([128, 4, 128]); stgB = cv.mk("stg")
            OUTB.append(stgB)
            qraw = cv.take([128, 512], BF16); qrB = cv.mk("qraw")
            r1 = cv.take([128, 512]); r2 = cv.take([128, 512]); rrB = cv.mk("rr")
            Pt = Rot([(cv.take([128, 2, 256], BF16), cv.mk("P%d" % i)) for i in range(3)])
            rden = cv.take([128, 2, 256]); on_ = cv.take([128, 2, 256]); od = cv.take([128, 256]); odsq = cv.take([128, 256], BF16); nB = cv.mk("nrm")
            rs2 = cv.take([128, 256]); tmp2 = cv.take([128, 256]); rs2B = cv.mk("rs2")
            gn = cv.take([128, 2]); gnB = cv.mk("gn")
            kb.ts(gn[:, 0:1], pv[:, l * PL + PV_DNG:l * PL + PV_DNG + 1], 1.0 - LAM_INIT[l], None, ALU.mult, None, [CB_], [gnB])
            if gi == 1:
                cos = cv.take([128, 1024]); sin = cv.take([128, 1024]); csB = cv.mk("cs")
                kb.ld(cos[:], cdram["cos"], [csB]); kb.ld(sin[:], cdram["sin"], [csB])
            PSC = [0, 1]
            qbc = [0]

            chk("att%d_pre" % gi)
            for h in range(0 if ('att' in SKIP or ('att%d' % gi) in SKIP) else 8):
                wv, wb = wload([(lambda v: v[:, :, 0, :], W[:, :, OFF_Q + h * 128:OFF_Q + (h + 1) * 128]),
                                (lambda v: v[:, :, 1, :], W[:, :, OFF_K + h * 128:OFF_K + (h + 1) * 128]),
                                (lambda v: v[:, :, 2, :], W[:, :, OFF_V + h * 128:OFF_V + (h + 1) * 128])], [128, 8, 3, 128])
                koff = Tk - T
                if gi == 1:
                    kb.dma("pool", lambda e, h=h: e.dma_start(out=kh[:, 0:512], in_=ckT_d[l, h]), (), [kB_])
                    kb.dma("pool", lambda e, h=h: e.dma_start(out=vh[:, 0:4, :], in_=cv_d[l][:, h, :].rearrange("(t p) e -> p t e", p=128)), (), [vB])
                for which, dst, dB, off in ((0, None, qB, 0), (1, kh, kB_, koff)):
                    for mi, mt in enumerate(mts):
                        b = 4 + (pcur[0] % 2); pcur[0] += 1
                        for kc in range(8):
                            kb.mm(ps[b][:], wv[:, kc, which, :], hT[:, kc, mt * 512:(mt + 1) * 512], kc == 0, kc == 7, [wb, hB[mt]], [pB[b]], sig=(kc == 7))
                        dcs = slice(off + mi * 512, off + (mi + 1) * 512)
                        if gi == 0:
                            if which == 0:
                                for c_ in range(2):
                                    kb.cp(qz[c_ * 64:(c_ + 1) * 64, 2 * mi:2 * mi + 2, c_, :], ps[b][c_ * 64:(c_ + 1) * 64, :].rearrange("p (a q) -> p a q", a=2),
                                          [pB[b]], [dB], eng="act")
                            else:
                                kb.cp(dst[:, dcs], ps[b][:], [pB[b]], [dB], eng="act")
                        else:
                            tcs = slice(mi * 512, (mi + 1) * 512)
                            kb.cp(qraw[:], ps[b][:], [pB[b]], [qrB], eng="act")
                            kb.mm(ps[6][:], perm[:], qraw[:], True, True, [CB_, qrB], [pB[6]], sig=True)
                            kb.tt(r1[:], ps[b][:], cos[:, tcs], ALU.mult, [pB[b], csB, qrB], [rrB])
                            kb.tt(r2[:], ps[6][:], sin[:, tcs], ALU.mult, [pB[6], csB], [rrB])
                            if which == 0:
                                for c_ in range(2):
                                    rs_ = slice(c_ * 64, (c_ + 1) * 64)
                                    kb.tt(qz[rs_, 2 * mi:2 * mi + 2, c_, :], r1[rs_, :].rearrange("p (a q) -> p a q", a=2),
                                          r2[rs_, :].rearrange("p (a q) -> p a q", a=2), ALU.add, [rrB], [dB])
                            else:
                                kb.tt(dst[:, dcs], r1[:], r2[:], ALU.add, [rrB], [dB])
                chk("att%d_h%d_qk" % (gi, h))
                for t0 in range(0, 0 if 'nov' in SKIP else T // 128, 4):
                    b = 4 + (pcur[0] % 2); pcur[0] += 1
                    for tt_ in range(4):
                        tl = t0 + tt_
                        for kc in range(8):
                            kb.mm(ps[b][:, tt_ * 128:(tt_ + 1) * 128], hT[:, kc, g0 + tl * 128:g0 + (tl + 1) * 128], wv[:, kc, 2, :], kc == 0, kc == 7,
                                  [wb] + hBs, [pB[b]], sig=(kc == 7 and tt_ == 3))
                    if gi == 0 and 'nost' not in SKIP:
                        kb.cp(stg[:], ps[b][:].rearrange("p (t e) -> p t e", t=4), [pB[b]], [stgB])
                        kb.cp(vh[:, koff // 128 + t0:koff // 128 + t0 + 4, :], stg[:], [stgB], [vB], eng="act")
                    else:
                        kb.cp(vh[:, koff // 128 + t0:koff // 128 + t0 + 4, :], ps[b][:].rearrange("p (t e) -> p t e", t=4), [pB[b]], [vB], eng="act")
                    if gi == 0 and 'nost' not in SKIP:
                        for k in range(0 if 'nodma' in SKIP else 2):
                            kb.st(nv_d[k, l][:, h, :].rearrange("(t p) e -> p t e", p=128), stg[:, 2 * k:2 * k + 2, :], [stgB])
                        if 'nokst' in SKIP:
                            continue
                        b2 = 4 + (pcur[0] % 2); pcur[0] += 1
                        for tt_ in range(4):
                            tl = t0 + tt_
                            for kc in range(8):
                                kb.mm(ps[b2][:, tt_ * 128:(tt_ + 1) * 128], hT[:, kc, g0 + tl * 128:g0 + (tl + 1) * 128], wv[:, kc, 1, :], kc == 0, kc == 7,
                                      [wb] + hBs, [pB[b2]], sig=(kc == 7 and tt_ == 3))
                        kb.cp(stg[:], ps[b2][:].rearrange("p (t e) -> p t e", t=4), [pB[b2]], [stgB])
                        for k in range(0 if 'nodma' in SKIP else 2):
                            kb.st(nk_d[k, l][:, h, :].rearrange("(t p) e -> p t e", p=128), stg[:, 2 * k:2 * k + 2, :], [stgB])
                for k, (o, L) in enumerate(seqs if 'nocore' not in SKIP else []):
                    if gi == 0:
                        ktiles = list(range(o // 128, (o + L) // 128))
                    else:
                        ktiles = list(range(ntk))
                    for qb in range(L // 256):
                        qs = slice(o + qb * 256, o + (qb + 1) * 256)
                        gq = (o + qb * 256) // 256
                        PO, PD = ((2, 3), (4, 5))[qbc[0] % 2]
                        qbc[0] += 1
                        nk_ = len(ktiles)
                        Ps = {}

                        def score(ki):
                            kt = ktiles[ki]
                            bs = ki % 2
                            kb.mm(ps[bs][:], kh[:, kt * 128:(kt + 1) * 128], qz[:, gq, :, :].rearrange("p c q -> p (c q)"), True, True, [kB_, qB], [pB[bs]], sig=True)
                            P_, PB_ = Pt.next()
                            Ps[ki] = (P_, PB_)
                            kb.act(P_[:].rearrange("p c q -> p (c q)"), ps[bs][:], AF.Exp, [pB[bs]], [PB_], scale=0.125)

                        score(0)
                        for ki, kt in enumerate(ktiles):
                            if ki + 1 < nk_:
                                score(ki + 1)
                            P_, PB_ = Ps.pop(ki)
                            kb.mm(ps[PO][:], vh[:, kt, :], P_[:].rearrange("p c q -> p (c q)"), ki == 0, ki == nk_ - 1, [vB, PB_], [pB[PO]], sig=False)
                            kb.mm(ps[PD][:], onesb[:], P_[:].rearrange("p c q -> p (c q)"), ki == 0, ki == nk_ - 1, [CB_, PB_], [pB[PD]], sig=True)
                        if 'nonorm' in SKIP:
                            kb.cp(rden[:].rearrange("p c q -> p (c q)"), ps[PD][:], [pB[PD]], [nB])
                            kb.cp(on_[:].rearrange("p c q -> p (c q)"), ps[PO][:], [pB[PO]], [nB])
                            continue
                        kb.recip(rden[:].rearrange("p c q -> p (c q)"), ps[PD][:], [pB[PD]], [nB])
                        kb.tt(on_[:].rearrange("p c q -> p (c q)"), ps[PO][:], rden[:].rearrange("p c q -> p (c q)"), ALU.mult, [pB[PO], nB], [nB])
                        kb.stt(od[:], on_[:, 1, :], lamt[:, l, 3:4], on_[:, 0, :], ALU.mult, ALU.add, [nB, MB], [nB])
                        kb.act(odsq[:], od[:], AF.Square, [nB], [nB])
                        kb.mm(ps[6][:, 0:256], onesb[:], odsq[:], True, True, [CB_, nB], [pB[6]], sig=True)
                        rstd_from_ps(ps[6][:, 0:256], pB[6], 256, 128, rs2, rs2B, tmp2, rs2B)
                        kb.stt(yT[:, h, qs], od[:], gn[:, 0:1], rs2[:], ALU.mult, ALU.mult, [nB, gnB, rs2B], yB)
            if 'att' in SKIP:
                for mi in range(nmt):
                    kb.memset(yT[:, :, mi * 512:(mi + 1) * 512], 0.0, [yB[mi]])
            chk("g%d_att" % gi)
            merge(1)
            chk("g%d_m1" % gi)

            cv.reset(MRG2)
            PP = T + 16 * nseq
            pu = Rot([(cv.take([128, PP]), cv.mk("pu%d" % i)) for i in range(2)])
            for (p_, pb_) in pu.items:
                kb.memset(p_[:], 0.0, [pb_])
            lv = [(cv.take([128, PP]), cv.mk("lv%d" % i)) for i in range(2)]
            pooled = cv.take([128, 2, T], BF16); plB = cv.mk("pooled")
            invc = cv.take([128, 4, PP], BF16); ivB = cv.mk("invc")
            kb.ld(invc[:].rearrange("p g n -> p (g n)"), cdram["invc%d" % gi], [ivB])
            for g in range(0 if 'pool' in SKIP else 4):
                for ci in range(2):
                    c = 2 * g + ci
                    pu_, puB = pu.next()

                    def ev_u(mi, pst, pb, pu_=pu_, puB=puB):
                        for k, (o, L) in enumerate(seqs):
                            lo = max(o, mi * 512); hi = min(o + L, (mi + 1) * 512)
                            if lo < hi:
                                kb.cp(pu_[:, 8 + lo + 16 * k:8 + hi + 16 * k], pst[:, lo - mi * 512:hi - mi * 512], [pb], [puB], eng="act")
                    proj([(lambda v: v, W[:, :, OFF_U + c * 128:OFF_U + (c + 1) * 128])], [128, 8, 128], lambda wv, kc: wv[:, kc, :], 128, ev_u)
                    src, srcB = pu_, puB
                    A_, AB_ = lv[0]
                    kb.tt(A_[:, 1:PP], src[:, 0:PP - 1], src[:, 1:PP], ALU.add, [srcB], [AB_])
                    kb.memset(A_[:, 0:1], 0.0, [AB_])
                    cur, curB = A_, AB_
                    for step, sh in enumerate((1, 2, 4)[:g]):
                        nx, nxB = lv[(step + 1) % 2]
                        kb.memset(nx[:, 0:sh], 0.0, [nxB]); kb.memset(nx[:, PP - sh:PP], 0.0, [nxB])
                        kb.tt(nx[:, sh:PP - sh], cur[:, 0:PP - 2 * sh], cur[:, 2 * sh:PP], ALU.add, [curB], [nxB])
                        cur, curB = nx, nxB
                    oth, othB = lv[0] if cur is lv[1][0] else lv[1]
                    kb.tt(oth[:], cur[:], invc[:, g, :], ALU.mult, [curB, ivB], [othB])
                    for k, (o, L) in enumerate(seqs):
                        kb.tt(pooled[:, ci, o:o + L], oth[:, 8 + o + 16 * k:8 + o + 16 * k + L], pu_[:, 8 + o + 16 * k:8 + o + 16 * k + L], ALU.subtract, [othB, puB], [plB])
                wv, wb = wload([(lambda v: v, pmap_d[l, g].rearrange("(kc p) n -> p kc n", p=128))], [128, 2, 256])
                for e_ in range(2):
                    for mi in range(nmt):
                        b = pbank()
                        for kc in range(2):
                            kb.mm(ps[b][:], wv[:, kc, e_ * 128:(e_ + 1) * 128], pooled[:, kc, mi * 512:(mi + 1) * 512], kc == 0, kc == 1, [wb, plB], [pB[b]], sig=(kc == 1))
                        kb.ts(yT[:, 2 * g + e_, mi * 512:(mi + 1) * 512], ps[b][:], pv[:, l * PL + PV_PS + 2 * g + e_:l * PL + PV_PS + 2 * g + e_ + 1], None, ALU.mult, None,
                              [pB[b], CB_], yB)
            chk("g%d_pool" % gi)
            merge(2)
            chk("g%d_m2" % gi)

            wo = wo_d[l].rearrange("(kc p) n -> p kc n", p=128)
            for m in range(8):
                wv, wb = wload([(lambda v: v, wo[:, :, m * 128:(m + 1) * 128])], [128, 8, 128])
                for mi, mt in enumerate(mts):
                    b = pbank()
                    for kc in range(8):
                        kb.mm(ps[b][:], wv[:, kc, :], mergedb[:, kc, mi * 512:(mi + 1) * 512], kc == 0, kc == 7, [wb, mgbB[mi]], [pB[b]], sig=(kc == 7))
                    cs = slice(mt * 512, (mt + 1) * 512)
                    kb.stt(xT[:, m, cs], ps[b][:], Gmod[:, l, 1, m, cnd:cnd + 1], xT[:, m, cs], ALU.mult, ALU.add, [pB[b], MB, xB[mt]], [xB[mt]])
            chk("g%d_out" % gi)

        try:
            for l in range(n_layers):
                ffn(l, 0)
                if do_mixer:
                    mixer(l)
                ffn(l, 1)
        except StopBuild as ex:
            print("STOPPED at", ex)

        cv.reset()
        sq = cv.take([128, 8, 512], BF16); sqB = cv.mk("sq")
        rs = cv.take([128, 512]); rsB = cv.mk("rs")
        tmp = cv.take([128, 512]); tmpB = cv.mk("tmp")
        yo = cv.take([128, 8, NT]); yoB = cv.mks("yo", 3)
        fg = pv[:, DEPTH * PL:DEPTH * PL + 8]
        for mt in range(3):
            cs = slice(mt * 512, (mt + 1) * 512)
            for kc in range(8):
                kb.act(sq[:, kc, :], xT[:, kc, cs], AF.Square, [xB[mt]], [sqB])
            for kc in range(8):
                kb.mm(ps[6][:], onesb[:], sq[:, kc, :], kc == 0, kc == 7, [sqB, CB_], [pB[6]], sig=(kc == 7))
            rstd_from_ps(ps[6][:], pB[6], 512, D, rs, rsB, tmp, tmpB)
            for kc in range(8):
                kb.stt(yo[:, kc, cs], xT[:, kc, cs], fg[:, kc:kc + 1], rs, ALU.mult, ALU.mult, [xB[mt], CB_, rsB], [yoB[mt]])
            kb.st(yT_d.rearrange("(c p) t -> p c t", p=128)[:, :, cs], yo[:, :, cs], [yoB[mt]])
        if os.environ.get("KVERB"):
            print("phase peak bytes", cv.peak, "of", PH, "nins", kb.nins, {e: len(kb.prog[e]) for e in kb.ENGS})
        kb.final_wait("sp", yoB + [DBGB] + OUTB)
        kb.emit(st)
    return nc


_NC_CACHE = {}


def prep_inputs(inp):
    f = lambda a: np.ascontiguousarray(np.asarray(a, dtype=np.float32))
    consts = host_consts()
    shared = {"w_ada": f(inp["w_ada"]), "ffn_w_in": f(inp["ffn_w_in"]), "ffn_w_out": f(inp["ffn_w_out"]),
              "w_in": f(inp["w_in"]), "pool_map": f(inp["pool_map"]), "w_branch": f(inp["w_branch"]), "w_out": f(inp["w_out"])}
    for k, v in consts.items():
        shared["c_" + k] = np.ascontiguousarray(v)
    pvv = np.zeros((128, DEPTH * PL + 8), np.float32)
    for l in range(DEPTH):
        o = l * PL
        pvv[:, o + PV_BADA:o + PV_BADA + 72] = f(inp["b_ada"])[l].reshape(72, 128).T
        pvv[:, o + PV_NG:o + PV_NG + 24] = f(inp["norm_gain"])[l].reshape(24, 128).T
        cw = f(inp["ssd_conv_w"])[l].reshape(5, 12, 128)
        pvv[:, o + PV_CW:o + PV_CW + 60] = cw.transpose(2, 1, 0).reshape(128, 60)
        pvv[:, o + PV_CB:o + PV_CB + 12] = f(inp["ssd_conv_b"])[l].reshape(12, 128).T
        pvv[:, o + PV_SNG:o + PV_SNG + 8] = f(inp["ssd_norm_gain"])[l].reshape(8, 128).T
        pvv[:, o + PV_PS:o + PV_PS + 8] = f(inp["pool_scale"])[l].reshape(8, 128).T
        pvv[:, o + PV_DV:o + PV_DV + 8] = np.repeat(f(inp["ssd_d"])[l], 64).reshape(8, 128).T
        pvv[:, o + PV_DNG] = f(inp["diff_norm_gain"])[l]
        pvv[:, o + PV_LAM:o + PV_LAM + 256] = f(inp["diff_lambda"])[l].reshape(1, 256)
        pvv[:32, o + PV_DTB] = f(inp["ssd_dt_bias"])[l].reshape(32)
        pvv[:32, o + PV_ALOG] = f(inp["ssd_a_log"])[l].reshape(32)
    pvv[:, DEPTH * PL:] = f(inp["final_gain"]).reshape(8, 128).T
    shared["pv"] = pvv
    xp = f(inp["x_prompt"]); xs = f(inp["x_sample"])
    ck = f(inp["cache_k"]); cvv = f(inp["cache_v"]); s0 = f(inp["state_ssm"])
    cc = f(inp["c"]); cctx = f(inp["c_ctx"])
    in_maps = []
    for c in range(8):
        m = dict(shared)
        xt = np.concatenate([xp[2 * c], xp[2 * c + 1], xs[c]], axis=0)
        m["xT"] = np.ascontiguousarray(xt.T)
        cd = np.zeros((128, 8, 2), np.float32)
        cd[:, :, 0] = cctx.reshape(8, 128).T
        cd[:, :, 1] = cc[c].reshape(8, 128).T
        m["cond"] = cd.reshape(128, 16)
        m["ckT"] = np.ascontiguousarray(ck[c].transpose(0, 2, 3, 1))
        m["cv"] = np.ascontiguousarray(cvv[c])
        m["st0"] = np.ascontiguousarray(s0[c].transpose(0, 1, 4, 2, 3).reshape(DEPTH, 2, 64, 1024))
        in_maps.append(m)
    return in_maps


def kernel(**inputs):
    in_maps = prep_inputs(inputs)
    key = "full"
    if key not in _NC_CACHE:
        _NC_CACHE[key] = build()
    nc = _NC_CACHE[key]
    res = run_bass_kernel_spmd(nc, in_maps, core_ids=list(range(8)))
    y_prompt = np.zeros((16, 256, D), np.float32)
    y_sample = np.zeros((8, 1024, D), np.float32)
    nk = np.zeros((16, DEPTH, 256, 8, 128), np.float32)
    nv = np.zeros((16, DEPTH, 256, 8, 128), np.float32)
    ns = np.zeros((16, DEPTH, 2, 16, 64, 64), np.float32)
    for c in range(8):
        r = res.results[c]
        y = np.asarray(r["yT"]).T
        y_prompt[2 * c] = y[0:256]
        y_prompt[2 * c + 1] = y[256:512]
        y_sample[c] = y[512:1536]
        nk[2 * c:2 * c + 2] = np.asarray(r["nk"])
        nv[2 * c:2 * c + 2] = np.asarray(r["nv"])
        s = np.asarray(r["ns"]).reshape(2, DEPTH, 2, 64, 16, 64)
        ns[2 * c:2 * c + 2] = s.transpose(0, 1, 2, 4, 5, 3)
    return (y_prompt, y_sample, nk, nv, ns)
```

```python
import math, os, sys
SKIP = set(os.environ.get('KSKIP', '').split(','))
import numpy as np
from contextlib import ExitStack
import ml_dtypes
import concourse.bass as bass
import concourse.mybir as mybir
from concourse.bass_utils import run_bass_kernel_spmd

F32 = mybir.dt.float32
BF16 = mybir.dt.bfloat16
AF = mybir.ActivationFunctionType
ALU = mybir.AluOpType

D = 1024
DEPTH = 4
DFF = 2816
NT = 1536
EPS = 1e-6
IN_W = 9760
OFF_Z, OFF_XBC, OFF_DT, OFF_Q, OFF_K, OFF_V, OFF_U, OFF_G = 0, 1024, 2560, 2592, 3616, 4640, 5664, 6688
PV_BADA, PV_NG, PV_CW, PV_CB, PV_SNG, PV_PS, PV_DV, PV_DNG, PV_LAM, PV_DTB, PV_ALOG = 0, 72, 96, 156, 168, 176, 184, 192, 193, 449, 450
PL = 451
NEG = -30000.0
DEBUG_ANNOT = bool(os.environ.get('KANNOT'))
STOPAT = os.environ.get('KSTOP', '')


class StopBuild(Exception):
    pass


def chk(name):
    if STOPAT and name == STOPAT:
        raise StopBuild(name)


class Buf:
    __slots__ = ("name", "w", "r", "dsem", "dcnt")

    def __init__(self, name):
        self.name = name
        self.w = None
        self.r = []
        self.dsem = None
        self.dcnt = 0


class KB:
    ENGS = ("pe", "dve", "act", "pool", "sp")

    def __init__(self, nc, n_dma_sems=70):
        self.nc = nc
        self.prog = {e: [] for e in self.ENGS}
        self.cnt = {e: 0 for e in self.ENGS}
        self.waited = {}
        self.dma_free = ["d%d" % i for i in range(n_dma_sems)]
        self.sem_names = list(self.ENGS) + list(self.dma_free)
        self.sems = {}
        self.pending = {e: False for e in self.ENGS}
        self.nins = 0
        self.dbufs = []

    def _deps(self, eng, reads, writes):
        toks = []
        for b in reads:
            if b.w is not None:
                toks.append(b.w)
        for b in writes:
            if b.w is not None:
                toks.append(b.w)
            toks.extend(b.r)
        best = {}
        for (sk, v) in toks:
            if sk == "pe" and eng == "pe":
                continue
            if sk in self.cnt and v > self.cnt[sk]:
                if sk == eng:
                    continue
                raise RuntimeError("dep on open nosig group %s (eng %s)" % (sk, eng))
            if v > best.get(sk, 0):
                best[sk] = v
        for sk, v in best.items():
            if self.waited.get((eng, sk), 0) >= v:
                continue
            self.waited[(eng, sk)] = v
            self.prog[eng].append(("wait", sk, v))

    def op(self, eng, fn, reads=(), writes=(), sig=True):
        self._deps(eng, reads, writes)
        tok = (eng, self.cnt[eng] + 1)
        f = sys._getframe(1)
        if f.f_code.co_filename == __file__ and f.f_code.co_name in ("mm", "tr", "act", "tt", "ts", "stt", "cp", "recip", "memset"):
            f = f.f_back
        self.prog[eng].append(("op", fn, sig, "L%d" % f.f_lineno))
        self.nins += 1
        if sig:
            self.cnt[eng] += 1
        self.pending[eng] = not sig
        for b in reads:
            if len(b.r) > 64:
                b.r = b.r[-32:] if False else b.r
            b.r.append(tok)
        for b in writes:
            b.w = tok
            b.r = []
        return tok

    def dma(self, eng, fn, reads=(), writes=(), sbuf=None):
        self._deps(eng, reads, writes)
        b = sbuf if sbuf is not None else (writes[0] if writes else reads[0])
        if b.dsem is None:
            b.dsem = self.dma_free.pop(0)
            self.dbufs.append(b)
        b.dcnt += 1
        tok = (b.dsem, 16 * b.dcnt)
        self.prog[eng].append(("dma", fn, b.dsem))
        self.nins += 1
        for x in reads:
            x.r.append(tok)
        for x in writes:
            x.w = tok
            x.r = []
        return tok

    def barrier(self, engs=("pe", "dve", "act", "sp")):
        for e in self.ENGS:
            assert not self.pending[e]
        targets = [(e2, self.cnt[e2]) for e2 in ("pe", "dve", "act", "pool")]
        targets += [(b.dsem, 16 * b.dcnt) for b in self.dbufs]
        for e in engs:
            for (sk, v) in targets:
                if v == 0 or self.waited.get((e, sk), 0) >= v:
                    continue
                self.waited[(e, sk)] = v
                self.prog[e].append(("wait", sk, v))

    def final_wait(self, eng, bufs):
        self._deps(eng, bufs, bufs)

    def simulate(self):
        sem = {n: 0 for n in self.sem_names}
        pc = {e: 0 for e in self.ENGS}
        progress = True
        while progress:
            progress = False
            for e in self.ENGS:
                prog = self.prog[e]
                while pc[e] < len(prog):
                    it = prog[pc[e]]
                    if it[0] == "wait":
                        if sem[it[1]] < it[2]:
                            break
                    elif it[0] == "op":
                        if it[2]:
                            sem[e] += 1
                    else:
                        sem[it[2]] += 16
                    pc[e] += 1
                    progress = True
        bad = [(e, pc[e], len(self.prog[e]), self.prog[e][pc[e]][:3], sem[self.prog[e][pc[e]][1]] if self.prog[e][pc[e]][0] == "wait" else None)
               for e in self.ENGS if pc[e] < len(self.prog[e])]
        if bad:
            raise RuntimeError("DEADLOCK in emitted program: %s" % (bad,))
        for e in self.ENGS:
            assert sem[e] == self.cnt[e]

    def emit(self, stack):
        self.simulate()
        nc = self.nc
        for nm in self.sem_names:
            self.sems[nm] = stack.enter_context(nc.semaphore("s_" + nm))
        block = stack.enter_context(nc.Block())
        deco = {"pe": block.tensor, "dve": block.vector, "act": block.scalar, "pool": block.gpsimd, "sp": block.sync}
        for e in self.ENGS:
            assert not self.pending[e], "engine %s ends with open group" % e
            prog = self.prog[e]
            sems = self.sems
            esem = sems[e]

            def body(engine, prog=prog, esem=esem, sems=sems):
                for item in prog:
                    if item[0] == "wait":
                        engine.wait_ge(sems[item[1]], item[2])
                    elif item[0] == "op":
                        ins = item[1](engine)
                        if DEBUG_ANNOT:
                            ins.annotate(item[3])
                        if item[2]:
                            ins.then_inc(esem, 1)
                    else:
                        item[1](engine).then_inc(sems[item[2]], 16)

            deco[e](body)

    def mm(self, out, lhsT, rhs, start, stop, reads, writes, sig):
        self.op("pe", lambda e: e.matmul(out, lhsT=lhsT, rhs=rhs, start=start, stop=stop), reads, writes, sig)

    def tr(self, out, in_, ident, reads, writes, sig=True):
        self.op("pe", lambda e: e.transpose(out, in_, ident), reads, writes, sig)

    def act(self, out, in_, func, reads, writes, bias=0.0, scale=1.0):
        self.op("act", lambda e: e.activation(out, in_, func, bias=bias, scale=scale), reads, writes)

    def tt(self, out, in0, in1, op, reads, writes, eng="dve"):
        self.op(eng, lambda e: e.tensor_tensor(out=out, in0=in0, in1=in1, op=op), reads, writes)

    def ts(self, out, in0, s1, s2, op0, op1, reads, writes, eng="dve"):
        if s2 is None:
            self.op(eng, lambda e: e.tensor_single_scalar(out, in0, s1, op0), reads, writes)
        else:
            self.op(eng, lambda e: e.tensor_scalar(out=out, in0=in0, scalar1=s1, scalar2=s2, op0=op0, op1=op1), reads, writes)

    def stt(self, out, in0, scalar, in1, op0, op1, reads, writes, eng="dve"):
        self.op(eng, lambda e: e.scalar_tensor_tensor(out=out, in0=in0, scalar=scalar, in1=in1, op0=op0, op1=op1), reads, writes)

    def cp(self, out, in_, reads, writes, eng="dve"):
        if eng == "act":
            self.op("act", lambda e: e.copy(out, in_), reads, writes)
        else:
            self.op(eng, lambda e: e.tensor_copy(out, in_), reads, writes)

    def recip(self, out, in_, reads, writes):
        self.op("dve", lambda e: e.reciprocal(out, in_), reads, writes)

    def memset(self, ap, val, writes, eng="dve"):
        self.op(eng, lambda e: e.memset(ap, val), (), writes)

    def ld(self, out, in_, writes, eng="sp", sbuf=None):
        self.dma(eng, lambda e: e.dma_start(out=out, in_=in_), (), writes, sbuf=sbuf)

    def st(self, out, in_, reads, eng="sp", sbuf=None):
        self.dma(eng, lambda e: e.dma_start(out=out, in_=in_), reads, (), sbuf=sbuf)


class Rot:
    def __init__(self, items):
        self.items = items
        self.i = 0

    def next(self):
        it = self.items[self.i % len(self.items)]
        self.i += 1
        return it


def host_consts():
    c = {}
    c["identf"] = np.eye(128, dtype=np.float32)
    c["identb"] = np.eye(128, dtype=np.float32).astype(ml_dtypes.bfloat16)
    c["onesb"] = np.ones((128, 128), np.float32).astype(ml_dtypes.bfloat16)
    c["onesf"] = np.ones((128, 128), np.float32)
    s = np.arange(128)[:, None]
    l = np.arange(128)[None, :]
    mf = np.where(l >= s, 0.0, NEG).astype(np.float32)
    mb = np.where(l <= s, 0.0, NEG).astype(np.float32)
    c["maskf"] = np.tile(mf, (1, 4)).astype(ml_dtypes.bfloat16)
    c["maskb"] = np.tile(mb, (1, 4)).astype(ml_dtypes.bfloat16)
    sel = np.zeros((32, 32, 128), np.float32)
    for h in range(32):
        sel[h, h, :] = 1.0
    c["sel"] = sel.reshape(32, 32 * 128)
    sg = np.zeros((32, 4), np.float32)
    sg[:16, 0] = 1.0
    sg[16:, 0] = -1.0
    sg[16:, 1] = 1.0
    sg[:, 2] = -1.0
    c["sgn"] = sg
    rm = np.ones((32, 1024), np.float32)
    rm[:, ::128] = 0.0
    c["rmask"] = rm
    n_freq = 16
    inv_freq = (10000.0 ** (-np.arange(n_freq, dtype=np.float32) / n_freq)).astype(np.float32)
    t = np.arange(1024)
    pos_row = (t // 64).astype(np.float32)
    pos_col = (t % 64).astype(np.float32)
    cos = np.zeros((128, 1024), np.float32)
    sin = np.zeros((128, 1024), np.float32)
    perm = np.zeros((128, 128), np.float32)
    for d in range(128):
        dd = d % 64
        pos = pos_row if dd < 32 else pos_col
        f = inv_freq[dd % 16]
        ang = (pos * f).astype(np.float32)
        cos[d] = np.cos(ang)
        sin[d] = np.sin(ang)
        if (d % 32) < 16:
            perm[d + 16, d] = -1.0
        else:
            perm[d - 16, d] = 1.0
    c["cos"] = cos
    c["sin"] = sin
    c["perm"] = perm.astype(ml_dtypes.bfloat16)
    def invc(seqs, padlen):
        out = np.ones((4, padlen), np.float32)
        for g, w in enumerate((2, 4, 8, 16)):
            for (o, L) in seqs:
                tt = np.arange(L)
                lo = np.clip(tt - w // 2, 0, L)
                hi = np.clip(tt + w - w // 2, 0, L)
                out[g, o:o + L] = 1.0 / (hi - lo).astype(np.float32)
        return np.broadcast_to(out[None], (128, 4, padlen)).reshape(128, 4 * padlen).astype(ml_dtypes.bfloat16)
    c["invc0"] = invc([(8, 256), (280, 256)], 544)
    c["invc1"] = invc([(8, 1024)], 1040)
    return c


CONST_SPECS = [("identf", [128, 128], F32), ("identb", [128, 128], BF16), ("onesb", [128, 128], BF16),
               ("onesf", [128, 128], F32), ("maskf", [128, 512], BF16), ("maskb", [128, 512], BF16),
               ("sel", [32, 4096], F32), ("sgn", [32, 4], F32), ("rmask", [32, 1024], F32),
               ("cos", [128, 1024], F32), ("sin", [128, 1024], F32), ("perm", [128, 128], BF16),
               ("invc0", [128, 4 * 544], BF16), ("invc1", [128, 4 * 1040], BF16)]


def build(n_layers=DEPTH, do_mixer=True, dbg=False):
    nc = bass.Bass("TRN2", target_bir_lowering=False)

    def din(name, shape, dt=F32):
        return nc.dram_tensor(name, shape, dt, kind="ExternalInput").ap()

    def dout(name, shape, dt=F32):
        return nc.dram_tensor(name, shape, dt, kind="ExternalOutput").ap()

    xT_d = din("xT", [D, NT])
    cond_d = din("cond", [128, 16])
    pv_d = din("pv", [128, DEPTH * PL + 8])
    ckT_d = din("ckT", [DEPTH, 8, 128, 512])
    cv_d = din("cv", [DEPTH, 512, 8, 128])
    st0_d = din("st0", [DEPTH, 2, 64, 1024])
    w_ada_d = din("w_ada", [DEPTH, D, 9 * D])
    ffn_in_d = din("ffn_w_in", [DEPTH, 2, D, 2 * DFF])
    ffn_out_d = din("ffn_w_out", [DEPTH, 2, DFF, D])
    w_in_d = din("w_in", [DEPTH, D, IN_W])
    pmap_d = din("pool_map", [DEPTH, 4, 256, 256])
    wbr_d = din("w_branch", [DEPTH, 3, D, D])
    wo_d = din("w_out", [DEPTH, D, D])
    cdram = {n: din("c_" + n, shp, dt) for (n, shp, dt) in CONST_SPECS}
    yT_d = dout("yT", [D, NT])
    nk_d = dout("nk", [2, DEPTH, 256, 8, 128])
    nv_d = dout("nv", [2, DEPTH, 256, 8, 128])
    ns_d = dout("ns", [2, DEPTH, 2, 64, 1024])
    dbg_d = dout("dbg", [8, 128, 1536]) if dbg else None

    st = ExitStack()
    with st:
        def sb(name, shape, dt=F32):
            return st.enter_context(nc.sbuf_tensor("s_" + name, shape, dt))

        kb = KB(nc)
        xT = sb("xT", [128, 8, NT])
        xB = [Buf("x%d" % i) for i in range(3)]
        hB = [Buf("h%d" % i) for i in range(3)]
        NSLOT = 2
        wslots = Rot([(sb("ws%d" % i, [128, 4096], BF16), Buf("ws%d" % i)) for i in range(NSLOT)])
        pv = sb("pv", [128, DEPTH * PL + 8])
        cond = sb("cond", [128, 8, 2])
        sc = sb("sc", [128, 8, 2])
        mod = sb("mod", [128, DEPTH, 72, 2])
        Amod = sb("Amod", [128, DEPTH, 3, 8, 2])
        Gmod = sb("Gmod", [128, DEPTH, 3, 8, 2])
        CB_ = Buf("const")
        MB = Buf("mod")
        identf = sb("identf", [128, 128]); identb = sb("identb", [128, 128], BF16)
        onesb = sb("onesb", [128, 128], BF16); onesf = sb("onesf", [128, 128])
        ps = [st.enter_context(nc.psum_tensor("ps%d" % i, [128, 512], F32)) for i in range(7)]
        pB = [Buf("ps%d" % i) for i in range(7)]
        psb = st.enter_context(nc.psum_tensor("psb", [128, 1024], BF16))
        psbB = Buf("psb")
        PH = 108 * 1024
        phase = sb("phase", [128, PH // 4])

        class Carver:
            def __init__(self):
                self.off = 0
                self.live = []
                self.last = (0, 0)
                self.mark = None

            def reset(self, off=None):
                self.off = HT_END if off is None else off
                self.mark = None

            def mk(self, name):
                lo, hi = (self.mark if self.mark is not None else self.last[0]), self.off
                self.mark = None
                self.last = (lo, hi)
                b = Buf(name)
                toks = []
                for (l2, h2, b2) in self.live:
                    if l2 < hi and lo < h2:
                        if b2.w is not None:
                            toks.append(b2.w)
                        toks.extend(b2.r)
                b.r = list(dict.fromkeys(toks))
                self.live = [(l2, h2, b2) for (l2, h2, b2) in self.live if not (lo <= l2 and h2 <= hi)]
                self.live.append((lo, hi, b))
                return b

            def mks(self, name, n):
                lo, hi = (self.mark if self.mark is not None else self.last[0]), self.off
                self.last = (lo, hi)
                bs = []
                keep = self.live
                for i in range(n):
                    self.live = list(keep)
                    self.mark = lo
                    bs.append(self.mk("%s%d" % (name, i)))
                self.live = [(l2, h2, b2) for (l2, h2, b2) in keep if not (lo <= l2 and h2 <= hi)] + [(lo, hi, b) for b in bs]
                return bs

            def take(self, shape, dt=F32):
                n = int(np.prod(shape[1:]))
                nbytes = n * (4 if dt == F32 else 2)
                nbytes = (nbytes + 63) // 64 * 64
                assert self.off + nbytes <= PH, "phase region overflow %d" % (self.off + nbytes)
                self.peak = max(getattr(self, "peak", 0), self.off + nbytes)
                self.last = (self.off, self.off + nbytes)
                if self.mark is None:
                    self.mark = self.off
                ap = phase[0:shape[0], self.off // 4:(self.off + nbytes) // 4]
                if dt != F32:
                    ap = ap.bitcast(BF16)[:, 0:n]
                else:
                    ap = ap[:, 0:n]
                self.off += nbytes
                if len(shape) > 2:
                    names = " ".join("a%d" % i for i in range(len(shape) - 1))
                    kw = {"a%d" % i: shape[i + 1] for i in range(len(shape) - 1)}
                    ap = ap.rearrange("p (%s) -> p %s" % (names, names), **kw)
                return ap

        cv = Carver()
        hT = cv.take([128, 8, NT], BF16)
        HT_END = cv.off

        for mt in range(3):
            kb.ld(xT[:, :, mt * 512:(mt + 1) * 512], xT_d.rearrange("(c p) t -> p c t", p=128)[:, :, mt * 512:(mt + 1) * 512], [xB[mt]])
        kb.ld(pv[:], pv_d, [CB_])
        kb.ld(cond[:].rearrange("p c j -> p (c j)"), cond_d, [CB_])
        kb.ld(identf[:], cdram["identf"], [CB_]); kb.ld(identb[:], cdram["identb"], [CB_])
        kb.ld(onesb[:], cdram["onesb"], [CB_]); kb.ld(onesf[:], cdram["onesf"], [CB_])

        def pvl(l, off, n=1):
            return pv[:, l * PL + off:l * PL + off + n]

        kb.act(sc[:], cond[:], AF.Silu, [CB_], [MB])
        cv.reset()
        wa = Rot([(cv.take([128, 8, 512], BF16), cv.mk("wa%d" % i)) for i in range(5)])
        scb = sb("scb", [128, 8, 2], BF16)
        kb.cp(scb[:], sc[:], [MB], [MB])
        for l in range(n_layers):
            wv = w_ada_d[l].rearrange("(kc p) n -> p kc n", p=128)
            for blk in range(18):
                wt, wb = wa.next()
                kb.ld(wt, wv[:, :, blk * 512:(blk + 1) * 512], [wb], eng="pool")
                for mi in range(4):
                    m = blk * 4 + mi
                    for kc in range(8):
                        kb.mm(ps[6][:, 2 * m:2 * m + 2], wt[:, kc, mi * 128:(mi + 1) * 128], scb[:, kc, :], kc == 0, kc == 7,
                              [wb, MB], [pB[6]], sig=(kc == 7))
            kb.tt(mod[:, l], ps[6][:, 0:144].rearrange("p (m j) -> p m j", j=2),
                  pvl(l, PV_BADA, 72).unsqueeze(2).to_broadcast([128, 72, 2]), ALU.add, [pB[6], CB_], [MB])
            for i in range(3):
                kb.stt(Amod[:, l, i], mod[:, l, (3 * i + 1) * 8:(3 * i + 2) * 8, :], 1.0,
                       pvl(l, PV_NG + 8 * i, 8).unsqueeze(2).to_broadcast([128, 8, 2]), ALU.add, ALU.mult, [MB, CB_], [MB])
                kb.ts(Gmod[:, l, i], mod[:, l, (3 * i + 2) * 8:(3 * i + 3) * 8, :], 0.5, None, ALU.mult, None, [MB], [MB])

        CND = [0, 1, 1]
        DBGB = Buf("dbg")

        def tap(idx, ap, reads, n):
            if dbg:
                kb.dma("sp", lambda e: e.dma_start(out=dbg_d[idx, 0:ap.shape[0], 0:n], in_=ap), reads, (), sbuf=DBGB)

        tap(0, mod[:, 0].rearrange("p m j -> p (m j)"), [MB], 144)
        tap(1, Amod[:, 0].rearrange("p i c j -> p (i c j)"), [MB], 48)
        tap(2, sc[:].rearrange("p c j -> p (c j)"), [MB], 16)

        def rstd_from_ps(pst, pbuf, n, dfeat, out_ap, outbuf, tmp_ap, tmpbuf):
            kb.ts(tmp_ap, pst, 1.0 / dfeat, EPS, ALU.mult, ALU.add, [pbuf], [tmpbuf])
            kb.act(tmp_ap, tmp_ap, AF.Ln, [tmpbuf], [tmpbuf])
            kb.act(out_ap, tmp_ap, AF.Exp, [tmpbuf], [outbuf], scale=-0.5)

        def norm_to_h(l, i, mts, sq, sqB, rs, rsB, tmp, tmpB):
            for mt in mts:
                cs = slice(mt * 512, (mt + 1) * 512)
                for kc in range(8):
                    kb.act(sq[:, kc, :], xT[:, kc, cs], AF.Square, [xB[mt]], [sqB])
                for kc in range(8):
                    kb.mm(ps[6][:], onesb[:], sq[:, kc, :], kc == 0, kc == 7, [sqB, CB_], [pB[6]], sig=(kc == 7))
                rstd_from_ps(ps[6][:], pB[6], 512, D, rs, rsB, tmp, tmpB)
                if l == 0 and i == 0:
                    tap(3, rs, [rsB], 512) if mt == 0 else None
                    tap(4, rs, [rsB], 512) if mt == 1 else None
                for kc in range(8):
                    kb.stt(tmp, xT[:, kc, cs], Amod[:, l, i, kc, CND[mt]:CND[mt] + 1], rs, ALU.mult, ALU.mult, [xB[mt], MB, rsB], [tmpB])
                    kb.act(hT[:, kc, cs], tmp, AF.Identity, [tmpB, MB], [hB[mt]], bias=mod[:, l, 3 * i * 8 + kc, CND[mt]:CND[mt] + 1])

        def wload(srcs, shape):
            wt, wb = wslots.next()
            n = int(np.prod(shape[1:]))
            names = " ".join("a%d" % i for i in range(len(shape) - 1))
            kw = {"a%d" % i: shape[i + 1] for i in range(len(shape) - 1)}
            view = wt[:, 0:n].rearrange("p (%s) -> p %s" % (names, names), **kw) if len(shape) > 2 else wt[:, 0:n]
            for (fn, src) in srcs:
                kb.dma("pool", (lambda e, o=fn(view), s=src: e.dma_start(out=o, in_=s)), (), [wb])
            return view, wb

        def ffn(l, which):
            cv.reset()
            sq = cv.take([128, 8, 512], BF16); sqB = cv.mk("sq")
            rs = cv.take([128, 512]); rsB = cv.mk("rs")
            tmp = cv.take([128, 512]); tmpB = cv.mk("tmp")
            norm_to_h(l, 0 if which == 0 else 2, [0, 1, 2], sq, sqB, rs, rsB, tmp, tmpB)
            chk("ffn%d_norm" % which)
            cv.reset()
            aT = cv.take([128, 12, NT], BF16)
            aB = cv.mks("a", 12)
            sgs = Rot([(cv.take([128, NT]), cv.mk("sg%d" % i)) for i in range(2)])
            w1 = ffn_in_d[l, which].rearrange("(kc p) n -> p kc n", p=128)
            w2 = ffn_out_d[l, which].rearrange("(kc p) n -> p kc n", p=128)
            gi = 0 if which == 0 else 2
            pset = Rot([(0, 1, 2), (3, 4, 5)])
            for (p0, p1) in ((0, 6), (6, 11)):
                nk = (p1 - p0) * 2
                for pr in range(p0, p1):
                    wv, wb = wload([(lambda v: v[:, :, 0, :], w1[:, :, pr * 256:(pr + 1) * 256]),
                                    (lambda v: v[:, :, 1, :], w1[:, :, DFF + pr * 256:DFF + (pr + 1) * 256])], [128, 8, 2, 256])
                    for ci in range(2):
                        j = (pr - p0) * 2 + ci
                        sg, sgB = sgs.next()
                        bg = pset.next()
                        for kc in range(8):
                            for mt in range(3):
                                kb.mm(ps[bg[mt]][:], wv[:, kc, 0, ci * 128:(ci + 1) * 128], hT[:, kc, mt * 512:(mt + 1) * 512],
                                      kc == 0, kc == 7, [wb, hB[mt]], [pB[bg[mt]]], sig=(kc == 7 and mt == 2))
                        for mt in range(3):
                            kb.act(sg[:, mt * 512:(mt + 1) * 512], ps[bg[mt]][:], AF.Silu, [pB[bg[mt]]], [sgB])
                        bu = pset.next()
                        for kc in range(8):
                            for mt in range(3):
                                kb.mm(ps[bu[mt]][:], wv[:, kc, 1, ci * 128:(ci + 1) * 128], hT[:, kc, mt * 512:(mt + 1) * 512],
                                      kc == 0, kc == 7, [wb, hB[mt]], [pB[bu[mt]]], sig=(kc == 7 and mt == 2))
                        for mt in range(3):
                            kb.tt(aT[:, j, mt * 512:(mt + 1) * 512], ps[bu[mt]][:], sg[:, mt * 512:(mt + 1) * 512], ALU.mult,
                                  [pB[bu[mt]], sgB], [aB[j]])
                chk("ffn%d_in%d" % (which, p0))
                for mp in range(4):
                    wv, wb = wload([(lambda v: v, w2[:, p0 * 2:p0 * 2 + nk, mp * 256:(mp + 1) * 256])], [128, nk, 256])
                    for mi in range(2):
                        m = mp * 2 + mi
                        bo = pset.next()
                        for kc in range(nk):
                            for mt in range(3):
                                kb.mm(ps[bo[mt]][:], wv[:, kc, mi * 128:(mi + 1) * 128], aT[:, kc, mt * 512:(mt + 1) * 512],
                                      kc == 0, kc == nk - 1, [wb, aB[kc]], [pB[bo[mt]]], sig=(kc == nk - 1 and mt == 2))
                        for mt in range(3):
                            cs = slice(mt * 512, (mt + 1) * 512)
                            kb.stt(xT[:, m, cs], ps[bo[mt]][:], Gmod[:, l, gi, m, CND[mt]:CND[mt] + 1], xT[:, m, cs], ALU.mult, ALU.add,
                                   [pB[bo[mt]], MB, xB[mt]], [xB[mt]])


        sel = sb("sel", [32, 4096])
        sgn = sb("sgn", [32, 4])
        rmask = sb("rmask", [32, 1024])
        maskf = sb("maskf", [128, 512], BF16); maskb = sb("maskb", [128, 512], BF16)
        perm = sb("perm", [128, 128], BF16)
        lamt = sb("lamt", [128, DEPTH, 4])
        for (t_, n_) in ((sel, "sel"), (sgn, "sgn"), (rmask, "rmask"), (maskf, "maskf"), (maskb, "maskb"), (perm, "perm")):
            kb.ld(t_[:], cdram[n_], [CB_])
        LAM_INIT = [0.8 - 0.6 * math.exp(-0.3 * l_) for l_ in range(DEPTH)]
        GROUPS = [([0], [(0, 256), (256, 256)]), ([1, 2], [(0, 1024)])]
        pcur = [0]
        OUTB = []

        def pbank():
            b = pcur[0] % 6
            pcur[0] += 1
            return b

        def mixer(l):
            W = w_in_d[l].rearrange("(kc p) n -> p kc n", p=128)
            cv.reset()
            lt = cv.take([128, 128]); ltB = cv.mk("lt")
            la = pv[:, l * PL + PV_LAM:l * PL + PV_LAM + 256].rearrange("p (a d) -> p a d", a=4)
            kb.tt(lt[:, 0:64], la[:, 0, :], la[:, 1, :], ALU.mult, [CB_], [ltB])
            kb.tt(lt[:, 64:128], la[:, 2, :], la[:, 3, :], ALU.mult, [CB_], [ltB])
            kb.op("dve", lambda e: e.reduce_sum(out=lamt[:, l, 0:2], in_=lt[:].rearrange("p (a d) -> p a d", a=2), axis=mybir.AxisListType.X), [ltB], [MB])
            kb.act(lamt[:, l, 0:2], lamt[:, l, 0:2], AF.Exp, [MB], [MB])
            kb.tt(lamt[:, l, 2:3], lamt[:, l, 0:1], lamt[:, l, 1:2], ALU.subtract, [MB], [MB])
            kb.ts(lamt[:, l, 2:3], lamt[:, l, 2:3], LAM_INIT[l], None, ALU.add, None, [MB], [MB])
            kb.ts(lamt[:, l, 3:4], lamt[:, l, 2:3], -1.0, None, ALU.mult, None, [MB], [MB])
            for gi, (mts, seqs) in enumerate(GROUPS):
                mixer_group(l, W, gi, mts, seqs)

        def mixer_group(l, W, gi, mts, seqs):
            g0 = mts[0] * 512
            T = len(mts) * 512
            nmt = len(mts)
            cnd = CND[mts[0]]
            cv.reset()
            sq = cv.take([128, 8, 512], BF16); sqB = cv.mk("sq")
            rs = cv.take([128, 512]); rsB = cv.mk("rs")
            tmp = cv.take([128, 512]); tmpB = cv.mk("tmp")
            norm_to_h(l, 1, mts, sq, sqB, rs, rsB, tmp, tmpB)
            cv.reset()
            hBs = [hB[mt] for mt in mts]

            def proj(srcs, shape, lhs_fn, M, evac):
                wv, wb = wload(srcs, shape)
                for mi, mt in enumerate(mts):
                    b = pbank()
                    for kc in range(8):
                        kb.mm(ps[b][0:M, :], lhs_fn(wv, kc), hT[:, kc, mt * 512:(mt + 1) * 512], kc == 0, kc == 7, [wb, hB[mt]], [pB[b]], sig=(kc == 7))
                    evac(mi, ps[b][0:M, :], pB[b])

            yT = cv.take([128, 8, T], BF16); yB = cv.mks("y", nmt)
            YEND = cv.off
            if 'ssd' in SKIP:
                for mi in range(nmt):
                    kb.memset(yT[:, :, mi * 512:(mi + 1) * 512], 0.0, [yB[mi]])
            dtT = cv.take([32, T]); aT_ = cv.take([32, T]); cumT = cv.take([32, T]); fmB = cv.mk("fm")
            SSD0 = cv.off
            t1 = cv.take([32, T]); t2 = cv.take([32, T]); t12B = cv.mk("t12")
            ac = cv.take([32, 2]); acB = cv.mk("ac")
            kb.act(ac[:, 0:1], pv[0:32, l * PL + PV_ALOG:l * PL + PV_ALOG + 1], AF.Exp, [CB_], [acB])
            kb.ts(ac[:, 1:2], ac[:, 0:1], -1.0, None, ALU.mult, None, [acB], [acB])

            def ev_dt(mi, pst, pb):
                kb.act(t1[:, mi * 512:(mi + 1) * 512], pst, AF.Identity, [pb, CB_], [t12B], bias=pv[0:32, l * PL + PV_DTB:l * PL + PV_DTB + 1])
            proj([(lambda v: v, W[:, :, OFF_DT:OFF_DT + 32])], [128, 8, 32], lambda wv, kc: wv[:, kc, :], 32, ev_dt)
            kb.ts(t2[:], t1[:], -1.0, None, ALU.mult, None, [t12B], [t12B])
            kb.tt(t2[:], t2[:], t1[:], ALU.max, [t12B], [t12B])
            kb.act(t2[:], t2[:], AF.Exp, [t12B], [t12B], scale=-1.0)
            kb.act(t2[:], t2[:], AF.Ln, [t12B], [t12B], bias=1.0)
            kb.ts(t1[:], t1[:], 0.0, None, ALU.max, None, [t12B], [t12B])
            kb.tt(dtT[:], t1[:], t2[:], ALU.add, [t12B], [fmB])
            kb.ts(aT_[:], dtT[:], ac[:, 1:2], None, ALU.mult, None, [fmB, acB], [fmB])
            for c0 in range(0, T, 1024):
                n_ = min(1024, T - c0)
                kb.op("dve", lambda e, c0=c0, n_=n_: e.tensor_tensor_scan(out=cumT[:, c0:c0 + n_], data0=rmask[:, 0:n_], data1=aT_[:, c0:c0 + n_],
                                                                      initial=0.0, op0=ALU.mult, op1=ALU.add), [fmB, CB_], [fmB])

            nseq = len(seqs)
            PW = T + 4 * nseq
            for j in range(0 if 'ssd' in SKIP else 2):
                cv.reset(SSD0)
                xbc = cv.take([128, 6, T], BF16); xbB = cv.mk("xbc")
                PC0 = cv.off
                pcs = Rot([(cv.take([128, PW]), cv.mk("pc%d" % i)) for i in range(1)])
                accs = Rot([(cv.take([128, PW]), cv.mk("acc%d" % i)) for i in range(1)])
                for (pc_, pcb_) in pcs.items:
                    kb.memset(pc_[:], 0.0, [pcb_])
                chunk_ids = [4 * j, 4 * j + 1, 4 * j + 2, 4 * j + 3, 8 + j, 10 + j]
                for ci, c in enumerate(chunk_ids):
                    pc_, pcb_ = pcs.next()
                    acc, accB = accs.next()

                    def ev_pc(mi, pst, pb, pc_=pc_, pcb_=pcb_):
                        for k, (o, L) in enumerate(seqs):
                            lo = max(o, mi * 512); hi = min(o + L, (mi + 1) * 512)
                            if lo < hi:
                                kb.cp(pc_[:, 2 + lo + 4 * k:2 + hi + 4 * k], pst[:, lo - mi * 512:hi - mi * 512], [pb], [pcb_], eng="act")
                    proj([(lambda v: v, W[:, :, OFF_XBC + c * 128:OFF_XBC + (c + 1) * 128])], [128, 8, 128], lambda wv, kc: wv[:, kc, :], 128, ev_pc)
                    n_ = PW - 4
                    cw = pv[:, l * PL + PV_CW + c * 5:l * PL + PV_CW + c * 5 + 5]
                    kb.ts(acc[:, 0:n_], pc_[:, 0:n_], cw[:, 0:1], None, ALU.mult, None, [pcb_, CB_], [accB])
                    for tp in range(1, 5):
                        kb.stt(acc[:, 0:n_], pc_[:, tp:tp + n_], cw[:, tp:tp + 1], acc[:, 0:n_], ALU.mult, ALU.add, [pcb_, CB_, accB], [accB])
                    for k, (o, L) in enumerate(seqs):
                        kb.act(xbc[:, ci, o:o + L], acc[:, o + 4 * k:o + 4 * k + L], AF.Silu, [accB, CB_], [xbB],
                               bias=pv[:, l * PL + PV_CB + c:l * PL + PV_CB + c + 1])
                cv.reset(PC0)
                Hf = cv.take([128, 512]); Hb = cv.take([128, 512]); Hfb = cv.take([128, 512], BF16); HB_ = cv.mk("H")
                ntmax = max(L for (_, L) in seqs) // 128
                Hbin = cv.take([128, ntmax, 512], BF16); HbinB = cv.mk("Hbin")
                sets = []
                for si in range(2):
                    S = {}
                    S["tok"] = cv.take([128, 96]); S["arg"] = cv.take([128, 32]); S["dte"] = cv.take([128, 32]); S["cd"] = cv.take([128, 32]); S["s2"] = cv.take([128, 32]); S["ct"] = cv.take([128, 16]); S["tokB"] = cv.mk("tok%d" % si)
                    S["btok"] = cv.take([128, 128], BF16); S["btB"] = cv.mk("btok%d" % si)
                    S["xdf"] = cv.take([128, 512], BF16); S["xdb"] = cv.take([128, 512], BF16); S["xdd"] = cv.take([128, 512], BF16); S["xdB"] = cv.mk("xd%d" % si)
                    S["Rt"] = cv.take([32, 128]); S["Cn"] = cv.take([32, 128]); S["EL"] = cv.take([32, 128]); S["tb_"] = cv.take([32, 128]); S["rB"] = cv.mk("R%d" % si)
                    S["ty"] = cv.take([128, 4, 128]); S["tyB"] = cv.mk("ty%d" % si)
                    sets.append(S)
                par = [0]
                ct = None
                tok = arg = dte = cd = s2 = tokB = btok = btB = xdf = xdb = xdd = xdB = Rt = Cn = EL = tb_ = rB = ty = tyB = None

                def nextset():
                    nonlocal tok, arg, dte, cd, s2, tokB, btok, btB, xdf, xdb, xdd, xdB, Rt, Cn, EL, tb_, rB, ty, tyB, ct
                    S = sets[par[0] % 2]
                    par[0] += 1
                    tok, arg, dte, cd, s2, tokB = S["tok"], S["arg"], S["dte"], S["cd"], S["s2"], S["tokB"]
                    ct = S["ct"]
                    btok, btB = S["btok"], S["btB"]
                    xdf, xdb, xdd, xdB = S["xdf"], S["xdb"], S["xdd"], S["xdB"]
                    Rt, Cn, EL, tb_, rB = S["Rt"], S["Cn"], S["EL"], S["tb_"], S["rB"]
                    ty, tyB = S["ty"], S["tyB"]

                cbsR = Rot([(cv.take([128, 128]), cv.mk("cb%d" % i)) for i in range(2)])
                decs = Rot([(cv.take([128, 512]), cv.mk("dec%d" % i)) for i in range(2)])
                scs = [(cv.take([128, 4, 128], BF16), cv.mk("sc%d" % i)) for i in range(2)]
                ebcR = Rot([(cv.take([128, 512]), cv.mk("ebc%d" % i)) for i in range(2)])
                ces = [(cv.take([128, 4, 128], BF16), cv.mk("ce%d" % i)) for i in range(2)]
                segb = Rot([3, 6])
                PS_T, PS_S, PS_CB, PS_SEG, PS_E, PS_Y = 0, 1, 2, 3, 4, 5

                for k, (o, L) in enumerate(seqs):
                    nt = L // 128
                    kb.memset(Hf[:], 0.0, [HB_]); kb.memset(Hb[:], 0.0, [HB_])
                    if gi == 1:
                        for d_, H_ in ((0, Hf), (1, Hb)):
                            for gg in range(2):
                                kb.ld(H_[gg * 64:(gg + 1) * 64, gg * 256:(gg + 1) * 256],
                                      st0_d[l, d_][:, (8 * j + 4 * gg) * 64:(8 * j + 4 * gg + 4) * 64], [HB_])
                    kb.cp(Hfb[:], Hf[:], [HB_], [HB_])

                    def prep(i):
                        tsl = slice(o + i * 128, o + (i + 1) * 128)
                        kb.tr(ps[PS_T][:, 0:32], dtT[:, tsl], identf[0:32, 0:32], [fmB, CB_], [pB[PS_T]], sig=False)
                        kb.tr(ps[PS_T][:, 32:64], aT_[:, tsl], identf[0:32, 0:32], [fmB, CB_], [pB[PS_T]], sig=False)
                        kb.tr(ps[PS_T][:, 64:96], cumT[:, tsl], identf[0:32, 0:32], [fmB, CB_], [pB[PS_T]], sig=True)
                        kb.cp(tok[:], ps[PS_T][:, 0:96], [pB[PS_T]], [tokB], eng="act")
                        kb.mm(ps[PS_T][:, 96:128], onesf[:], tok[:, 32:64], True, True, [tokB, CB_], [pB[PS_T]], sig=True)
                        kb.tt(arg[:, 0:16], ps[PS_T][:, 96:112], tok[:, 64:80], ALU.subtract, [pB[PS_T], tokB], [tokB])
                        kb.tt(arg[:, 16:32], tok[:, 80:96], tok[:, 48:64], ALU.subtract, [tokB], [tokB])
                        kb.ts(ct[:], tok[:, 64:80], -1.0, None, ALU.mult, None, [tokB], [tokB])
                        kb.act(dte[:], arg[:], AF.Exp, [tokB], [tokB])
                        kb.act(cd[:], ps[PS_T][:, 96:128], AF.Exp, [pB[PS_T]], [tokB])
                        kb.tt(s2[:], tok[:, 0:32], dte[:], ALU.mult, [tokB], [tokB])
                        for ci in range(4):
                            kb.tr(psb[:, ci * 128:(ci + 1) * 128], xbc[:, ci, tsl], identb[:], [xbB, CB_], [psbB], sig=False)
                        kb.tr(psb[:, 512:640], xbc[:, 4, tsl], identb[:], [xbB, CB_], [psbB], sig=True)
                        kb.cp(btok[:], psb[:, 512:640], [psbB], [btB])
                        return tsl

                    def xprod(out, col0):
                        kb.tt(out[:].rearrange("p (h d) -> p h d", h=8), psb[:, 0:512].rearrange("p (h d) -> p h d", h=8),
                              col0.unsqueeze(2).to_broadcast([128, 8, 64]), ALU.mult, [psbB, tokB], [xdB])

                    def state_update(H_, xsrc, cdcol):
                        kb.mm(ps[PS_S][:], btok[:], xsrc[:], True, True, [btB, xdB], [pB[PS_S]], sig=True)
                        kb.tt(H_[:].rearrange("p (h d) -> p h d", h=8), H_[:].rearrange("p (h d) -> p h d", h=8),
                              cdcol.unsqueeze(2).to_broadcast([128, 8, 64]), ALU.mult, [HB_, tokB], [HB_])
                        kb.tt(H_[:], H_[:], ps[PS_S][:], ALU.add, [HB_, pB[PS_S]], [HB_])

                    for i in range(nt - 1, -1, -1):
                        nextset()
                        prep(i)
                        kb.cp(Hbin[:, i, :], Hb[:], [HB_], [HbinB])
                        xprod(xdd, s2[:, 16 + 8 * j:16 + 8 * j + 8])
                        state_update(Hb, xdd, cd[:, 16 + 8 * j:16 + 8 * j + 8])
                    for i in range(nt):
                        nextset()
                        tsl = prep(i)
                        xprod(xdf, tok[:, 8 * j:8 * j + 8])
                        xprod(xdb, tok[:, 16 + 8 * j:16 + 8 * j + 8])
                        xprod(xdd, s2[:, 8 * j:8 * j + 8])
                        kb.ts(tb_[:], aT_[:, tsl], sgn[:, 1:2], None, ALU.mult, None, [fmB, CB_], [rB])
                        kb.stt(Rt[:], cumT[:, tsl], sgn[:, 0:1], tb_[:], ALU.mult, ALU.add, [fmB, CB_, rB], [rB])
                        last = o + i * 128 + 127
                        kb.stt(EL[:], cumT[:, last:last + 1].to_broadcast([32, 128]), sgn[:, 1:2], Rt[:], ALU.mult, ALU.add, [fmB, CB_, rB], [rB])
                        for gg in range(2):
                            r0 = gg * 64
                            cbs, cbB = cbsR.next()
                            kb.mm(ps[PS_CB][:, 0:128], xbc[r0:r0 + 64, 4, tsl], xbc[r0:r0 + 64, 5, tsl], True, True, [xbB], [pB[PS_CB]], sig=True)
                            kb.cp(cbs[:], ps[PS_CB][:, 0:128], [pB[PS_CB]], [cbB], eng="act")
                            for d_ in range(2):
                                msk = maskf if d_ == 0 else maskb
                                sc_, scB = scs[d_]
                                ce_, ceB = ces[d_]
                                dec, decB = decs.next()
                                PS_SEG = segb.next()
                                ebc, ebB = ebcR.next()
                                kb.mm(ps[PS_SEG][:], identb[:], msk[:], True, False, [CB_], [pB[PS_SEG]], sig=False)
                                for hh in range(4):
                                    dh = d_ * 16 + 8 * j + 4 * gg + hh
                                    kb.mm(ps[PS_SEG][:, hh * 128:(hh + 1) * 128], sel[:, dh * 128:(dh + 1) * 128], Rt[:], False, hh == 3, [CB_, rB], [pB[PS_SEG]], sig=(hh == 3))
                                for hh in range(4):
                                    hx = 8 * j + 4 * gg + hh
                                    bcol = ct[:, hx:hx + 1] if d_ == 0 else arg[:, 16 + hx:16 + hx + 1]
                                    kb.act(dec[:, hh * 128:(hh + 1) * 128], ps[PS_SEG][:, hh * 128:(hh + 1) * 128], AF.Exp, [pB[PS_SEG], tokB], [decB], bias=bcol)
                                kb.tt(sc_[:], dec[:].rearrange("p (h s) -> p h s", h=4), cbs[:].unsqueeze(1).to_broadcast([128, 4, 128]), ALU.mult, [decB, cbB], [scB])
                                for hh in range(4):
                                    dh = d_ * 16 + 8 * j + 4 * gg + hh
                                    kb.mm(ps[PS_E][:, hh * 128:(hh + 1) * 128], sel[:, dh * 128:(dh + 1) * 128], EL[:], True, True, [CB_, rB], [pB[PS_E]], sig=(hh == 3))
                                kb.act(ebc[:], ps[PS_E][:], AF.Exp, [pB[PS_E]], [ebB])
                                kb.tt(ce_[r0:r0 + 64], ebc[r0:r0 + 64, :].rearrange("p (h s) -> p h s", h=4),
                                      xbc[r0:r0 + 64, 5, tsl].unsqueeze(1).to_broadcast([64, 4, 128]), ALU.mult, [ebB, xbB], [ceB])
                            for hh in range(4):
                                hl = gg * 4 + hh
                                cl = hl // 2
                                yr = (hl % 2) * 64
                                out = ps[PS_Y][yr:yr + 64, cl * 128:(cl + 1) * 128]
                                hs = slice(hl * 64, (hl + 1) * 64)
                                kb.mm(out, xdf[:, hs], scs[0][0][:, hh, :], True, False, [xdB, scs[0][1]], [pB[PS_Y]], sig=False)
                                kb.mm(out, xdb[:, hs], scs[1][0][:, hh, :], False, False, [xdB, scs[1][1]], [pB[PS_Y]], sig=False)
                                kb.mm(out, Hfb[r0:r0 + 64, hs], ces[0][0][r0:r0 + 64, hh, :], False, False, [HB_, ces[0][1]], [pB[PS_Y]], sig=False)
                                kb.mm(out, Hbin[r0:r0 + 64, i, hs], ces[1][0][r0:r0 + 64, hh, :], False, True, [HbinB, ces[1][1]], [pB[PS_Y]], sig=True)
                        dv = pv[:, l * PL + PV_DV + 4 * j:l * PL + PV_DV + 4 * j + 4]
                        kb.tt(ty[:], xbc[:, 0:4, tsl], dv.unsqueeze(2).to_broadcast([128, 4, 128]), ALU.mult, [xbB, CB_], [tyB])
                        kb.tt(yT[:, 4 * j:4 * j + 4, tsl], ty[:], ps[PS_Y][:].rearrange("p (c s) -> p c s", c=4), ALU.add, [tyB, pB[PS_Y]], yB)
                        state_update(Hf, xdd, cd[:, 8 * j:8 * j + 8])
                        kb.cp(Hfb[:], Hf[:], [HB_], [HB_])
                    if gi == 0:
                        for d_, H_ in ((0, Hf), (1, Hb)):
                            for gg in range(2):
                                kb.st(ns_d[k, l, d_][:, (8 * j + 4 * gg) * 64:(8 * j + 4 * gg + 4) * 64],
                                      H_[gg * 64:(gg + 1) * 64, gg * 256:(gg + 1) * 256], [HB_])
                        kb.final_wait("dve", [HB_])
                        OUTB.append(HB_)

            cv.reset(YEND)
            merged = cv.take([128, 8, T], BF16); mgB = cv.mks("mg", nmt)
            mergedb = merged; mgbB = mgB
            tgs = Rot([(cv.take([128, 512]), cv.mk("tg%d" % i)) for i in range(2)])
            tms = Rot([(cv.take([128, 512]), cv.mk("tm%d" % i)) for i in range(2)])
            MRG2 = cv.off
            szs = Rot([(cv.take([128, T]), cv.mk("sz%d" % i)) for i in range(2)])
            for c in range(8):
                sz, szB = szs.next()

                def ev_z(mi, pst, pb, sz=sz, szB=szB):
                    kb.act(sz[:, mi * 512:(mi + 1) * 512], pst, AF.Silu, [pb], [szB])
                proj([(lambda v: v, W[:, :, OFF_Z + c * 128:OFF_Z + (c + 1) * 128])], [128, 8, 128], lambda wv, kc: wv[:, kc, :], 128, ev_z)
                kb.tt(yT[:, c, :], yT[:, c, :], sz[:], ALU.mult, yB + [szB], yB)
            sq = cv.take([128, 8, 512], BF16); sqB = cv.mk("sq")
            rs = cv.take([128, 512]); rsB = cv.mk("rs")
            tmp = cv.take([128, 512]); tmpB = cv.mk("tmp")
            for mi in range(nmt):
                cs = slice(mi * 512, (mi + 1) * 512)
                for kc in range(8):
                    kb.act(sq[:, kc, :], yT[:, kc, cs], AF.Square, yB, [sqB])
                for kc in range(8):
                    kb.mm(ps[6][:], onesb[:], sq[:, kc, :], kc == 0, kc == 7, [sqB, CB_], [pB[6]], sig=(kc == 7))
                rstd_from_ps(ps[6][:], pB[6], 512, D, rs, rsB, tmp, tmpB)
                for kc in range(8):
                    kb.stt(yT[:, kc, cs], yT[:, kc, cs], pv[:, l * PL + PV_SNG + kc:l * PL + PV_SNG + kc + 1], rs, ALU.mult, ALU.mult, yB + [CB_, rsB], yB)


            def merge(n):
                for m in range(8):
                    wv, wb = wload([(lambda v: v[:, :, 0, :], wbr_d[l, n].rearrange("(kc p) n -> p kc n", p=128)[:, :, m * 128:(m + 1) * 128]),
                                    (lambda v: v[:, :, 1, :], W[:, :, OFF_G + n * 1024 + m * 128:OFF_G + n * 1024 + (m + 1) * 128])], [128, 8, 2, 128])
                    for mi, mt in enumerate(mts):
                        cs = slice(mi * 512, (mi + 1) * 512)
                        bp = pbank(); bq = pbank()
                        for kc in range(8):
                            kb.mm(ps[bp][:], wv[:, kc, 0, :], yT[:, kc, cs], kc == 0, kc == 7, [wb] + yB, [pB[bp]], sig=(kc == 7))
                        for kc in range(8):
                            kb.mm(ps[bq][:], wv[:, kc, 1, :], hT[:, kc, mt * 512:(mt + 1) * 512], kc == 0, kc == 7, [wb, hB[mt]], [pB[bq]], sig=(kc == 7))
                        tg, tgB = tgs.next()
                        kb.act(tg[:], ps[bq][:], AF.Tanh, [pB[bq]], [tgB], scale=0.5)
                        if n == 0:
                            kb.stt(merged[:, m, cs], tg[:], 1.0, ps[bp][:], ALU.add, ALU.mult, [tgB, pB[bp]], [mgB[mi]])
                        else:
                            tm, tmB = tms.next()
                            kb.stt(tm[:], tg[:], 1.0, ps[bp][:], ALU.add, ALU.mult, [tgB, pB[bp]], [tmB])
                            if n == 1:
                                kb.tt(merged[:, m, cs], merged[:, m, cs], tm[:], ALU.add, [mgB[mi], tmB], [mgB[mi]])
                            else:
                                kb.tt(mergedb[:, m, cs], merged[:, m, cs], tm[:], ALU.add, [mgB[mi], tmB], [mgbB[mi]])

            if 'ssd' in SKIP:
                for mi in range(nmt):
                    kb.memset(yT[:, :, mi * 512:(mi + 1) * 512], 0.0, [yB[mi]])
            merge(0)

            cv.reset(MRG2)
            Tk = T + (512 if gi == 1 else 0)
            ntk = Tk // 128
            NQB = T // 256
            qz = cv.take([128, NQB, 2, 256], BF16); qB = cv.mk("qz")
            kb.memset(qz[0:64, :, 1, :], 0.0, [qB])
            kb.memset(qz[64:128, :, 0, :], 0.0, [qB])
            kh = cv.take([128, Tk], BF16); kB_ = cv.mk("kh")
            vh = cv.take([128, ntk, 128], BF16); vB = cv.mk("vh")
            stg = cv.take([128, 4, 128]); stgB = cv.mk("stg")
            OUTB.append(stgB)
            qraw = cv.take([128, 512], BF16); qrB = cv.mk("qraw")
            r1 = cv.take([128, 512]); r2 = cv.take([128, 512]); rrB = cv.mk("rr")
            Pt = Rot([(cv.take([128, 2, 256], BF16), cv.mk("P%d" % i)) for i in range(3)])
            rden = cv.take([128, 2, 256]); on_ = cv.take([128, 2, 256]); od = cv.take([128, 256]); odsq = cv.take([128, 256], BF16); nB = cv.mk("nrm")
            rs2 = cv.take([128, 256]); tmp2 = cv.take([128, 256]); rs2B = cv.mk("rs2")
            gn = cv.take([128, 2]); gnB = cv.mk("gn")
            kb.ts(gn[:, 0:1], pv[:, l * PL + PV_DNG:l * PL + PV_DNG + 1], 1.0 - LAM_INIT[l], None, ALU.mult, None, [CB_], [gnB])
            if gi == 1:
                cos = cv.take([128, 1024]); sin = cv.take([128, 1024]); csB = cv.mk("cs")
                kb.ld(cos[:], cdram["cos"], [csB]); kb.ld(sin[:], cdram["sin"], [csB])
            PSC = [0, 1]
            qbc = [0]

            chk("att%d_pre" % gi)
            for h in range(0 if ('att' in SKIP or ('att%d' % gi) in SKIP) else 8):
                wv, wb = wload([(lambda v: v[:, :, 0, :], W[:, :, OFF_Q + h * 128:OFF_Q + (h + 1) * 128]),
                                (lambda v: v[:, :, 1, :], W[:, :, OFF_K + h * 128:OFF_K + (h + 1) * 128]),
                                (lambda v: v[:, :, 2, :], W[:, :, OFF_V + h * 128:OFF_V + (h + 1) * 128])], [128, 8, 3, 128])
                koff = Tk - T
                if gi == 1:
                    kb.dma("pool", lambda e, h=h: e.dma_start(out=kh[:, 0:512], in_=ckT_d[l, h]), (), [kB_])
                    kb.dma("pool", lambda e, h=h: e.dma_start(out=vh[:, 0:4, :], in_=cv_d[l][:, h, :].rearrange("(t p) e -> p t e", p=128)), (), [vB])
                for which, dst, dB, off in ((0, None, qB, 0), (1, kh, kB_, koff)):
                    for mi, mt in enumerate(mts):
                        b = 4 + (pcur[0] % 2); pcur[0] += 1
                        for kc in range(8):
                            kb.mm(ps[b][:], wv[:, kc, which, :], hT[:, kc, mt * 512:(mt + 1) * 512], kc == 0, kc == 7, [wb, hB[mt]], [pB[b]], sig=(kc == 7))
                        dcs = slice(off + mi * 512, off + (mi + 1) * 512)
                        if gi == 0:
                            if which == 0:
                                for c_ in range(2):
                                    kb.cp(qz[c_ * 64:(c_ + 1) * 64, 2 * mi:2 * mi + 2, c_, :], ps[b][c_ * 64:(c_ + 1) * 64, :].rearrange("p (a q) -> p a q", a=2),
                                          [pB[b]], [dB], eng="act")
                            else:
                                kb.cp(dst[:, dcs], ps[b][:], [pB[b]], [dB], eng="act")
                        else:
                            tcs = slice(mi * 512, (mi + 1) * 512)
                            kb.cp(qraw[:], ps[b][:], [pB[b]], [qrB], eng="act")
                            kb.mm(ps[6][:], perm[:], qraw[:], True, True, [CB_, qrB], [pB[6]], sig=True)
                            kb.tt(r1[:], ps[b][:], cos[:, tcs], ALU.mult, [pB[b], csB, qrB], [rrB])
                            kb.tt(r2[:], ps[6][:], sin[:, tcs], ALU.mult, [pB[6], csB], [rrB])
                            if which == 0:
                                for c_ in range(2):
                                    rs_ = slice(c_ * 64, (c_ + 1) * 64)
                                    kb.tt(qz[rs_, 2 * mi:2 * mi + 2, c_, :], r1[rs_, :].rearrange("p (a q) -> p a q", a=2),
                                          r2[rs_, :].rearrange("p (a q) -> p a q", a=2), ALU.add, [rrB], [dB])
                            else:
                                kb.tt(dst[:, dcs], r1[:], r2[:], ALU.add, [rrB], [dB])
                chk("att%d_h%d_qk" % (gi, h))
                for t0 in range(0, 0 if 'nov' in SKIP else T // 128, 4):
                    b = 4 + (pcur[0] % 2); pcur[0] += 1
                    for tt_ in range(4):
                        tl = t0 + tt_
                        for kc in range(8):
                            kb.mm(ps[b][:, tt_ * 128:(tt_ + 1) * 128], hT[:, kc, g0 + tl * 128:g0 + (tl + 1) * 128], wv[:, kc, 2, :], kc == 0, kc == 7,
                                  [wb] + hBs, [pB[b]], sig=(kc == 7 and tt_ == 3))
                    if gi == 0 and 'nost' not in SKIP:
                        kb.cp(stg[:], ps[b][:].rearrange("p (t e) -> p t e", t=4), [pB[b]], [stgB])
                        kb.cp(vh[:, koff // 128 + t0:koff // 128 + t0 + 4, :], stg[:], [stgB], [vB], eng="act")
                    else:
                        kb.cp(vh[:, koff // 128 + t0:koff // 128 + t0 + 4, :], ps[b][:].rearrange("p (t e) -> p t e", t=4), [pB[b]], [vB], eng="act")
                    if gi == 0 and 'nost' not in SKIP:
                        for k in range(0 if 'nodma' in SKIP else 2):
                            kb.st(nv_d[k, l][:, h, :].rearrange("(t p) e -> p t e", p=128), stg[:, 2 * k:2 * k + 2, :], [stgB])
                        if 'nokst' in SKIP:
                            continue
                        b2 = 4 + (pcur[0] % 2); pcur[0] += 1
                        for tt_ in range(4):
                            tl = t0 + tt_
                            for kc in range(8):
                                kb.mm(ps[b2][:, tt_ * 128:(tt_ + 1) * 128], hT[:, kc, g0 + tl * 128:g0 + (tl + 1) * 128], wv[:, kc, 1, :], kc == 0, kc == 7,
                                      [wb] + hBs, [pB[b2]], sig=(kc == 7 and tt_ == 3))
                        kb.cp(stg[:], ps[b2][:].rearrange("p (t e) -> p t e", t=4), [pB[b2]], [stgB])
                        for k in range(0 if 'nodma' in SKIP else 2):
                            kb.st(nk_d[k, l][:, h, :].rearrange("(t p) e -> p t e", p=128), stg[:, 2 * k:2 * k + 2, :], [stgB])
                for k, (o, L) in enumerate(seqs if 'nocore' not in SKIP else []):
                    if gi == 0:
                        ktiles = list(range(o // 128, (o + L) // 128))
                    else:
                        ktiles = list(range(ntk))
                    for qb in range(L // 256):
                        qs = slice(o + qb * 256, o + (qb + 1) * 256)
                        gq = (o + qb * 256) // 256
                        PO, PD = ((2, 3), (4, 5))[qbc[0] % 2]
                        qbc[0] += 1
                        nk_ = len(ktiles)
                        Ps = {}

                        def score(ki):
                            kt = ktiles[ki]
                            bs = ki % 2
                            kb.mm(ps[bs][:], kh[:, kt * 128:(kt + 1) * 128], qz[:, gq, :, :].rearrange("p c q -> p (c q)"), True, True, [kB_, qB], [pB[bs]], sig=True)
                            P_, PB_ = Pt.next()
                            Ps[ki] = (P_, PB_)
                            kb.act(P_[:].rearrange("p c q -> p (c q)"), ps[bs][:], AF.Exp, [pB[bs]], [PB_], scale=0.125)

                        score(0)
                        for ki, kt in enumerate(ktiles):
                            if ki + 1 < nk_:
                                score(ki + 1)
                            P_, PB_ = Ps.pop(ki)
                            kb.mm(ps[PO][:], vh[:, kt, :], P_[:].rearrange("p c q -> p (c q)"), ki == 0, ki == nk_ - 1, [vB, PB_], [pB[PO]], sig=False)
                            kb.mm(ps[PD][:], onesb[:], P_[:].rearrange("p c q -> p (c q)"), ki == 0, ki == nk_ - 1, [CB_, PB_], [pB[PD]], sig=True)
                        if 'nonorm' in SKIP:
                            kb.cp(rden[:].rearrange("p c q -> p (c q)"), ps[PD][:], [pB[PD]], [nB])
                            kb.cp(on_[:].rearrange("p c q -> p (c q)"), ps[PO][:], [pB[PO]], [nB])
                            continue
                        kb.recip(rden[:].rearrange("p c q -> p (c q)"), ps[PD][:], [pB[PD]], [nB])
                        kb.tt(on_[:].rearrange("p c q -> p (c q)"), ps[PO][:], rden[:].rearrange("p c q -> p (c q)"), ALU.mult, [pB[PO], nB], [nB])
                        kb.stt(od[:], on_[:, 1, :], lamt[:, l, 3:4], on_[:, 0, :], ALU.mult, ALU.add, [nB, MB], [nB])
                        kb.act(odsq[:], od[:], AF.Square, [nB], [nB])
                        kb.mm(ps[6][:, 0:256], onesb[:], odsq[:], True, True, [CB_, nB], [pB[6]], sig=True)
                        rstd_from_ps(ps[6][:, 0:256], pB[6], 256, 128, rs2, rs2B, tmp2, rs2B)
                        kb.stt(yT[:, h, qs], od[:], gn[:, 0:1], rs2[:], ALU.mult, ALU.mult, [nB, gnB, rs2B], yB)
            if 'att' in SKIP:
                for mi in range(nmt):
                    kb.memset(yT[:, :, mi * 512:(mi + 1) * 512], 0.0, [yB[mi]])
            chk("g%d_att" % gi)
            merge(1)
            chk("g%d_m1" % gi)

            cv.reset(MRG2)
            PP = T + 16 * nseq
            pu = Rot([(cv.take([128, PP]), cv.mk("pu%d" % i)) for i in range(2)])
            for (p_, pb_) in pu.items:
                kb.memset(p_[:], 0.0, [pb_])
            lv = [(cv.take([128, PP]), cv.mk("lv%d" % i)) for i in range(2)]
            pooled = cv.take([128, 2, T], BF16); plB = cv.mk("pooled")
            invc = cv.take([128, 4, PP], BF16); ivB = cv.mk("invc")
            kb.ld(invc[:].rearrange("p g n -> p (g n)"), cdram["invc%d" % gi], [ivB])
            for g in range(0 if 'pool' in SKIP else 4):
                for ci in range(2):
                    c = 2 * g + ci
                    pu_, puB = pu.next()

                    def ev_u(mi, pst, pb, pu_=pu_, puB=puB):
                        for k, (o, L) in enumerate(seqs):
                            lo = max(o, mi * 512); hi = min(o + L, (mi + 1) * 512)
                            if lo < hi:
                                kb.cp(pu_[:, 8 + lo + 16 * k:8 + hi + 16 * k], pst[:, lo - mi * 512:hi - mi * 512], [pb], [puB], eng="act")
                    proj([(lambda v: v, W[:, :, OFF_U + c * 128:OFF_U + (c + 1) * 128])], [128, 8, 128], lambda wv, kc: wv[:, kc, :], 128, ev_u)
                    src, srcB = pu_, puB
                    A_, AB_ = lv[0]
                    kb.tt(A_[:, 1:PP], src[:, 0:PP - 1], src[:, 1:PP], ALU.add, [srcB], [AB_])
                    kb.memset(A_[:, 0:1], 0.0, [AB_])
                    cur, curB = A_, AB_
                    for step, sh in enumerate((1, 2, 4)[:g]):
                        nx, nxB = lv[(step + 1) % 2]
                        kb.memset(nx[:, 0:sh], 0.0, [nxB]); kb.memset(nx[:, PP - sh:PP], 0.0, [nxB])
                        kb.tt(nx[:, sh:PP - sh], cur[:, 0:PP - 2 * sh], cur[:, 2 * sh:PP], ALU.add, [curB], [nxB])
                        cur, curB = nx, nxB
                    oth, othB = lv[0] if cur is lv[1][0] else lv[1]
                    kb.tt(oth[:], cur[:], invc[:, g, :], ALU.mult, [curB, ivB], [othB])
                    for k, (o, L) in enumerate(seqs):
                        kb.tt(pooled[:, ci, o:o + L], oth[:, 8 + o + 16 * k:8 + o + 16 * k + L], pu_[:, 8 + o + 16 * k:8 + o + 16 * k + L], ALU.subtract, [othB, puB], [plB])
                wv, wb = wload([(lambda v: v, pmap_d[l, g].rearrange("(kc p) n -> p kc n", p=128))], [128, 2, 256])
                for e_ in range(2):
                    for mi in range(nmt):
                        b = pbank()
                        for kc in range(2):
                            kb.mm(ps[b][:], wv[:, kc, e_ * 128:(e_ + 1) * 128], pooled[:, kc, mi * 512:(mi + 1) * 512], kc == 0, kc == 1, [wb, plB], [pB[b]], sig=(kc == 1))
                        kb.ts(yT[:, 2 * g + e_, mi * 512:(mi + 1) * 512], ps[b][:], pv[:, l * PL + PV_PS + 2 * g + e_:l * PL + PV_PS + 2 * g + e_ + 1], None, ALU.mult, None,
                              [pB[b], CB_], yB)
            chk("g%d_pool" % gi)
            merge(2)
            chk("g%d_m2" % gi)

            wo = wo_d[l].rearrange("(kc p) n -> p kc n", p=128)
            for m in range(8):
                wv, wb = wload([(lambda v: v, wo[:, :, m * 128:(m + 1) * 128])], [128, 8, 128])
                for mi, mt in enumerate(mts):
                    b = pbank()
                    for kc in range(8):
                        kb.mm(ps[b][:], wv[:, kc, :], mergedb[:, kc, mi * 512:(mi + 1) * 512], kc == 0, kc == 7, [wb, mgbB[mi]], [pB[b]], sig=(kc == 7))
                    cs = slice(mt * 512, (mt + 1) * 512)
                    kb.stt(xT[:, m, cs], ps[b][:], Gmod[:, l, 1, m, cnd:cnd + 1], xT[:, m, cs], ALU.mult, ALU.add, [pB[b], MB, xB[mt]], [xB[mt]])
            chk("g%d_out" % gi)

        try:
            for l in range(n_layers):
                ffn(l, 0)
                if do_mixer:
                    mixer(l)
                ffn(l, 1)
        except StopBuild as ex:
            print("STOPPED at", ex)

        cv.reset()
        sq = cv.take([128, 8, 512], BF16); sqB = cv.mk("sq")
        rs = cv.take([128, 512]); rsB = cv.mk("rs")
        tmp = cv.take([128, 512]); tmpB = cv.mk("tmp")
        yo = cv.take([128, 8, NT]); yoB = cv.mks("yo", 3)
        fg = pv[:, DEPTH * PL:DEPTH * PL + 8]
        for mt in range(3):
            cs = slice(mt * 512, (mt + 1) * 512)
            for kc in range(8):
                kb.act(sq[:, kc, :], xT[:, kc, cs], AF.Square, [xB[mt]], [sqB])
            for kc in range(8):
                kb.mm(ps[6][:], onesb[:], sq[:, kc, :], kc == 0, kc == 7, [sqB, CB_], [pB[6]], sig=(kc == 7))
            rstd_from_ps(ps[6][:], pB[6], 512, D, rs, rsB, tmp, tmpB)
            for kc in range(8):
                kb.stt(yo[:, kc, cs], xT[:, kc, cs], fg[:, kc:kc + 1], rs, ALU.mult, ALU.mult, [xB[mt], CB_, rsB], [yoB[mt]])
            kb.st(yT_d.rearrange("(c p) t -> p c t", p=128)[:, :, cs], yo[:, :, cs], [yoB[mt]])
        if os.environ.get("KVERB"):
            print("phase peak bytes", cv.peak, "of", PH, "nins", kb.nins, {e: len(kb.prog[e]) for e in kb.ENGS})
        kb.final_wait("sp", yoB + [DBGB] + OUTB)
        kb.emit(st)
    return nc


_NC_CACHE = {}


def prep_inputs(inp):
    f = lambda a: np.ascontiguousarray(np.asarray(a, dtype=np.float32))
    consts = host_consts()
    shared = {"w_ada": f(inp["w_ada"]), "ffn_w_in": f(inp["ffn_w_in"]), "ffn_w_out": f(inp["ffn_w_out"]),
              "w_in": f(inp["w_in"]), "pool_map": f(inp["pool_map"]), "w_branch": f(inp["w_branch"]), "w_out": f(inp["w_out"])}
    for k, v in consts.items():
        shared["c_" + k] = np.ascontiguousarray(v)
    pvv = np.zeros((128, DEPTH * PL + 8), np.float32)
    for l in range(DEPTH):
        o = l * PL
        pvv[:, o + PV_BADA:o + PV_BADA + 72] = f(inp["b_ada"])[l].reshape(72, 128).T
        pvv[:, o + PV_NG:o + PV_NG + 24] = f(inp["norm_gain"])[l].reshape(24, 128).T
        cw = f(inp["ssd_conv_w"])[l].reshape(5, 12, 128)
        pvv[:, o + PV_CW:o + PV_CW + 60] = cw.transpose(2, 1, 0).reshape(128, 60)
        pvv[:, o + PV_CB:o + PV_CB + 12] = f(inp["ssd_conv_b"])[l].reshape(12, 128).T
        pvv[:, o + PV_SNG:o + PV_SNG + 8] = f(inp["ssd_norm_gain"])[l].reshape(8, 128).T
        pvv[:, o + PV_PS:o + PV_PS + 8] = f(inp["pool_scale"])[l].reshape(8, 128).T
        pvv[:, o + PV_DV:o + PV_DV + 8] = np.repeat(f(inp["ssd_d"])[l], 64).reshape(8, 128).T
        pvv[:, o + PV_DNG] = f(inp["diff_norm_gain"])[l]
        pvv[:, o + PV_LAM:o + PV_LAM + 256] = f(inp["diff_lambda"])[l].reshape(1, 256)
        pvv[:32, o + PV_DTB] = f(inp["ssd_dt_bias"])[l].reshape(32)
        pvv[:32, o + PV_ALOG] = f(inp["ssd_a_log"])[l].reshape(32)
    pvv[:, DEPTH * PL:] = f(inp["final_gain"]).reshape(8, 128).T
    shared["pv"] = pvv
    xp = f(inp["x_prompt"]); xs = f(inp["x_sample"])
    ck = f(inp["cache_k"]); cvv = f(inp["cache_v"]); s0 = f(inp["state_ssm"])
    cc = f(inp["c"]); cctx = f(inp["c_ctx"])
    in_maps = []
    for c in range(8):
        m = dict(shared)
        xt = np.concatenate([xp[2 * c], xp[2 * c + 1], xs[c]], axis=0)
        m["xT"] = np.ascontiguousarray(xt.T)
        cd = np.zeros((128, 8, 2), np.float32)
        cd[:, :, 0] = cctx.reshape(8, 128).T
        cd[:, :, 1] = cc[c].reshape(8, 128).T
        m["cond"] = cd.reshape(128, 16)
        m["ckT"] = np.ascontiguousarray(ck[c].transpose(0, 2, 3, 1))
        m["cv"] = np.ascontiguousarray(cvv[c])
        m["st0"] = np.ascontiguousarray(s0[c].transpose(0, 1, 4, 2, 3).reshape(DEPTH, 2, 64, 1024))
        in_maps.append(m)
    return in_maps


def kernel(**inputs):
    in_maps = prep_inputs(inputs)
    key = "full"
    if key not in _NC_CACHE:
        _NC_CACHE[key] = build()
    nc = _NC_CACHE[key]
    res = run_bass_kernel_spmd(nc, in_maps, core_ids=list(range(8)))
    y_prompt = np.zeros((16, 256, D), np.float32)
    y_sample = np.zeros((8, 1024, D), np.float32)
    nk = np.zeros((16, DEPTH, 256, 8, 128), np.float32)
    nv = np.zeros((16, DEPTH, 256, 8, 128), np.float32)
    ns = np.zeros((16, DEPTH, 2, 16, 64, 64), np.float32)
    for c in range(8):
        r = res.results[c]
        y = np.asarray(r["yT"]).T
        y_prompt[2 * c] = y[0:256]
        y_prompt[2 * c + 1] = y[256:512]
        y_sample[c] = y[512:1536]
        nk[2 * c:2 * c + 2] = np.asarray(r["nk"])
        nv[2 * c:2 * c + 2] = np.asarray(r["nv"])
        s = np.asarray(r["ns"]).reshape(2, DEPTH, 2, 64, 16, 64)
        ns[2 * c:2 * c + 2] = s.transpose(0, 1, 2, 4, 5, 3)
    return (y_prompt, y_sample, nk, nv, ns)
```

```python
import math, os, sys
SKIP = set(os.environ.get('KSKIP', '').split(','))
import numpy as np
from contextlib import ExitStack
import ml_dtypes
import concourse.bass as bass
import concourse.mybir as mybir
from concourse.bass_utils import run_bass_kernel_spmd

F32 = mybir.dt.float32
BF16 = mybir.dt.bfloat16
AF = mybir.ActivationFunctionType
ALU = mybir.AluOpType

D = 1024
DEPTH = 4
DFF = 2816
NT = 1536
EPS = 1e-6
IN_W = 9760
OFF_Z, OFF_XBC, OFF_DT, OFF_Q, OFF_K, OFF_V, OFF_U, OFF_G = 0, 1024, 2560, 2592, 3616, 4640, 5664, 6688
PV_BADA, PV_NG, PV_CW, PV_CB, PV_SNG, PV_PS, PV_DV, PV_DNG, PV_LAM, PV_DTB, PV_ALOG = 0, 72, 96, 156, 168, 176, 184, 192, 193, 449, 450
PL = 451
NEG = -30000.0
DEBUG_ANNOT = bool(os.environ.get('KANNOT'))
STOPAT = os.environ.get('KSTOP', '')


class StopBuild(Exception):
    pass


def chk(name):
    if STOPAT and name == STOPAT:
        raise StopBuild(name)


class Buf:
    __slots__ = ("name", "w", "r", "dsem", "dcnt")

    def __init__(self, name):
        self.name = name
        self.w = None
        self.r = []
        self.dsem = None
        self.dcnt = 0


class KB:
    ENGS = ("pe", "dve", "act", "pool", "sp")

    def __init__(self, nc, n_dma_sems=70):
        self.nc = nc
        self.prog = {e: [] for e in self.ENGS}
        self.cnt = {e: 0 for e in self.ENGS}
        self.waited = {}
        self.dma_free = ["d%d" % i for i in range(n_dma_sems)]
        self.sem_names = list(self.ENGS) + list(self.dma_free)
        self.sems = {}
        self.pending = {e: False for e in self.ENGS}
        self.nins = 0
        self.dbufs = []

    def _deps(self, eng, reads, writes):
        toks = []
        for b in reads:
            if b.w is not None:
                toks.append(b.w)
        for b in writes:
            if b.w is not None:
                toks.append(b.w)
            toks.extend(b.r)
        best = {}
        for (sk, v) in toks:
            if sk == "pe" and eng == "pe":
                continue
            if sk in self.cnt and v > self.cnt[sk]:
                if sk == eng:
                    continue
                raise RuntimeError("dep on open nosig group %s (eng %s)" % (sk, eng))
            if v > best.get(sk, 0):
                best[sk] = v
        for sk, v in best.items():
            if self.waited.get((eng, sk), 0) >= v:
                continue
            self.waited[(eng, sk)] = v
            self.prog[eng].append(("wait", sk, v))

    def op(self, eng, fn, reads=(), writes=(), sig=True):
        self._deps(eng, reads, writes)
        tok = (eng, self.cnt[eng] + 1)
        f = sys._getframe(1)
        if f.f_code.co_filename == __file__ and f.f_code.co_name in ("mm", "tr", "act", "tt", "ts", "stt", "cp", "recip", "memset"):
            f = f.f_back
        self.prog[eng].append(("op", fn, sig, "L%d" % f.f_lineno))
        self.nins += 1
        if sig:
            self.cnt[eng] += 1
        self.pending[eng] = not sig
        for b in reads:
            if len(b.r) > 64:
                b.r = b.r[-32:] if False else b.r
            b.r.append(tok)
        for b in writes:
            b.w = tok
            b.r = []
        return tok

    def dma(self, eng, fn, reads=(), writes=(), sbuf=None):
        self._deps(eng, reads, writes)
        b = sbuf if sbuf is not None else (writes[0] if writes else reads[0])
        if b.dsem is None:
            b.dsem = self.dma_free.pop(0)
            self.dbufs.append(b)
        b.dcnt += 1
        tok = (b.dsem, 16 * b.dcnt)
        self.prog[eng].append(("dma", fn, b.dsem))
        self.nins += 1
        for x in reads:
            x.r.append(tok)
        for x in writes:
            x.w = tok
            x.r = []
        return tok

    def barrier(self, engs=("pe", "dve", "act", "sp")):
        for e in self.ENGS:
            assert not self.pending[e]
        targets = [(e2, self.cnt[e2]) for e2 in ("pe", "dve", "act", "pool")]
        targets += [(b.dsem, 16 * b.dcnt) for b in self.dbufs]
        for e in engs:
            for (sk, v) in targets:
                if v == 0 or self.waited.get((e, sk), 0) >= v:
                    continue
                self.waited[(e, sk)] = v
                self.prog[e].append(("wait", sk, v))

    def final_wait(self, eng, bufs):
        self._deps(eng, bufs, bufs)

    def simulate(self):
        sem = {n: 0 for n in self.sem_names}
        pc = {e: 0 for e in self.ENGS}
        progress = True
        while progress:
            progress = False
            for e in self.ENGS:
                prog = self.prog[e]
                while pc[e] < len(prog):
                    it = prog[pc[e]]
                    if it[0] == "wait":
                        if sem[it[1]] < it[2]:
                            break
                    elif it[0] == "op":
                        if it[2]:
                            sem[e] += 1
                    else:
                        sem[it[2]] += 16
                    pc[e] += 1
                    progress = True
        bad = [(e, pc[e], len(self.prog[e]), self.prog[e][pc[e]][:3], sem[self.prog[e][pc[e]][1]] if self.prog[e][pc[e]][0] == "wait" else None)
               for e in self.ENGS if pc[e] < len(self.prog[e])]
        if bad:
            raise RuntimeError("DEADLOCK in emitted program: %s" % (bad,))
        for e in self.ENGS:
            assert sem[e] == self.cnt[e]

    def emit(self, stack):
        self.simulate()
        nc = self.nc
        for nm in self.sem_names:
            self.sems[nm] = stack.enter_context(nc.semaphore("s_" + nm))
        block = stack.enter_context(nc.Block())
        deco = {"pe": block.tensor, "dve": block.vector, "act": block.scalar, "pool": block.gpsimd, "sp": block.sync}
        for e in self.ENGS:
            assert not self.pending[e], "engine %s ends with open group" % e
            prog = self.prog[e]
            sems = self.sems
            esem = sems[e]

            def body(engine, prog=prog, esem=esem, sems=sems):
                for item in prog:
                    if item[0] == "wait":
                        engine.wait_ge(sems[item[1]], item[2])
                    elif item[0] == "op":
                        ins = item[1](engine)
                        if DEBUG_ANNOT:
                            ins.annotate(item[3])
                        if item[2]:
                            ins.then_inc(esem, 1)
                    else:
                        item[1](engine).then_inc(sems[item[2]], 16)

            deco[e](body)

    def mm(self, out, lhsT, rhs, start, stop, reads, writes, sig):
        self.op("pe", lambda e: e.matmul(out, lhsT=lhsT, rhs=rhs, start=start, stop=stop), reads, writes, sig)

    def tr(self, out, in_, ident, reads, writes, sig=True):
        self.op("pe", lambda e: e.transpose(out, in_, ident), reads, writes, sig)

    def act(self, out, in_, func, reads, writes, bias=0.0, scale=1.0):
        self.op("act", lambda e: e.activation(out, in_, func, bias=bias, scale=scale), reads, writes)

    def tt(self, out, in0, in1, op, reads, writes, eng="dve"):
        self.op(eng, lambda e: e.tensor_tensor(out=out, in0=in0, in1=in1, op=op), reads, writes)

    def ts(self, out, in0, s1, s2, op0, op1, reads, writes, eng="dve"):
        if s2 is None:
            self.op(eng, lambda e: e.tensor_single_scalar(out, in0, s1, op0), reads, writes)
        else:
            self.op(eng, lambda e: e.tensor_scalar(out=out, in0=in0, scalar1=s1, scalar2=s2, op0=op0, op1=op1), reads, writes)

    def stt(self, out, in0, scalar, in1, op0, op1, reads, writes, eng="dve"):
        self.op(eng, lambda e: e.scalar_tensor_tensor(out=out, in0=in0, scalar=scalar, in1=in1, op0=op0, op1=op1), reads, writes)

    def cp(self, out, in_, reads, writes, eng="dve"):
        if eng == "act":
            self.op("act", lambda e: e.copy(out, in_), reads, writes)
        else:
            self.op(eng, lambda e: e.tensor_copy(out, in_), reads, writes)

    def recip(self, out, in_, reads, writes):
        self.op("dve", lambda e: e.reciprocal(out, in_), reads, writes)

    def memset(self, ap, val, writes, eng="dve"):
        self.op(eng, lambda e: e.memset(ap, val), (), writes)

    def ld(self, out, in_, writes, eng="sp", sbuf=None):
        self.dma(eng, lambda e: e.dma_start(out=out, in_=in_), (), writes, sbuf=sbuf)

    def st(self, out, in_, reads, eng="sp", sbuf=None):
        self.dma(eng, lambda e: e.dma_start(out=out, in_=in_), reads, (), sbuf=sbuf)


class Rot:
    def __init__(self, items):
        self.items = items
        self.i = 0

    def next(self):
        it = self.items[self.i % len(self.items)]
        self.i += 1
        return it


def host_consts():
    c = {}
    c["identf"] = np.eye(128, dtype=np.float32)
    c["identb"] = np.eye(128, dtype=np.float32).astype(ml_dtypes.bfloat16)
    c["onesb"] = np.ones((128, 128), np.float32).astype(ml_dtypes.bfloat16)
    c["onesf"] = np.ones((128, 128), np.float32)
    s = np.arange(128)[:, None]
    l = np.arange(128)[None, :]
    mf = np.where(l >= s, 0.0, NEG).astype(np.float32)
    mb = np.where(l <= s, 0.0, NEG).astype(np.float32)
    c["maskf"] = np.tile(mf, (1, 4)).astype(ml_dtypes.bfloat16)
    c["maskb"] = np.tile(mb, (1, 4)).astype(ml_dtypes.bfloat16)
    sel = np.zeros((32, 32, 128), np.float32)
    for h in range(32):
        sel[h, h, :] = 1.0
    c["sel"] = sel.reshape(32, 32 * 128)
    sg = np.zeros((32, 4), np.float32)
    sg[:16, 0] = 1.0
    sg[16:, 0] = -1.0
    sg[16:, 1] = 1.0
    sg[:, 2] = -1.0
    c["sgn"] = sg
    rm = np.ones((32, 1024), np.float32)
    rm[:, ::128] = 0.0
    c["rmask"] = rm
    n_freq = 16
    inv_freq = (10000.0 ** (-np.arange(n_freq, dtype=np.float32) / n_freq)).astype(np.float32)
    t = np.arange(1024)
    pos_row = (t // 64).astype(np.float32)
    pos_col = (t % 64).astype(np.float32)
    cos = np.zeros((128, 1024), np.float32)
    sin = np.zeros((128, 1024), np.float32)
    perm = np.zeros((128, 128), np.float32)
    for d in range(128):
        dd = d % 64
        pos = pos_row if dd < 32 else pos_col
        f = inv_freq[dd % 16]
        ang = (pos * f).astype(np.float32)
        cos[d] = np.cos(ang)
        sin[d] = np.sin(ang)
        if (d % 32) < 16:
            perm[d + 16, d] = -1.0
        else:
            perm[d - 16, d] = 1.0
    c["cos"] = cos
    c["sin"] = sin
    c["perm"] = perm.astype(ml_dtypes.bfloat16)
    def invc(seqs, padlen):
        out = np.ones((4, padlen), np.float32)
        for g, w in enumerate((2, 4, 8, 16)):
            for (o, L) in seqs:
                tt = np.arange(L)
                lo = np.clip(tt - w // 2, 0, L)
                hi = np.clip(tt + w - w // 2, 0, L)
                out[g, o:o + L] = 1.0 / (hi - lo).astype(np.float32)
        return np.broadcast_to(out[None], (128, 4, padlen)).reshape(128, 4 * padlen).astype(ml_dtypes.bfloat16)
    c["invc0"] = invc([(8, 256), (280, 256)], 544)
    c["invc1"] = invc([(8, 1024)], 1040)
    return c


CONST_SPECS = [("identf", [128, 128], F32), ("identb", [128, 128], BF16), ("onesb", [128, 128], BF16),
               ("onesf", [128, 128], F32), ("maskf", [128, 512], BF16), ("maskb", [128, 512], BF16),
               ("sel", [32, 4096], F32), ("sgn", [32, 4], F32), ("rmask", [32, 1024], F32),
               ("cos", [128, 1024], F32), ("sin", [128, 1024], F32), ("perm", [128, 128], BF16),
               ("invc0", [128, 4 * 544], BF16), ("invc1", [128, 4 * 1040], BF16)]


def build(n_layers=DEPTH, do_mixer=True, dbg=False):
    nc = bass.Bass("TRN2", target_bir_lowering=False)

    def din(name, shape, dt=F32):
        return nc.dram_tensor(name, shape, dt, kind="ExternalInput").ap()

    def dout(name, shape, dt=F32):
        return nc.dram_tensor(name, shape, dt, kind="ExternalOutput").ap()

    xT_d = din("xT", [D, NT])
    cond_d = din("cond", [128, 16])
    pv_d = din("pv", [128, DEPTH * PL + 8])
    ckT_d = din("ckT", [DEPTH, 8, 128, 512])
    cv_d = din("cv", [DEPTH, 512, 8, 128])
    st0_d = din("st0", [DEPTH, 2, 64, 1024])
    w_ada_d = din("w_ada", [DEPTH, D, 9 * D])
    ffn_in_d = din("ffn_w_in", [DEPTH, 2, D, 2 * DFF])
    ffn_out_d = din("ffn_w_out", [DEPTH, 2, DFF, D])
    w_in_d = din("w_in", [DEPTH, D, IN_W])
    pmap_d = din("pool_map", [DEPTH, 4, 256, 256])
    wbr_d = din("w_branch", [DEPTH, 3, D, D])
    wo_d = din("w_out", [DEPTH, D, D])
    cdram = {n: din("c_" + n, shp, dt) for (n, shp, dt) in CONST_SPECS}
    yT_d = dout("yT", [D, NT])
    nk_d = dout("nk", [2, DEPTH, 256, 8, 128])
    nv_d = dout("nv", [2, DEPTH, 256, 8, 128])
    ns_d = dout("ns", [2, DEPTH, 2, 64, 1024])
    dbg_d = dout("dbg", [8, 128, 1536]) if dbg else None

    st = ExitStack()
    with st:
        def sb(name, shape, dt=F32):
            return st.enter_context(nc.sbuf_tensor("s_" + name, shape, dt))

        kb = KB(nc)
        xT = sb("xT", [128, 8, NT])
        xB = [Buf("x%d" % i) for i in range(3)]
        hB = [Buf("h%d" % i) for i in range(3)]
        NSLOT = 2
        wslots = Rot([(sb("ws%d" % i, [128, 4096], BF16), Buf("ws%d" % i)) for i in range(NSLOT)])
        pv = sb("pv", [128, DEPTH * PL + 8])
        cond = sb("cond", [128, 8, 2])
        sc = sb("sc", [128, 8, 2])
        mod = sb("mod", [128, DEPTH, 72, 2])
        Amod = sb("Amod", [128, DEPTH, 3, 8, 2])
        Gmod = sb("Gmod", [128, DEPTH, 3, 8, 2])
        CB_ = Buf("const")
        MB = Buf("mod")
        identf = sb("identf", [128, 128]); identb = sb("identb", [128, 128], BF16)
        onesb = sb("onesb", [128, 128], BF16); onesf = sb("onesf", [128, 128])
        ps = [st.enter_context(nc.psum_tensor("ps%d" % i, [128, 512], F32)) for i in range(7)]
        pB = [Buf("ps%d" % i) for i in range(7)]
        psb = st.enter_context(nc.psum_tensor("psb", [128, 1024], BF16))
        psbB = Buf("psb")
        PH = 108 * 1024
        phase = sb("phase", [128, PH // 4])

        class Carver:
            def __init__(self):
                self.off = 0
                self.live = []
                self.last = (0, 0)
                self.mark = None

            def reset(self, off=None):
                self.off = HT_END if off is None else off
                self.mark = None

            def mk(self, name):
                lo, hi = (self.mark if self.mark is not None else self.last[0]), self.off
                self.mark = None
                self.last = (lo, hi)
                b = Buf(name)
                toks = []
                for (l2, h2, b2) in self.live:
                    if l2 < hi and lo < h2:
                        if b2.w is not None:
                            toks.append(b2.w)
                        toks.extend(b2.r)
                b.r = list(dict.fromkeys(toks))
                self.live = [(l2, h2, b2) for (l2, h2, b2) in self.live if not (lo <= l2 and h2 <= hi)]
                self.live.append((lo, hi, b))
                return b

            def mks(self, name, n):
                lo, hi = (self.mark if self.mark is not None else self.last[0]), self.off
                self.last = (lo, hi)
                bs = []
                keep = self.live
                for i in range(n):
                    self.live = list(keep)
                    self.mark = lo
                    bs.append(self.mk("%s%d" % (name, i)))
                self.live = [(l2, h2, b2) for (l2, h2, b2) in keep if not (lo <= l2 and h2 <= hi)] + [(lo, hi, b) for b in bs]
                return bs

            def take(self, shape, dt=F32):
                n = int(np.prod(shape[1:]))
                nbytes = n * (4 if dt == F32 else 2)
                nbytes = (nbytes + 63) // 64 * 64
                assert self.off + nbytes <= PH, "phase region overflow %d" % (self.off + nbytes)
                self.peak = max(getattr(self, "peak", 0), self.off + nbytes)
                self.last = (self.off, self.off + nbytes)
                if self.mark is None:
                    self.mark = self.off
                ap = phase[0:shape[0], self.off // 4:(self.off + nbytes) // 4]
                if dt != F32:
                    ap = ap.bitcast(BF16)[:, 0:n]
                else:
                    ap = ap[:, 0:n]
                self.off += nbytes
                if len(shape) > 2:
                    names = " ".join("a%d" % i for i in range(len(shape) - 1))
                    kw = {"a%d" % i: shape[i + 1] for i in range(len(shape) - 1)}
                    ap = ap.rearrange("p (%s) -> p %s" % (names, names), **kw)
                return ap

        cv = Carver()
        hT = cv.take([128, 8, NT], BF16)
        HT_END = cv.off

        for mt in range(3):
            kb.ld(xT[:, :, mt * 512:(mt + 1) * 512], xT_d.rearrange("(c p) t -> p c t", p=128)[:, :, mt * 512:(mt + 1) * 512], [xB[mt]])
        kb.ld(pv[:], pv_d, [CB_])
        kb.ld(cond[:].rearrange("p c j -> p (c j)"), cond_d, [CB_])
        kb.ld(identf[:], cdram["identf"], [CB_]); kb.ld(identb[:], cdram["identb"], [CB_])
        kb.ld(onesb[:], cdram["onesb"], [CB_]); kb.ld(onesf[:], cdram["onesf"], [CB_])

        def pvl(l, off, n=1):
            return pv[:, l * PL + off:l * PL + off + n]

        kb.act(sc[:], cond[:], AF.Silu, [CB_], [MB])
        cv.reset()
        wa = Rot([(cv.take([128, 8, 512], BF16), cv.mk("wa%d" % i)) for i in range(5)])
        scb = sb("scb", [128, 8, 2], BF16)
        kb.cp(scb[:], sc[:], [MB], [MB])
        for l in range(n_layers):
            wv = w_ada_d[l].rearrange("(kc p) n -> p kc n", p=128)
            for blk in range(18):
                wt, wb = wa.next()
                kb.ld(wt, wv[:, :, blk * 512:(blk + 1) * 512], [wb], eng="pool")
                for mi in range(4):
                    m = blk * 4 + mi
                    for kc in range(8):
                        kb.mm(ps[6][:, 2 * m:2 * m + 2], wt[:, kc, mi * 128:(mi + 1) * 128], scb[:, kc, :], kc == 0, kc == 7,
                              [wb, MB], [pB[6]], sig=(kc == 7))
            kb.tt(mod[:, l], ps[6][:, 0:144].rearrange("p (m j) -> p m j", j=2),
                  pvl(l, PV_BADA, 72).unsqueeze(2).to_broadcast([128, 72, 2]), ALU.add, [pB[6], CB_], [MB])
            for i in range(3):
                kb.stt(Amod[:, l, i], mod[:, l, (3 * i + 1) * 8:(3 * i + 2) * 8, :], 1.0,
                       pvl(l, PV_NG + 8 * i, 8).unsqueeze(2).to_broadcast([128, 8, 2]), ALU.add, ALU.mult, [MB, CB_], [MB])
                kb.ts(Gmod[:, l, i], mod[:, l, (3 * i + 2) * 8:(3 * i + 3) * 8, :], 0.5, None, ALU.mult, None, [MB], [MB])

        CND = [0, 1, 1]
        DBGB = Buf("dbg")

        def tap(idx, ap, reads, n):
            if dbg:
                kb.dma("sp", lambda e: e.dma_start(out=dbg_d[idx, 0:ap.shape[0], 0:n], in_=ap), reads, (), sbuf=DBGB)

        tap(0, mod[:, 0].rearrange("p m j -> p (m j)"), [MB], 144)
        tap(1, Amod[:, 0].rearrange("p i c j -> p (i c j)"), [MB], 48)
        tap(2, sc[:].rearrange("p c j -> p (c j)"), [MB], 16)

        def rstd_from_ps(pst, pbuf, n, dfeat, out_ap, outbuf, tmp_ap, tmpbuf):
            kb.ts(tmp_ap, pst, 1.0 / dfeat, EPS, ALU.mult, ALU.add, [pbuf], [tmpbuf])
            kb.act(tmp_ap, tmp_ap, AF.Ln, [tmpbuf], [tmpbuf])
            kb.act(out_ap, tmp_ap, AF.Exp, [tmpbuf], [outbuf], scale=-0.5)

        def norm_to_h(l, i, mts, sq, sqB, rs, rsB, tmp, tmpB):
            for mt in mts:
                cs = slice(mt * 512, (mt + 1) * 512)
                for kc in range(8):
                    kb.act(sq[:, kc, :], xT[:, kc, cs], AF.Square, [xB[mt]], [sqB])
                for kc in range(8):
                    kb.mm(ps[6][:], onesb[:], sq[:, kc, :], kc == 0, kc == 7, [sqB, CB_], [pB[6]], sig=(kc == 7))
                rstd_from_ps(ps[6][:], pB[6], 512, D, rs, rsB, tmp, tmpB)
                if l == 0 and i == 0:
                    tap(3, rs, [rsB], 512) if mt == 0 else None
                    tap(4, rs, [rsB], 512) if mt == 1 else None
                for kc in range(8):
                    kb.stt(tmp, xT[:, kc, cs], Amod[:, l, i, kc, CND[mt]:CND[mt] + 1], rs, ALU.mult, ALU.mult, [xB[mt], MB, rsB], [tmpB])
                    kb.act(hT[:, kc, cs], tmp, AF.Identity, [tmpB, MB], [hB[mt]], bias=mod[:, l, 3 * i * 8 + kc, CND[mt]:CND[mt] + 1])

        def wload(srcs, shape):
            wt, wb = wslots.next()
            n = int(np.prod(shape[1:]))
            names = " ".join("a%d" % i for i in range(len(shape) - 1))
            kw = {"a%d" % i: shape[i + 1] for i in range(len(shape) - 1)}
            view = wt[:, 0:n].rearrange("p (%s) -> p %s" % (names, names), **kw) if len(shape) > 2 else wt[:, 0:n]
            for (fn, src) in srcs:
                kb.dma("pool", (lambda e, o=fn(view), s=src: e.dma_start(out=o, in_=s)), (), [wb])
            return view, wb

        def ffn(l, which):
            cv.reset()
            sq = cv.take([128, 8, 512], BF16); sqB = cv.mk("sq")
            rs = cv.take([128, 512]); rsB = cv.mk("rs")
            tmp = cv.take([128, 512]); tmpB = cv.mk("tmp")
            norm_to_h(l, 0 if which == 0 else 2, [0, 1, 2], sq, sqB, rs, rsB, tmp, tmpB)
            chk("ffn%d_norm" % which)
            cv.reset()
            aT = cv.take([128, 12, NT], BF16)
            aB = cv.mks("a", 12)
            sgs = Rot([(cv.take([128, NT]), cv.mk("sg%d" % i)) for i in range(2)])
            w1 = ffn_in_d[l, which].rearrange("(kc p) n -> p kc n", p=128)
            w2 = ffn_out_d[l, which].rearrange("(kc p) n -> p kc n", p=128)
            gi = 0 if which == 0 else 2
            pset = Rot([(0, 1, 2), (3, 4, 5)])
            for (p0, p1) in ((0, 6), (6, 11)):
                nk = (p1 - p0) * 2
                for pr in range(p0, p1):
                    wv, wb = wload([(lambda v: v[:, :, 0, :], w1[:, :, pr * 256:(pr + 1) * 256]),
                                    (lambda v: v[:, :, 1, :], w1[:, :, DFF + pr * 256:DFF + (pr + 1) * 256])], [128, 8, 2, 256])
                    for ci in range(2):
                        j = (pr - p0) * 2 + ci
                        sg, sgB = sgs.next()
                        bg = pset.next()
                        for kc in range(8):
                            for mt in range(3):
                                kb.mm(ps[bg[mt]][:], wv[:, kc, 0, ci * 128:(ci + 1) * 128], hT[:, kc, mt * 512:(mt + 1) * 512],
                                      kc == 0, kc == 7, [wb, hB[mt]], [pB[bg[mt]]], sig=(kc == 7 and mt == 2))
                        for mt in range(3):
                            kb.act(sg[:, mt * 512:(mt + 1) * 512], ps[bg[mt]][:], AF.Silu, [pB[bg[mt]]], [sgB])
                        bu = pset.next()
                        for kc in range(8):
                            for mt in range(3):
                                kb.mm(ps[bu[mt]][:], wv[:, kc, 1, ci * 128:(ci + 1) * 128], hT[:, kc, mt * 512:(mt + 1) * 512],
                                      kc == 0, kc == 7, [wb, hB[mt]], [pB[bu[mt]]], sig=(kc == 7 and mt == 2))
                        for mt in range(3):
                            kb.tt(aT[:, j, mt * 512:(mt + 1) * 512], ps[bu[mt]][:], sg[:, mt * 512:(mt + 1) * 512], ALU.mult,
                                  [pB[bu[mt]], sgB], [aB[j]])
                chk("ffn%d_in%d" % (which, p0))
                for mp in range(4):
                    wv, wb = wload([(lambda v: v, w2[:, p0 * 2:p0 * 2 + nk, mp * 256:(mp + 1) * 256])], [128, nk, 256])
                    for mi in range(2):
                        m = mp * 2 + mi
                        bo = pset.next()
                        for kc in range(nk):
                            for mt in range(3):
                                kb.mm(ps[bo[mt]][:], wv[:, kc, mi * 128:(mi + 1) * 128], aT[:, kc, mt * 512:(mt + 1) * 512],
                                      kc == 0, kc == nk - 1, [wb, aB[kc]], [pB[bo[mt]]], sig=(kc == nk - 1 and mt == 2))
                        for mt in range(3):
                            cs = slice(mt * 512, (mt + 1) * 512)
                            kb.stt(xT[:, m, cs], ps[bo[mt]][:], Gmod[:, l, gi, m, CND[mt]:CND[mt] + 1], xT[:, m, cs], ALU.mult, ALU.add,
                                   [pB[bo[mt]], MB, xB[mt]], [xB[mt]])


        sel = sb("sel", [32, 4096])
        sgn = sb("sgn", [32, 4])
        rmask = sb("rmask", [32, 1024])
        maskf = sb("maskf", [128, 512], BF16); maskb = sb("maskb", [128, 512], BF16)
        perm = sb("perm", [128, 128], BF16)
        lamt = sb("lamt", [128, DEPTH, 4])
        for (t_, n_) in ((sel, "sel"), (sgn, "sgn"), (rmask, "rmask"), (maskf, "maskf"), (maskb, "maskb"), (perm, "perm")):
            kb.ld(t_[:], cdram[n_], [CB_])
        LAM_INIT = [0.8 - 0.6 * math.exp(-0.3 * l_) for l_ in range(DEPTH)]
        GROUPS = [([0], [(0, 256), (256, 256)]), ([1, 2], [(0, 1024)])]
        pcur = [0]
        OUTB = []

        def pbank():
            b = pcur[0] % 6
            pcur[0] += 1
            return b

        def mixer(l):
            W = w_in_d[l].rearrange("(kc p) n -> p kc n", p=128)
            cv.reset()
            lt = cv.take([128, 128]); ltB = cv.mk("lt")
            la = pv[:, l * PL + PV_LAM:l * PL + PV_LAM + 256].rearrange("p (a d) -> p a d", a=4)
            kb.tt(lt[:, 0:64], la[:, 0, :], la[:, 1, :], ALU.mult, [CB_], [ltB])
            kb.tt(lt[:, 64:128], la[:, 2, :], la[:, 3, :], ALU.mult, [CB_], [ltB])
            kb.op("dve", lambda e: e.reduce_sum(out=lamt[:, l, 0:2], in_=lt[:].rearrange("p (a d) -> p a d", a=2), axis=mybir.AxisListType.X), [ltB], [MB])
            kb.act(lamt[:, l, 0:2], lamt[:, l, 0:2], AF.Exp, [MB], [MB])
            kb.tt(lamt[:, l, 2:3], lamt[:, l, 0:1], lamt[:, l, 1:2], ALU.subtract, [MB], [MB])
            kb.ts(lamt[:, l, 2:3], lamt[:, l, 2:3], LAM_INIT[l], None, ALU.add, None, [MB], [MB])
            kb.ts(lamt[:, l, 3:4], lamt[:, l, 2:3], -1.0, None, ALU.mult, None, [MB], [MB])
            for gi, (mts, seqs) in enumerate(GROUPS):
                mixer_group(l, W, gi, mts, seqs)

        def mixer_group(l, W, gi, mts, seqs):
            g0 = mts[0] * 512
            T = len(mts) * 512
            nmt = len(mts)
            cnd = CND[mts[0]]
            cv.reset()
            sq = cv.take([128, 8, 512], BF16); sqB = cv.mk("sq")
            rs = cv.take([128, 512]); rsB = cv.mk("rs")
            tmp = cv.take([128, 512]); tmpB = cv.mk("tmp")
            norm_to_h(l, 1, mts, sq, sqB, rs, rsB, tmp, tmpB)
            cv.reset()
            hBs = [hB[mt] for mt in mts]

            def proj(srcs, shape, lhs_fn, M, evac):
                wv, wb = wload(srcs, shape)
                for mi, mt in enumerate(mts):
                    b = pbank()
                    for kc in range(8):
                        kb.mm(ps[b][0:M, :], lhs_fn(wv, kc), hT[:, kc, mt * 512:(mt + 1) * 512], kc == 0, kc == 7, [wb, hB[mt]], [pB[b]], sig=(kc == 7))
                    evac(mi, ps[b][0:M, :], pB[b])

            yT = cv.take([128, 8, T], BF16); yB = cv.mks("y", nmt)
            YEND = cv.off
            if 'ssd' in SKIP:
                for mi in range(nmt):
                    kb.memset(yT[:, :, mi * 512:(mi + 1) * 512], 0.0, [yB[mi]])
            dtT = cv.take([32, T]); aT_ = cv.take([32, T]); cumT = cv.take([32, T]); fmB = cv.mk("fm")
            SSD0 = cv.off
            t1 = cv.take([32, T]); t2 = cv.take([32, T]); t12B = cv.mk("t12")
            ac = cv.take([32, 2]); acB = cv.mk("ac")
            kb.act(ac[:, 0:1], pv[0:32, l * PL + PV_ALOG:l * PL + PV_ALOG + 1], AF.Exp, [CB_], [acB])
            kb.ts(ac[:, 1:2], ac[:, 0:1], -1.0, None, ALU.mult, None, [acB], [acB])

            def ev_dt(mi, pst, pb):
                kb.act(t1[:, mi * 512:(mi + 1) * 512], pst, AF.Identity, [pb, CB_], [t12B], bias=pv[0:32, l * PL + PV_DTB:l * PL + PV_DTB + 1])
            proj([(lambda v: v, W[:, :, OFF_DT:OFF_DT + 32])], [128, 8, 32], lambda wv, kc: wv[:, kc, :], 32, ev_dt)
            kb.ts(t2[:], t1[:], -1.0, None, ALU.mult, None, [t12B], [t12B])
            kb.tt(t2[:], t2[:], t1[:], ALU.max, [t12B], [t12B])
            kb.act(t2[:], t2[:], AF.Exp, [t12B], [t12B], scale=-1.0)
            kb.act(t2[:], t2[:], AF.Ln, [t12B], [t12B], bias=1.0)
            kb.ts(t1[:], t1[:], 0.0, None, ALU.max, None, [t12B], [t12B])
            kb.tt(dtT[:], t1[:], t2[:], ALU.add, [t12B], [fmB])
            kb.ts(aT_[:], dtT[:], ac[:, 1:2], None, ALU.mult, None, [fmB, acB], [fmB])
            for c0 in range(0, T, 1024):
                n_ = min(1024, T - c0)
                kb.op("dve", lambda e, c0=c0, n_=n_: e.tensor_tensor_scan(out=cumT[:, c0:c0 + n_], data0=rmask[:, 0:n_], data1=aT_[:, c0:c0 + n_],
                                                                      initial=0.0, op0=ALU.mult, op1=ALU.add), [fmB, CB_], [fmB])

            nseq = len(seqs)
            PW = T + 4 * nseq
            for j in range(0 if 'ssd' in SKIP else 2):
                cv.reset(SSD0)
                xbc = cv.take([128, 6, T], BF16); xbB = cv.mk("xbc")
                PC0 = cv.off
                pcs = Rot([(cv.take([128, PW]), cv.mk("pc%d" % i)) for i in range(1)])
                accs = Rot([(cv.take([128, PW]), cv.mk("acc%d" % i)) for i in range(1)])
                for (pc_, pcb_) in pcs.items:
                    kb.memset(pc_[:], 0.0, [pcb_])
                chunk_ids = [4 * j, 4 * j + 1, 4 * j + 2, 4 * j + 3, 8 + j, 10 + j]
                for ci, c in enumerate(chunk_ids):
                    pc_, pcb_ = pcs.next()
                    acc, accB = accs.next()

                    def ev_pc(mi, pst, pb, pc_=pc_, pcb_=pcb_):
                        for k, (o, L) in enumerate(seqs):
                            lo = max(o, mi * 512); hi = min(o + L, (mi + 1) * 512)
                            if lo < hi:
                                kb.cp(pc_[:, 2 + lo + 4 * k:2 + hi + 4 * k], pst[:, lo - mi * 512:hi - mi * 512], [pb], [pcb_], eng="act")
                    proj([(lambda v: v, W[:, :, OFF_XBC + c * 128:OFF_XBC + (c + 1) * 128])], [128, 8, 128], lambda wv, kc: wv[:, kc, :], 128, ev_pc)
                    n_ = PW - 4
                    cw = pv[:, l * PL + PV_CW + c * 5:l * PL + PV_CW + c * 5 + 5]
                    kb.ts(acc[:, 0:n_], pc_[:, 0:n_], cw[:, 0:1], None, ALU.mult, None, [pcb_, CB_], [accB])
                    for tp in range(1, 5):
                        kb.stt(acc[:, 0:n_], pc_[:, tp:tp + n_], cw[:, tp:tp + 1], acc[:, 0:n_], ALU.mult, ALU.add, [pcb_, CB_, accB], [accB])
                    for k, (o, L) in enumerate(seqs):
                        kb.act(xbc[:, ci, o:o + L], acc[:, o + 4 * k:o + 4 * k + L], AF.Silu, [accB, CB_], [xbB],
                               bias=pv[:, l * PL + PV_CB + c:l * PL + PV_CB + c + 1])
                cv.reset(PC0)
                Hf = cv.take([128, 512]); Hb = cv.take([128, 512]); Hfb = cv.take([128, 512], BF16); HB_ = cv.mk("H")
                ntmax = max(L for (_, L) in seqs) // 128
                Hbin = cv.take([128, ntmax, 512], BF16); HbinB = cv.mk("Hbin")
                sets = []
                for si in range(2):
                    S = {}
                    S["tok"] = cv.take([128, 96]); S["arg"] = cv.take([128, 32]); S["dte"] = cv.take([128, 32]); S["cd"] = cv.take([128, 32]); S["s2"] = cv.take([128, 32]); S["ct"] = cv.take([128, 16]); S["tokB"] = cv.mk("tok%d" % si)
                    S["btok"] = cv.take([128, 128], BF16); S["btB"] = cv.mk("btok%d" % si)
                    S["xdf"] = cv.take([128, 512], BF16); S["xdb"] = cv.take([128, 512], BF16); S["xdd"] = cv.take([128, 512], BF16); S["xdB"] = cv.mk("xd%d" % si)
                    S["Rt"] = cv.take([32, 128]); S["Cn"] = cv.take([32, 128]); S["EL"] = cv.take([32, 128]); S["tb_"] = cv.take([32, 128]); S["rB"] = cv.mk("R%d" % si)
                    S["ty"] = cv.take([128, 4, 128]); S["tyB"] = cv.mk("ty%d" % si)
                    sets.append(S)
                par = [0]
                ct = None
                tok = arg = dte = cd = s2 = tokB = btok = btB = xdf = xdb = xdd = xdB = Rt = Cn = EL = tb_ = rB = ty = tyB = None

                def nextset():
                    nonlocal tok, arg, dte, cd, s2, tokB, btok, btB, xdf, xdb, xdd, xdB, Rt, Cn, EL, tb_, rB, ty, tyB, ct
                    S = sets[par[0] % 2]
                    par[0] += 1
                    tok, arg, dte, cd, s2, tokB = S["tok"], S["arg"], S["dte"], S["cd"], S["s2"], S["tokB"]
                    ct = S["ct"]
                    btok, btB = S["btok"], S["btB"]
                    xdf, xdb, xdd, xdB = S["xdf"], S["xdb"], S["xdd"], S["xdB"]
                    Rt, Cn, EL, tb_, rB = S["Rt"], S["Cn"], S["EL"], S["tb_"], S["rB"]
                    ty, tyB = S["ty"], S["tyB"]

                cbsR = Rot([(cv.take([128, 128]), cv.mk("cb%d" % i)) for i in range(2)])
                decs = Rot([(cv.take([128, 512]), cv.mk("dec%d" % i)) for i in range(2)])
                scs = [(cv.take([128, 4, 128], BF16), cv.mk("sc%d" % i)) for i in range(2)]
                ebcR = Rot([(cv.take([128, 512]), cv.mk("ebc%d" % i)) for i in range(2)])
                ces = [(cv.take([128, 4, 128], BF16), cv.mk("ce%d" % i)) for i in range(2)]
                segb = Rot([3, 6])
                PS_T, PS_S, PS_CB, PS_SEG, PS_E, PS_Y = 0, 1, 2, 3, 4, 5

                for k, (o, L) in enumerate(seqs):
                    nt = L // 128
                    kb.memset(Hf[:], 0.0, [HB_]); kb.memset(Hb[:], 0.0, [HB_])
                    if gi == 1:
                        for d_, H_ in ((0, Hf), (1, Hb)):
                            for gg in range(2):
                                kb.ld(H_[gg * 64:(gg + 1) * 64, gg * 256:(gg + 1) * 256],
                                      st0_d[l, d_][:, (8 * j + 4 * gg) * 64:(8 * j + 4 * gg + 4) * 64], [HB_])
                    kb.cp(Hfb[:], Hf[:], [HB_], [HB_])

                    def prep(i):
                        tsl = slice(o + i * 128, o + (i + 1) * 128)
                        kb.tr(ps[PS_T][:, 0:32], dtT[:, tsl], identf[0:32, 0:32], [fmB, CB_], [pB[PS_T]], sig=False)
                        kb.tr(ps[PS_T][:, 32:64], aT_[:, tsl], identf[0:32, 0:32], [fmB, CB_], [pB[PS_T]], sig=False)
                        kb.tr(ps[PS_T][:, 64:96], cumT[:, tsl], identf[0:32, 0:32], [fmB, CB_], [pB[PS_T]], sig=True)
                        kb.cp(tok[:], ps[PS_T][:, 0:96], [pB[PS_T]], [tokB], eng="act")
                        kb.mm(ps[PS_T][:, 96:128], onesf[:], tok[:, 32:64], True, True, [tokB, CB_], [pB[PS_T]], sig=True)
                        kb.tt(arg[:, 0:16], ps[PS_T][:, 96:112], tok[:, 64:80], ALU.subtract, [pB[PS_T], tokB], [tokB])
                        kb.tt(arg[:, 16:32], tok[:, 80:96], tok[:, 48:64], ALU.subtract, [tokB], [tokB])
                        kb.ts(ct[:], tok[:, 64:80], -1.0, None, ALU.mult, None, [tokB], [tokB])
                        kb.act(dte[:], arg[:], AF.Exp, [tokB], [tokB])
                        kb.act(cd[:], ps[PS_T][:, 96:128], AF.Exp, [pB[PS_T]], [tokB])
                        kb.tt(s2[:], tok[:, 0:32], dte[:], ALU.mult, [tokB], [tokB])
                        for ci in range(4):
                            kb.tr(psb[:, ci * 128:(ci + 1) * 128], xbc[:, ci, tsl], identb[:], [xbB, CB_], [psbB], sig=False)
                        kb.tr(psb[:, 512:640], xbc[:, 4, tsl], identb[:], [xbB, CB_], [psbB], sig=True)
                        kb.cp(btok[:], psb[:, 512:640], [psbB], [btB])
                        return tsl

                    def xprod(out, col0):
                        kb.tt(out[:].rearrange("p (h d) -> p h d", h=8), psb[:, 0:512].rearrange("p (h d) -> p h d", h=8),
                              col0.unsqueeze(2).to_broadcast([128, 8, 64]), ALU.mult, [psbB, tokB], [xdB])

                    def state_update(H_, xsrc, cdcol):
                        kb.mm(ps[PS_S][:], btok[:], xsrc[:], True, True, [btB, xdB], [pB[PS_S]], sig=True)
                        kb.tt(H_[:].rearrange("p (h d) -> p h d", h=8), H_[:].rearrange("p (h d) -> p h d", h=8),
                              cdcol.unsqueeze(2).to_broadcast([128, 8, 64]), ALU.mult, [HB_, tokB], [HB_])
                        kb.tt(H_[:], H_[:], ps[PS_S][:], ALU.add, [HB_, pB[PS_S]], [HB_])

                    for i in range(nt - 1, -1, -1):
                        nextset()
                        prep(i)
                        kb.cp(Hbin[:, i, :], Hb[:], [HB_], [HbinB])
                        xprod(xdd, s2[:, 16 + 8 * j:16 + 8 * j + 8])
                        state_update(Hb, xdd, cd[:, 16 + 8 * j:16 + 8 * j + 8])
                    for i in range(nt):
                        nextset()
                        tsl = prep(i)
                        xprod(xdf, tok[:, 8 * j:8 * j + 8])
                        xprod(xdb, tok[:, 16 + 8 * j:16 + 8 * j + 8])
                        xprod(xdd, s2[:, 8 * j:8 * j + 8])
                        kb.ts(tb_[:], aT_[:, tsl], sgn[:, 1:2], None, ALU.mult, None, [fmB, CB_], [rB])
                        kb.stt(Rt[:], cumT[:, tsl], sgn[:, 0:1], tb_[:], ALU.mult, ALU.add, [fmB, CB_, rB], [rB])
                        last = o + i * 128 + 127
                        kb.stt(EL[:], cumT[:, last:last + 1].to_broadcast([32, 128]), sgn[:, 1:2], Rt[:], ALU.mult, ALU.add, [fmB, CB_, rB], [rB])
                        for gg in range(2):
                            r0 = gg * 64
                            cbs, cbB = cbsR.next()
                            kb.mm(ps[PS_CB][:, 0:128], xbc[r0:r0 + 64, 4, tsl], xbc[r0:r0 + 64, 5, tsl], True, True, [xbB], [pB[PS_CB]], sig=True)
                            kb.cp(cbs[:], ps[PS_CB][:, 0:128], [pB[PS_CB]], [cbB], eng="act")
                            for d_ in range(2):
                                msk = maskf if d_ == 0 else maskb
                                sc_, scB = scs[d_]
                                ce_, ceB = ces[d_]
                                dec, decB = decs.next()
                                PS_SEG = segb.next()
                                ebc, ebB = ebcR.next()
                                kb.mm(ps[PS_SEG][:], identb[:], msk[:], True, False, [CB_], [pB[PS_SEG]], sig=False)
                                for hh in range(4):
                                    dh = d_ * 16 + 8 * j + 4 * gg + hh
                                    kb.mm(ps[PS_SEG][:, hh * 128:(hh + 1) * 128], sel[:, dh * 128:(dh + 1) * 128], Rt[:], False, hh == 3, [CB_, rB], [pB[PS_SEG]], sig=(hh == 3))
                                for hh in range(4):
                                    hx = 8 * j + 4 * gg + hh
                                    bcol = ct[:, hx:hx + 1] if d_ == 0 else arg[:, 16 + hx:16 + hx + 1]
                                    kb.act(dec[:, hh * 128:(hh + 1) * 128], ps[PS_SEG][:, hh * 128:(hh + 1) * 128], AF.Exp, [pB[PS_SEG], tokB], [decB], bias=bcol)
                                kb.tt(sc_[:], dec[:].rearrange("p (h s) -> p h s", h=4), cbs[:].unsqueeze(1).to_broadcast([128, 4, 128]), ALU.mult, [decB, cbB], [scB])
                                for hh in range(4):
                                    dh = d_ * 16 + 8 * j + 4 * gg + hh
                                    kb.mm(ps[PS_E][:, hh * 128:(hh + 1) * 128], sel[:, dh * 128:(dh + 1) * 128], EL[:], True, True, [CB_, rB], [pB[PS_E]], sig=(hh == 3))
                                kb.act(ebc[:], ps[PS_E][:], AF.Exp, [pB[PS_E]], [ebB])
                                kb.tt(ce_[r0:r0 + 64], ebc[r0:r0 + 64, :].rearrange("p (h s) -> p h s", h=4),
                                      xbc[r0:r0 + 64, 5, tsl].unsqueeze(1).to_broadcast([64, 4, 128]), ALU.mult, [ebB, xbB], [ceB])
                            for hh in range(4):
                                hl = gg * 4 + hh
                                cl = hl // 2
                                yr = (hl % 2) * 64
                                out = ps[PS_Y][yr:yr + 64, cl * 128:(cl + 1) * 128]
                                hs = slice(hl * 64, (hl + 1) * 64)
                                kb.mm(out, xdf[:, hs], scs[0][0][:, hh, :], True, False, [xdB, scs[0][1]], [pB[PS_Y]], sig=False)
                                kb.mm(out, xdb[:, hs], scs[1][0][:, hh, :], False, False, [xdB, scs[1][1]], [pB[PS_Y]], sig=False)
                                kb.mm(out, Hfb[r0:r0 + 64, hs], ces[0][0][r0:r0 + 64, hh, :], False, False, [HB_, ces[0][1]], [pB[PS_Y]], sig=False)
                                kb.mm(out, Hbin[r0:r0 + 64, i, hs], ces[1][0][r0:r0 + 64, hh, :], False, True, [HbinB, ces[1][1]], [pB[PS_Y]], sig=True)
                        dv = pv[:, l * PL + PV_DV + 4 * j:l * PL + PV_DV + 4 * j + 4]
                        kb.tt(ty[:], xbc[:, 0:4, tsl], dv.unsqueeze(2).to_broadcast([128, 4, 128]), ALU.mult, [xbB, CB_], [tyB])
                        kb.tt(yT[:, 4 * j:4 * j + 4, tsl], ty[:], ps[PS_Y][:].rearrange("p (c s) -> p c s", c=4), ALU.add, [tyB, pB[PS_Y]], yB)
                        state_update(Hf, xdd, cd[:, 8 * j:8 * j + 8])
                        kb.cp(Hfb[:], Hf[:], [HB_], [HB_])
                    if gi == 0:
                        for d_, H_ in ((0, Hf), (1, Hb)):
                            for gg in range(2):
                                kb.st(ns_d[k, l, d_][:, (8 * j + 4 * gg) * 64:(8 * j + 4 * gg + 4) * 64],
                                      H_[gg * 64:(gg + 1) * 64, gg * 256:(gg + 1) * 256], [HB_])
                        kb.final_wait("dve", [HB_])
                        OUTB.append(HB_)

            cv.reset(YEND)
            merged = cv.take([128, 8, T], BF16); mgB = cv.mks("mg", nmt)
            mergedb = merged; mgbB = mgB
            tgs = Rot([(cv.take([128, 512]), cv.mk("tg%d" % i)) for i in range(2)])
            tms = Rot([(cv.take([128, 512]), cv.mk("tm%d" % i)) for i in range(2)])
            MRG2 = cv.off
            szs = Rot([(cv.take([128, T]), cv.mk("sz%d" % i)) for i in range(2)])
            for c in range(8):
                sz, szB = szs.next()

                def ev_z(mi, pst, pb, sz=sz, szB=szB):
                    kb.act(sz[:, mi * 512:(mi + 1) * 512], pst, AF.Silu, [pb], [szB])
                proj([(lambda v: v, W[:, :, OFF_Z + c * 128:OFF_Z + (c + 1) * 128])], [128, 8, 128], lambda wv, kc: wv[:, kc, :], 128, ev_z)
                kb.tt(yT[:, c, :], yT[:, c, :], sz[:], ALU.mult, yB + [szB], yB)
            sq = cv.take([128, 8, 512], BF16); sqB = cv.mk("sq")
            rs = cv.take([128, 512]); rsB = cv.mk("rs")
            tmp = cv.take([128, 512]); tmpB = cv.mk("tmp")
            for mi in range(nmt):
                cs = slice(mi * 512, (mi + 1) * 512)
                for kc in range(8):
                    kb.act(sq[:, kc, :], yT[:, kc, cs], AF.Square, yB, [sqB])
                for kc in range(8):
                    kb.mm(ps[6][:], onesb[:], sq[:, kc, :], kc == 0, kc == 7, [sqB, CB_], [pB[6]], sig=(kc == 7))
                rstd_from_ps(ps[6][:], pB[6], 512, D, rs, rsB, tmp, tmpB)
                for kc in range(8):
                    kb.stt(yT[:, kc, cs], yT[:, kc, cs], pv[:, l * PL + PV_SNG + kc:l * PL + PV_SNG + kc + 1], rs, ALU.mult, ALU.mult, yB + [CB_, rsB], yB)


            def merge(n):
                for m in range(8):
                    wv, wb = wload([(lambda v: v[:, :, 0, :], wbr_d[l, n].rearrange("(kc p) n -> p kc n", p=128)[:, :, m * 128:(m + 1) * 128]),
                                    (lambda v: v[:, :, 1, :], W[:, :, OFF_G + n * 1024 + m * 128:OFF_G + n * 1024 + (m + 1) * 128])], [128, 8, 2, 128])
                    for mi, mt in enumerate(mts):
                        cs = slice(mi * 512, (mi + 1) * 512)
                        bp = pbank(); bq = pbank()
                        for kc in range(8):
                            kb.mm(ps[bp][:], wv[:, kc, 0, :], yT[:, kc, cs], kc == 0, kc == 7, [wb] + yB, [pB[bp]], sig=(kc == 7))
                        for kc in range(8):
                            kb.mm(ps[bq][:], wv[:, kc, 1, :], hT[:, kc, mt * 512:(mt + 1) * 512], kc == 0, kc == 7, [wb, hB[mt]], [pB[bq]], sig=(kc == 7))
                        tg, tgB = tgs.next()
                        kb.act(tg[:], ps[bq][:], AF.Tanh, [pB[bq]], [tgB], scale=0.5)
                        if n == 0:
                            kb.stt(merged[:, m, cs], tg[:], 1.0, ps[bp][:], ALU.add, ALU.mult, [tgB, pB[bp]], [mgB[mi]])
                        else:
                            tm, tmB = tms.next()
                            kb.stt(tm[:], tg[:], 1.0, ps[bp][:], ALU.add, ALU.mult, [tgB, pB[bp]], [tmB])
                            if n == 1:
                                kb.tt(merged[:, m, cs], merged[:, m, cs], tm[:], ALU.add, [mgB[mi], tmB], [mgB[mi]])
                            else:
                                kb.tt(mergedb[:, m, cs], merged[:, m, cs], tm[:], ALU.add, [mgB[mi], tmB], [mgbB[mi]])

            if 'ssd' in SKIP:
                for mi in range(nmt):
                    kb.memset(yT[:, :, mi * 512:(mi + 1) * 512], 0.0, [yB[mi]])
            merge(0)

            cv.reset(MRG2)
            Tk = T + (512 if gi == 1 else 0)
            ntk = Tk // 128
            NQB = T // 256
            qz = cv.take([128, NQB, 2, 256], BF16); qB = cv.mk("qz")
            kb.memset(qz[0:64, :, 1, :], 0.0, [qB])
            kb.memset(qz[64:128, :, 0, :], 0.0, [qB])
            kh = cv.take([128, Tk], BF16); kB_ = cv.mk("kh")
            vh = cv.take([128, ntk, 128], BF16); vB = cv.mk("vh")
            stg = cv.take([128, 4, 128]); stgB = cv.mk("stg")
            OUTB.append(stgB)
            qraw = cv.take([128, 512], BF16); qrB = cv.mk("qraw")
            r1 = cv.take([128, 512]); r2 = cv.take([128, 512]); rrB = cv.mk("rr")
            Pt = Rot([(cv.take([128, 2, 256], BF16), cv.mk("P%d" % i)) for i in range(3)])
            rden = cv.take([128, 2, 256]); on_ = cv.take([128, 2, 256]); od = cv.take([128, 256]); odsq = cv.take([128, 256], BF16); nB = cv.mk("nrm")
            rs2 = cv.take([128, 256]); tmp2 = cv.take([128, 256]); rs2B = cv.mk("rs2")
            gn = cv.take([128, 2]); gnB = cv.mk("gn")
            kb.ts(gn[:, 0:1], pv[:, l * PL + PV_DNG:l * PL + PV_DNG + 1], 1.0 - LAM_INIT[l], None, ALU.mult, None, [CB_], [gnB])
            if gi == 1:
                cos = cv.take([128, 1024]); sin = cv.take([128, 1024]); csB = cv.mk("cs")
                kb.ld(cos[:], cdram["cos"], [csB]); kb.ld(sin[:], cdram["sin"], [csB])
            PSC = [0, 1]
            qbc = [0]

            chk("att%d_pre" % gi)
            for h in range(0 if ('att' in SKIP or ('att%d' % gi) in SKIP) else 8):
                wv, wb = wload([(lambda v: v[:, :, 0, :], W[:, :, OFF_Q + h * 128:OFF_Q + (h + 1) * 128]),
                                (lambda v: v[:, :, 1, :], W[:, :, OFF_K + h * 128:OFF_K + (h + 1) * 128]),
                                (lambda v: v[:, :, 2, :], W[:, :, OFF_V + h * 128:OFF_V + (h + 1) * 128])], [128, 8, 3, 128])
                koff = Tk - T
                if gi == 1:
                    kb.dma("pool", lambda e, h=h: e.dma_start(out=kh[:, 0:512], in_=ckT_d[l, h]), (), [kB_])
                    kb.dma("pool", lambda e, h=h: e.dma_start(out=vh[:, 0:4, :], in_=cv_d[l][:, h, :].rearrange("(t p) e -> p t e", p=128)), (), [vB])
                for which, dst, dB, off in ((0, None, qB, 0), (1, kh, kB_, koff)):
                    for mi, mt in enumerate(mts):
                        b = 4 + (pcur[0] % 2); pcur[0] += 1
                        for kc in range(8):
                            kb.mm(ps[b][:], wv[:, kc, which, :], hT[:, kc, mt * 512:(mt + 1) * 512], kc == 0, kc == 7, [wb, hB[mt]], [pB[b]], sig=(kc == 7))
                        dcs = slice(off + mi * 512, off + (mi + 1) * 512)
                        if gi == 0:
                            if which == 0:
                                for c_ in range(2):
                                    kb.cp(qz[c_ * 64:(c_ + 1) * 64, 2 * mi:2 * mi + 2, c_, :], ps[b][c_ * 64:(c_ + 1) * 64, :].rearrange("p (a q) -> p a q", a=2),
                                          [pB[b]], [dB], eng="act")
                            else:
                                kb.cp(dst[:, dcs], ps[b][:], [pB[b]], [dB], eng="act")
                        else:
                            tcs = slice(mi * 512, (mi + 1) * 512)
                            kb.cp(qraw[:], ps[b][:], [pB[b]], [qrB], eng="act")
                            kb.mm(ps[6][:], perm[:], qraw[:], True, True, [CB_, qrB], [pB[6]], sig=True)
                            kb.tt(r1[:], ps[b][:], cos[:, tcs], ALU.mult, [pB[b], csB, qrB], [rrB])
                            kb.tt(r2[:], ps[6][:], sin[:, tcs], ALU.mult, [pB[6], csB], [rrB])
                            if which == 0:
                                for c_ in range(2):
                                    rs_ = slice(c_ * 64, (c_ + 1) * 64)
                                    kb.tt(qz[rs_, 2 * mi:2 * mi + 2, c_, :], r1[rs_, :].rearrange("p (a q) -> p a q", a=2),
                                          r2[rs_, :].rearrange("p (a q) -> p a q", a=2), ALU.add, [rrB], [dB])
                            else:
                                kb.tt(dst[:, dcs], r1[:], r2[:], ALU.add, [rrB], [dB])
                chk("att%d_h%d_qk" % (gi, h))
                for t0 in range(0, 0 if 'nov' in SKIP else T // 128, 4):
                    b = 4 + (pcur[0] % 2); pcur[0] += 1
                    for tt_ in range(4):
                        tl = t0 + tt_
                        for kc in range(8):
                            kb.mm(ps[b][:, tt_ * 128:(tt_ + 1) * 128], hT[:, kc, g0 + tl * 128:g0 + (tl + 1) * 128], wv[:, kc, 2, :], kc == 0, kc == 7,
                                  [wb] + hBs, [pB[b]], sig=(kc == 7 and tt_ == 3))
                    if gi == 0 and 'nost' not in SKIP:
                        kb.cp(stg[:], ps[b][:].rearrange("p (t e) -> p t e", t=4), [pB[b]], [stgB])
                        kb.cp(vh[:, koff // 128 + t0:koff // 128 + t0 + 4, :], stg[:], [stgB], [vB], eng="act")
                    else:
                        kb.cp(vh[:, koff // 128 + t0:koff // 128 + t0 + 4, :], ps[b][:].rearrange("p (t e) -> p t e", t=4), [pB[b]], [vB], eng="act")
                    if gi == 0 and 'nost' not in SKIP:
                        for k in range(0 if 'nodma' in SKIP else 2):
                            kb.st(nv_d[k, l][:, h, :].rearrange("(t p) e -> p t e", p=128), stg[:, 2 * k:2 * k + 2, :], [stgB])
                        if 'nokst' in SKIP:
                            continue
                        b2 = 4 + (pcur[0] % 2); pcur[0] += 1
                        for tt_ in range(4):
                            tl = t0 + tt_
                            for kc in range(8):
                                kb.mm(ps[b2][:, tt_ * 128:(tt_ + 1) * 128], hT[:, kc, g0 + tl * 128:g0 + (tl + 1) * 128], wv[:, kc, 1, :], kc == 0, kc == 7,
                                      [wb] + hBs, [pB[b2]], sig=(kc == 7 and tt_ == 3))
                        kb.cp(stg[:], ps[b2][:].rearrange("p (t e) -> p t e", t=4), [pB[b2]], [stgB])
                        for k in range(0 if 'nodma' in SKIP else 2):
                            kb.st(nk_d[k, l][:, h, :].rearrange("(t p) e -> p t e", p=128), stg[:, 2 * k:2 * k + 2, :], [stgB])
                for k, (o, L) in enumerate(seqs if 'nocore' not in SKIP else []):
                    if gi == 0:
                        ktiles = list(range(o // 128, (o + L) // 128))
                    else:
                        ktiles = list(range(ntk))
                    for qb in range(L // 256):
                        qs = slice(o + qb * 256, o + (qb + 1) * 256)
                        gq = (o + qb * 256) // 256
                        PO, PD = ((2, 3), (4, 5))[qbc[0] % 2]
                        qbc[0] += 1
                        nk_ = len(ktiles)
                        Ps = {}

                        def score(ki):
                            kt = ktiles[ki]
                            bs = ki % 2
                            kb.mm(ps[bs][:], kh[:, kt * 128:(kt + 1) * 128], qz[:, gq, :, :].rearrange("p c q -> p (c q)"), True, True, [kB_, qB], [pB[bs]], sig=True)
                            P_, PB_ = Pt.next()
                            Ps[ki] = (P_, PB_)
                            kb.act(P_[:].rearrange("p c q -> p (c q)"), ps[bs][:], AF.Exp, [pB[bs]], [PB_], scale=0.125)

                        score(0)
                        for ki, kt in enumerate(ktiles):
                            if ki + 1 < nk_:
                                score(ki + 1)
                            P_, PB_ = Ps.pop(ki)
                            kb.mm(ps[PO][:], vh[:, kt, :], P_[:].rearrange("p c q -> p (c q)"), ki == 0, ki == nk_ - 1, [vB, PB_], [pB[PO]], sig=False)
                            kb.mm(ps[PD][:], onesb[:], P_[:].rearrange("p c q -> p (c q)"), ki == 0, ki == nk_ - 1, [CB_, PB_], [pB[PD]], sig=True)
                        if 'nonorm' in SKIP:
                            kb.cp(rden[:].rearrange("p c q -> p (c q)"), ps[PD][:], [pB[PD]], [nB])
                            kb.cp(on_[:].rearrange("p c q -> p (c q)"), ps[PO][:], [pB[PO]], [nB])
                            continue
                        kb.act(rden[:].rearrange("p c q -> p (c q)"), ps[PD][:], AF.Ln, [pB[PD]], [nB])
                        kb.act(rden[:].rearrange("p c q -> p (c q)"), rden[:].rearrange("p c q -> p (c q)"), AF.Exp, [nB], [nB], scale=-1.0)
                        kb.tt(on_[:].rearrange("p c q -> p (c q)"), ps[PO][:], rden[:].rearrange("p c q -> p (c q)"), ALU.mult, [pB[PO], nB], [nB])
                        kb.stt(od[:], on_[:, 1, :], lamt[:, l, 3:4], on_[:, 0, :], ALU.mult, ALU.add, [nB, MB], [nB])
                        kb.act(odsq[:], od[:], AF.Square, [nB], [nB])
                        kb.mm(ps[6][:, 0:256], onesb[:], odsq[:], True, True, [CB_, nB], [pB[6]], sig=True)
                        rstd_from_ps(ps[6][:, 0:256], pB[6], 256, 128, rs2, rs2B, tmp2, rs2B)
                        kb.stt(yT[:, h, qs], od[:], gn[:, 0:1], rs2[:], ALU.mult, ALU.mult, [nB, gnB, rs2B], yB)
            if 'att' in SKIP:
                for mi in range(nmt):
                    kb.memset(yT[:, :, mi * 512:(mi + 1) * 512], 0.0, [yB[mi]])
            chk("g%d_att" % gi)
            merge(1)
            chk("g%d_m1" % gi)

            cv.reset(MRG2)
            PP = T + 16 * nseq
            pu = Rot([(cv.take([128, PP]), cv.mk("pu%d" % i)) for i in range(2)])
            for (p_, pb_) in pu.items:
                kb.memset(p_[:], 0.0, [pb_])
            lv = [(cv.take([128, PP]), cv.mk("lv%d" % i)) for i in range(2)]
            pooled = cv.take([128, 2, T], BF16); plB = cv.mk("pooled")
            invc = cv.take([128, 4, PP], BF16); ivB = cv.mk("invc")
            kb.ld(invc[:].rearrange("p g n -> p (g n)"), cdram["invc%d" % gi], [ivB])
            for g in range(0 if 'pool' in SKIP else 4):
                for ci in range(2):
                    c = 2 * g + ci
                    pu_, puB = pu.next()

                    def ev_u(mi, pst, pb, pu_=pu_, puB=puB):
                        for k, (o, L) in enumerate(seqs):
                            lo = max(o, mi * 512); hi = min(o + L, (mi + 1) * 512)
                            if lo < hi:
                                kb.cp(pu_[:, 8 + lo + 16 * k:8 + hi + 16 * k], pst[:, lo - mi * 512:hi - mi * 512], [pb], [puB], eng="act")
                    proj([(lambda v: v, W[:, :, OFF_U + c * 128:OFF_U + (c + 1) * 128])], [128, 8, 128], lambda wv, kc: wv[:, kc, :], 128, ev_u)
                    src, srcB = pu_, puB
                    A_, AB_ = lv[0]
                    kb.tt(A_[:, 1:PP], src[:, 0:PP - 1], src[:, 1:PP], ALU.add, [srcB], [AB_])
                    kb.memset(A_[:, 0:1], 0.0, [AB_])
                    cur, curB = A_, AB_
                    for step, sh in enumerate((1, 2, 4)[:g]):
                        nx, nxB = lv[(step + 1) % 2]
                        kb.memset(nx[:, 0:sh], 0.0, [nxB]); kb.memset(nx[:, PP - sh:PP], 0.0, [nxB])
                        kb.tt(nx[:, sh:PP - sh], cur[:, 0:PP - 2 * sh], cur[:, 2 * sh:PP], ALU.add, [curB], [nxB])
                        cur, curB = nx, nxB
                    oth, othB = lv[0] if cur is lv[1][0] else lv[1]
                    kb.tt(oth[:], cur[:], invc[:, g, :], ALU.mult, [curB, ivB], [othB])
                    for k, (o, L) in enumerate(seqs):
                        kb.tt(pooled[:, ci, o:o + L], oth[:, 8 + o + 16 * k:8 + o + 16 * k + L], pu_[:, 8 + o + 16 * k:8 + o + 16 * k + L], ALU.subtract, [othB, puB], [plB])
                wv, wb = wload([(lambda v: v, pmap_d[l, g].rearrange("(kc p) n -> p kc n", p=128))], [128, 2, 256])
                for e_ in range(2):
                    for mi in range(nmt):
                        b = pbank()
                        for kc in range(2):
                            kb.mm(ps[b][:], wv[:, kc, e_ * 128:(e_ + 1) * 128], pooled[:, kc, mi * 512:(mi + 1) * 512], kc == 0, kc == 1, [wb, plB], [pB[b]], sig=(kc == 1))
                        kb.ts(yT[:, 2 * g + e_, mi * 512:(mi + 1) * 512], ps[b][:], pv[:, l * PL + PV_PS + 2 * g + e_:l * PL + PV_PS + 2 * g + e_ + 1], None, ALU.mult, None,
                              [pB[b], CB_], yB)
            chk("g%d_pool" % gi)
            merge(2)
            chk("g%d_m2" % gi)

            wo = wo_d[l].rearrange("(kc p) n -> p kc n", p=128)
            for m in range(8):
                wv, wb = wload([(lambda v: v, wo[:, :, m * 128:(m + 1) * 128])], [128, 8, 128])
                for mi, mt in enumerate(mts):
                    b = pbank()
                    for kc in range(8):
                        kb.mm(ps[b][:], wv[:, kc, :], mergedb[:, kc, mi * 512:(mi + 1) * 512], kc == 0, kc == 7, [wb, mgbB[mi]], [pB[b]], sig=(kc == 7))
                    cs = slice(mt * 512, (mt + 1) * 512)
                    kb.stt(xT[:, m, cs], ps[b][:], Gmod[:, l, 1, m, cnd:cnd + 1], xT[:, m, cs], ALU.mult, ALU.add, [pB[b], MB, xB[mt]], [xB[mt]])
            chk("g%d_out" % gi)

        try:
            for l in range(n_layers):
                ffn(l, 0)
                if do_mixer:
                    mixer(l)
                ffn(l, 1)
        except StopBuild as ex:
            print("STOPPED at", ex)

        cv.reset()
        sq = cv.take([128, 8, 512], BF16); sqB = cv.mk("sq")
        rs = cv.take([128, 512]); rsB = cv.mk("rs")
        tmp = cv.take([128, 512]); tmpB = cv.mk("tmp")
        yo = cv.take([128, 8, NT]); yoB = cv.mks("yo", 3)
        fg = pv[:, DEPTH * PL:DEPTH * PL + 8]
        for mt in range(3):
            cs = slice(mt * 512, (mt + 1) * 512)
            for kc in range(8):
                kb.act(sq[:, kc, :], xT[:, kc, cs], AF.Square, [xB[mt]], [sqB])
            for kc in range(8):
                kb.mm(ps[6][:], onesb[:], sq[:, kc, :], kc == 0, kc == 7, [sqB, CB_], [pB[6]], sig=(kc == 7))
            rstd_from_ps(ps[6][:], pB[6], 512, D, rs, rsB, tmp, tmpB)
            for kc in range(8):
                kb.stt(yo[:, kc, cs], xT[:, kc, cs], fg[:, kc:kc + 1], rs, ALU.mult, ALU.mult, [xB[mt], CB_, rsB], [yoB[mt]])
            kb.st(yT_d.rearrange("(c p) t -> p c t", p=128)[:, :, cs], yo[:, :, cs], [yoB[mt]])
        if os.environ.get("KVERB"):
            print("phase peak bytes", cv.peak, "of", PH, "nins", kb.nins, {e: len(kb.prog[e]) for e in kb.ENGS})
        kb.final_wait("sp", yoB + [DBGB] + OUTB)
        kb.emit(st)
    return nc


_NC_CACHE = {}


def prep_inputs(inp):
    f = lambda a: np.ascontiguousarray(np.asarray(a, dtype=np.float32))
    consts = host_consts()
    shared = {"w_ada": f(inp["w_ada"]), "ffn_w_in": f(inp["ffn_w_in"]), "ffn_w_out": f(inp["ffn_w_out"]),
              "w_in": f(inp["w_in"]), "pool_map": f(inp["pool_map"]), "w_branch": f(inp["w_branch"]), "w_out": f(inp["w_out"])}
    for k, v in consts.items():
        shared["c_" + k] = np.ascontiguousarray(v)
    pvv = np.zeros((128, DEPTH * PL + 8), np.float32)
    for l in range(DEPTH):
        o = l * PL
        pvv[:, o + PV_BADA:o + PV_BADA + 72] = f(inp["b_ada"])[l].reshape(72, 128).T
        pvv[:, o + PV_NG:o + PV_NG + 24] = f(inp["norm_gain"])[l].reshape(24, 128).T
        cw = f(inp["ssd_conv_w"])[l].reshape(5, 12, 128)
        pvv[:, o + PV_CW:o + PV_CW + 60] = cw.transpose(2, 1, 0).reshape(128, 60)
        pvv[:, o + PV_CB:o + PV_CB + 12] = f(inp["ssd_conv_b"])[l].reshape(12, 128).T
        pvv[:, o + PV_SNG:o + PV_SNG + 8] = f(inp["ssd_norm_gain"])[l].reshape(8, 128).T
        pvv[:, o + PV_PS:o + PV_PS + 8] = f(inp["pool_scale"])[l].reshape(8, 128).T
        pvv[:, o + PV_DV:o + PV_DV + 8] = np.repeat(f(inp["ssd_d"])[l], 64).reshape(8, 128).T
        pvv[:, o + PV_DNG] = f(inp["diff_norm_gain"])[l]
        pvv[:, o + PV_LAM:o + PV_LAM + 256] = f(inp["diff_lambda"])[l].reshape(1, 256)
        pvv[:32, o + PV_DTB] = f(inp["ssd_dt_bias"])[l].reshape(32)
        pvv[:32, o + PV_ALOG] = f(inp["ssd_a_log"])[l].reshape(32)
    pvv[:, DEPTH * PL:] = f(inp["final_gain"]).reshape(8, 128).T
    shared["pv"] = pvv
    xp = f(inp["x_prompt"]); xs = f(inp["x_sample"])
    ck = f(inp["cache_k"]); cvv = f(inp["cache_v"]); s0 = f(inp["state_ssm"])
    cc = f(inp["c"]); cctx = f(inp["c_ctx"])
    in_maps = []
    for c in range(8):
        m = dict(shared)
        xt = np.concatenate([xp[2 * c], xp[2 * c + 1], xs[c]], axis=0)
        m["xT"] = np.ascontiguousarray(xt.T)
        cd = np.zeros((128, 8, 2), np.float32)
        cd[:, :, 0] = cctx.reshape(8, 128).T
        cd[:, :, 1] = cc[c].reshape(8, 128).T
        m["cond"] = cd.reshape(128, 16)
        m["ckT"] = np.ascontiguousarray(ck[c].transpose(0, 2, 3, 1))
        m["cv"] = np.ascontiguousarray(cvv[c])
        m["st0"] = np.ascontiguousarray(s0[c].transpose(0, 1, 4, 2, 3).reshape(DEPTH, 2, 64, 1024))
        in_maps.append(m)
    return in_maps


def kernel(**inputs):
    in_maps = prep_inputs(inputs)
    key = "full"
    if key not in _NC_CACHE:
        _NC_CACHE[key] = build()
    nc = _NC_CACHE[key]
    res = run_bass_kernel_spmd(nc, in_maps, core_ids=list(range(8)))
    y_prompt = np.zeros((16, 256, D), np.float32)
    y_sample = np.zeros((8, 1024, D), np.float32)
    nk = np.zeros((16, DEPTH, 256, 8, 128), np.float32)
    nv = np.zeros((16, DEPTH, 256, 8, 128), np.float32)
    ns = np.zeros((16, DEPTH, 2, 16, 64, 64), np.float32)
    for c in range(8):
        r = res.results[c]
        y = np.asarray(r["yT"]).T
        y_prompt[2 * c] = y[0:256]
        y_prompt[2 * c + 1] = y[256:512]
        y_sample[c] = y[512:1536]
        nk[2 * c:2 * c + 2] = np.asarray(r["nk"])
        nv[2 * c:2 * c + 2] = np.asarray(r["nv"])
        s = np.asarray(r["ns"]).reshape(2, DEPTH, 2, 64, 16, 64)
        ns[2 * c:2 * c + 2] = s.transpose(0, 1, 2, 4, 5, 3)
    return (y_prompt, y_sample, nk, nv, ns)
```
